# Optimizing a Trainium2 kernel written in Bass

```python
import jax
import jax.numpy as jnp
from jax import lax
import numpy as np

D_MODEL = 1024
BATCH = 32
SEQ = 2048
DEPTH = 2

N_EVEN = (DEPTH + 1) // 2
N_ODD = DEPTH // 2
RMS_EPS = 1e-6

MLA_HEADS = 8
MLA_NOPE = 64
MLA_ROPE = 32
MLA_V = 64
Q_LORA = 384
KV_LORA = 256
ROPE_BASE = 10000.0
Q_BLOCK = 128
MLA_OUT = MLA_HEADS * MLA_V
MLA_COLS = Q_LORA + KV_LORA + MLA_ROPE

RWKV_HEAD = 64
RWKV_DIM = D_MODEL // 2
RWKV_HEADS = RWKV_DIM // RWKV_HEAD
DECAY_LORA = 64
AAA_LORA = 64
GATE_LORA = 128
RWKV_GN_EPS = RWKV_HEAD * 1e-5
RWKV_COLS = 3 * RWKV_DIM + DECAY_LORA + AAA_LORA + GATE_LORA
IN_EVEN = MLA_COLS + RWKV_COLS
MIX_EVEN = MLA_OUT + RWKV_DIM

HG_K = 128
HG_HEADS = D_MODEL // HG_K
HG_V = D_MODEL // HG_HEADS
HG_QK_DIM = HG_HEADS * HG_K
HG_V_DIM = HG_HEADS * HG_V
HG_CHUNK = 32
IN_ODD = 2 * HG_QK_DIM + 2 * HG_V_DIM

FFN_HIDDEN = -(-8 * D_MODEL // (3 * 256)) * 256

kernel_name = 'hybrid_mla_rwkv7_hgrn2_adaln'


def rms_norm(x, gain, eps=RMS_EPS):
    xf = x.astype(jnp.float32)
    y = xf * lax.rsqrt(jnp.mean(xf * xf, axis=-1, keepdims=True) + eps)
    return (y * gain.astype(jnp.float32)).astype(x.dtype)


def modulate(h, shift, scale):
    return h * (1 + scale[:, None, :]) + shift[:, None, :]


def rope(x, cos, sin):
    x1, x2 = jnp.split(x, 2, axis=-1)
    return jnp.concatenate([x1 * cos - x2 * sin, x1 * sin + x2 * cos], axis=-1)


def mla_mix(p, positions, q_norm, w_uq, kv_norm, w_ukv):
    B, S, _ = p.shape
    c_q, c_kv, k_rope = jnp.split(p, [Q_LORA, Q_LORA + KV_LORA], axis=-1)
    q = (rms_norm(c_q, q_norm) @ w_uq).reshape(B, S, MLA_HEADS, MLA_NOPE + MLA_ROPE)
    q_nope, q_rope = q[..., :MLA_NOPE], q[..., MLA_NOPE:]
    kv = (rms_norm(c_kv, kv_norm) @ w_ukv).reshape(B, S, MLA_HEADS, MLA_NOPE + MLA_V)
    k_nope, v = kv[..., :MLA_NOPE], kv[..., MLA_NOPE:]
    inv_freq = 1.0 / (ROPE_BASE ** (jnp.arange(0, MLA_ROPE, 2, dtype=jnp.float32) / MLA_ROPE))
    ang = positions.astype(jnp.float32)[..., None] * inv_freq
    cos, sin = jnp.cos(ang).astype(p.dtype), jnp.sin(ang).astype(p.dtype)
    q_rope = rope(q_rope, cos[:, :, None, :], sin[:, :, None, :])
    k_rope = rope(k_rope, cos, sin)
    nb = S // Q_BLOCK
    qn_b = q_nope.reshape(B, nb, Q_BLOCK, MLA_HEADS, MLA_NOPE).transpose(1, 0, 2, 3, 4)
    qr_b = q_rope.reshape(B, nb, Q_BLOCK, MLA_HEADS, MLA_ROPE).transpose(1, 0, 2, 3, 4)
    scale = (MLA_NOPE + MLA_ROPE) ** -0.5
    kpos = jnp.arange(S)

    def block(args):
        i, qn, qr = args
        s = jnp.einsum('bqhd,bkhd->bhqk', qn, k_nope) + jnp.einsum('bqhr,bkr->bhqk', qr, k_rope)
        s = s.astype(jnp.float32) * scale
        qpos = i * Q_BLOCK + jnp.arange(Q_BLOCK)
        s = jnp.where(kpos[None, :] <= qpos[:, None], s, -jnp.inf)
        pr = jax.nn.softmax(s, axis=-1).astype(v.dtype)
        return jnp.einsum('bhqk,bkhd->bqhd', pr, v)

    o = lax.map(block, (jnp.arange(nb), qn_b, qr_b))
    return o.transpose(1, 0, 2, 3, 4).reshape(B, S, MLA_OUT)


def token_shift(p):
    return jnp.pad(p, ((0, 0), (1, 0), (0, 0)))[:, :-1]


def rwkv7_mix(p, mu, w0, w2, a0, a2, g2, k_k, k_a, r_k, ln_w, ln_b):
    B, S, _ = p.shape
    f32 = jnp.float32
    p = p + (token_shift(p) - p) * mu
    r, k, v, w_lo, a_lo, g_lo = jnp.split(
        p, [RWKV_DIM, 2 * RWKV_DIM, 3 * RWKV_DIM, 3 * RWKV_DIM + DECAY_LORA,
            3 * RWKV_DIM + DECAY_LORA + AAA_LORA], axis=-1)
    w_log = -jax.nn.softplus(-(w0 + jnp.tanh(w_lo) @ w2)) - 0.5
    decay = jnp.exp(-jnp.exp(w_log.astype(f32)))
    a = jax.nn.sigmoid(a0 + a_lo @ a2)
    g = jax.nn.sigmoid(g_lo) @ g2

    def heads(t):
        return t.reshape(B, S, RWKV_HEADS, RWKV_HEAD).astype(f32)

    kk = heads(k * k_k)
    kk = kk / jnp.maximum(jnp.linalg.norm(kk, axis=-1, keepdims=True), 1e-12)
    k = k * (1 + (a - 1) * k_a)
    r_h, k_h, v_h, w_h, a_h = heads(r), heads(k), heads(v), heads(decay), heads(a)

    def step(state, inp):
        r_t, w_t, k_t, v_t, kk_t, a_t = inp
        sa = jnp.einsum('bhvk,bhk->bhv', state, -kk_t)
        state = (state * w_t[:, :, None, :] + sa[..., None] * (kk_t * a_t)[:, :, None, :]
                 + v_t[..., None] * k_t[:, :, None, :])
        return state, jnp.einsum('bhvk,bhk->bhv', state, r_t)

    xs = tuple(t.transpose(1, 0, 2, 3) for t in (r_h, w_h, k_h, v_h, kk, a_h))
    state0 = jnp.zeros((B, RWKV_HEADS, RWKV_HEAD, RWKV_HEAD), f32)
    _, y = lax.scan(step, state0, xs)
    y = y.transpose(1, 0, 2, 3)
    mean = jnp.mean(y, axis=-1, keepdims=True)
    var = jnp.mean(jnp.square(y - mean), axis=-1, keepdims=True)
    y = (y - mean) * lax.rsqrt(var + RWKV_GN_EPS)
    y = y * ln_w.reshape(RWKV_HEADS, RWKV_HEAD) + ln_b.reshape(RWKV_HEADS, RWKV_HEAD)
    bonus = jnp.sum(r_h * k_h * r_k, axis=-1, keepdims=True) * v_h
    y = (y + bonus).reshape(B, S, RWKV_DIM) * g
    return y.astype(p.dtype)


def hgrn2_mix(p, lb, out_norm):
    B, S, _ = p.shape
    f32 = jnp.float32
    q, f, i, g = jnp.split(p, [HG_QK_DIM, 2 * HG_QK_DIM, 2 * HG_QK_DIM + HG_V_DIM], axis=-1)
    q = jax.nn.silu(q.astype(f32))
    forget = lb + (1 - lb) * jax.nn.sigmoid(f.astype(f32))
    key = 1 - forget
    logf = jnp.log(forget)
    nc = S // HG_CHUNK

    def chunks(t, d):
        return t.reshape(B, nc, HG_CHUNK, HG_HEADS, d).transpose(1, 0, 3, 2, 4)

    causal = jnp.tril(jnp.ones((HG_CHUNK, HG_CHUNK), bool))

    def step(state, inp):
        q_c, k_c, g_c, v_c = inp
        b = jnp.cumsum(g_c, axis=2)
        o_inter = jnp.einsum('bhck,bhkv->bhcv', q_c * jnp.exp(b), state)
        diff = b[:, :, :, None, :] - b[:, :, None, :, :]
        dec = jnp.exp(jnp.where(causal[:, :, None], diff, -jnp.inf))
        attn = jnp.einsum('bhtk,bhtsk,bhsk->bhts', q_c, dec, k_c)
        o = o_inter + jnp.einsum('bhts,bhsv->bhtv', attn, v_c)
        b_last = b[:, :, -1:, :]
        state = (state * jnp.exp(b_last[:, :, 0, :, None])
                 + jnp.einsum('bhsk,bhsv->bhkv', k_c * jnp.exp(b_last - b), v_c))
        return state, o

    xs = (chunks(q, HG_K), chunks(key, HG_K), chunks(logf, HG_K), chunks(i.astype(f32), HG_V))
    state0 = jnp.zeros((B, HG_HEADS, HG_K, HG_V), f32)
    _, o = lax.scan(step, state0, xs)
    o = o.transpose(1, 0, 3, 2, 4).reshape(B, S, HG_HEADS, HG_V)
    o = rms_norm(o, out_norm).reshape(B, S, HG_V_DIM) * jax.nn.silu(g.astype(f32))
    return o.astype(p.dtype)


def swiglu(h, w_gate, w_up, w_down):
    return (jax.nn.silu(h @ w_gate) * (h @ w_up)) @ w_down


def setup_inputs(seed: int = 0) -> dict:
    key = jax.random.key(seed)
    ks = iter(jax.random.split(key, 40))
    D = D_MODEL

    def nrm(shape, scale):
        return jax.random.normal(next(ks), shape, jnp.float32) * scale

    def gain(shape):
        return 1.0 + nrm(shape, 0.02)

    x = nrm((BATCH, SEQ, D), 1.0)
    c = nrm((BATCH, D), 1.0)
    offset = jax.random.randint(next(ks), (BATCH, 1), 0, 4096, dtype=jnp.int32)
    positions = offset + jnp.arange(SEQ, dtype=jnp.int32)[None, :]
    return {
        'x': x,
        'c': c,
        'positions': positions,
        'ada_w': nrm((DEPTH, D, 6 * D), 0.5 * D ** -0.5),
        'ada_b': nrm((DEPTH, 6 * D), 0.02),
        'norm_mix': gain((DEPTH, D)),
        'norm_ffn': gain((DEPTH, D)),
        'w_in_even': nrm((N_EVEN, D, IN_EVEN), D ** -0.5),
        'mla_q_norm': gain((N_EVEN, Q_LORA)),
        'mla_w_uq': nrm((N_EVEN, Q_LORA, MLA_HEADS * (MLA_NOPE + MLA_ROPE)), Q_LORA ** -0.5),
        'mla_kv_norm': gain((N_EVEN, KV_LORA)),
        'mla_w_ukv': nrm((N_EVEN, KV_LORA, MLA_HEADS * (MLA_NOPE + MLA_V)), KV_LORA ** -0.5),
        'rwkv_mu': jax.random.uniform(next(ks), (N_EVEN, RWKV_COLS), jnp.float32),
        'rwkv_w0': jax.random.uniform(next(ks), (N_EVEN, RWKV_DIM), jnp.float32, -6.0, -1.0),
        'rwkv_w2': nrm((N_EVEN, DECAY_LORA, RWKV_DIM), 0.5 * DECAY_LORA ** -0.5),
        'rwkv_a0': nrm((N_EVEN, RWKV_DIM), 0.1),
        'rwkv_a2': nrm((N_EVEN, AAA_LORA, RWKV_DIM), 0.5 * AAA_LORA ** -0.5),
        'rwkv_g2': nrm((N_EVEN, GATE_LORA, RWKV_DIM), GATE_LORA ** -0.5),
        'rwkv_k_k': 0.85 + nrm((N_EVEN, RWKV_DIM), 0.05),
        'rwkv_k_a': 1.0 + nrm((N_EVEN, RWKV_DIM), 0.05),
        'rwkv_r_k': nrm((N_EVEN, RWKV_HEADS, RWKV_HEAD), 0.1),
        'rwkv_ln_w': gain((N_EVEN, RWKV_DIM)),
        'rwkv_ln_b': nrm((N_EVEN, RWKV_DIM), 0.02),
        'w_out_even': nrm((N_EVEN, MIX_EVEN, D), MIX_EVEN ** -0.5),
        'w_in_odd': nrm((N_ODD, D, IN_ODD), D ** -0.5),
        'hg_lb_logits': nrm((DEPTH, HG_QK_DIM), 0.5),
        'hg_out_norm': gain((N_ODD, HG_V)),
        'w_out_odd': nrm((N_ODD, HG_V_DIM, D), HG_V_DIM ** -0.5),
        'ffn_w_gate': nrm((DEPTH, D, FFN_HIDDEN), D ** -0.5),
        'ffn_w_up': nrm((DEPTH, D, FFN_HIDDEN), D ** -0.5),
        'ffn_w_down': nrm((DEPTH, FFN_HIDDEN, D), FFN_HIDDEN ** -0.5),
        'final_norm': gain((D,)),
    }


def reference(x, c, positions, ada_w, ada_b, norm_mix, norm_ffn, w_in_even, mla_q_norm, mla_w_uq,
              mla_kv_norm, mla_w_ukv, rwkv_mu, rwkv_w0, rwkv_w2, rwkv_a0, rwkv_a2, rwkv_g2, rwkv_k_k,
              rwkv_k_a, rwkv_r_k, rwkv_ln_w, rwkv_ln_b, w_out_even, w_in_odd, hg_lb_logits, hg_out_norm,
              w_out_odd, ffn_w_gate, ffn_w_up, ffn_w_down, final_norm):
    cond = jax.nn.silu(c)
    lb_p = jax.nn.softmax(hg_lb_logits.astype(jnp.float32), axis=0)
    lb_all = jnp.cumsum(lb_p, axis=0) - lb_p[0]
    for l in range(DEPTH):
        mod = cond @ ada_w[l] + ada_b[l]
        sh_m, sc_m, g_m, sh_f, sc_f, g_f = jnp.split(mod, 6, axis=-1)
        h = modulate(rms_norm(x, norm_mix[l]), sh_m, sc_m)
        j = l // 2
        if l % 2 == 0:
            proj = h @ w_in_even[j]
            y_a = mla_mix(proj[..., :MLA_COLS], positions, mla_q_norm[j], mla_w_uq[j],
                          mla_kv_norm[j], mla_w_ukv[j])
            y_b = rwkv7_mix(proj[..., MLA_COLS:], rwkv_mu[j], rwkv_w0[j], rwkv_w2[j], rwkv_a0[j],
                            rwkv_a2[j], rwkv_g2[j], rwkv_k_k[j], rwkv_k_a[j], rwkv_r_k[j],
                            rwkv_ln_w[j], rwkv_ln_b[j])
            y = jnp.concatenate([y_a, y_b], axis=-1) @ w_out_even[j]
        else:
            y = hgrn2_mix(h @ w_in_odd[j], lb_all[l], hg_out_norm[j]) @ w_out_odd[j]
        x = x + g_m[:, None, :] * y
        h = modulate(rms_norm(x, norm_ffn[l]), sh_f, sc_f)
        x = x + g_f[:, None, :] * swiglu(h, ffn_w_gate[l], ffn_w_up[l], ffn_w_down[l])
    return rms_norm(x, final_norm)
```

```python
import numpy as np
import concourse.bass as bass
import concourse.mybir as mybir

F32 = mybir.dt.float32
BF16 = mybir.dt.bfloat16
I32 = mybir.dt.int32
AF = mybir.ActivationFunctionType
ALU = mybir.AluOpType
AX = mybir.AxisListType

ENGS = ("pe", "dve", "act", "pool", "sp")
NSLOT = 8


class Buf:
    __slots__ = ("name", "w", "r")

    def __init__(self, name=""):
        self.name = name
        self.w = None
        self.r = {}


class Rec:
    def __init__(self, nc, es, same_eng_sync=True):
        self.nc = nc
        self.sem = {}
        for e in ENGS:
            self.sem[e] = es.enter_context(nc.semaphore("s_" + e))
        self.cnt = {e: 0 for e in ENGS}
        self.q = {e: [] for e in ENGS}
        self.seen = {e: {} for e in ENGS}
        self.slots = {}
        for e in ("sp", "act", "pool"):
            self.slots[e] = [[es.enter_context(nc.semaphore("d_%s%d" % (e, i))), 0] for i in range(NSLOT)]
        self.slot_i = {e: 0 for e in self.slots}
        self.same = same_eng_sync
        self.nops = 0

    def _need(self, eng, ev, waits):
        if ev is None:
            return
        key, val, src = ev
        if src == eng and key[0] == "e":
            if eng == "pe" or not self.same:
                return
        if self.seen[eng].get(key, 0) >= val:
            return
        self.seen[eng][key] = val
        waits.append((key, val))

    def _deps(self, eng, reads, writes):
        waits = []
        for b in reads:
            self._need(eng, b.w, waits)
        for b in writes:
            self._need(eng, b.w, waits)
            for ev in b.r.values():
                self._need(eng, ev, waits)
        return waits

    def _mark(self, ev, reads, writes):
        for b in reads:
            b.r[ev[0]] = ev
        for b in writes:
            b.w = ev
            b.r = {}

    def op(self, eng, fn, reads=(), writes=()):
        waits = self._deps(eng, reads, writes)
        self.cnt[eng] += 1
        ev = (("e", eng), self.cnt[eng], eng)
        self.q[eng].append((waits, fn, ("e", eng), 1))
        self._mark(ev, reads, writes)
        self.nops += 1

    def dma(self, eng, out, in_, reads=(), writes=()):
        sl = self.slots[eng]
        i = self.slot_i[eng]
        self.slot_i[eng] = (i + 1) % NSLOT
        key = ("d", eng, i)
        waits = []
        if sl[i][1] > 0:
            self._need(eng, (key, sl[i][1], "dma"), waits)
        waits += self._deps(eng, reads, writes)
        sl[i][1] += 16
        ev = (key, sl[i][1], "dma")
        self.q[eng].append((waits, lambda e, o=out, s=in_: e.dma_start(out=o, in_=s), key, 16))
        self._mark(ev, reads, writes)
        self.nops += 1

    def _semh(self, key):
        if key[0] == "e":
            return self.sem[key[1]]
        return self.slots[key[1]][key[2]][0]

    def drain_dmas(self):
        for e in self.slots:
            for i, (s, v) in enumerate(self.slots[e]):
                if v > 0:
                    key = ("d", e, i)
                    if self.seen[e].get(key, 0) < v:
                        self.seen[e][key] = v
                        self.q[e].append(([(key, v)], None, None, 0))

    def emit(self):
        self.drain_dmas()
        nc = self.nc
        rec = self

        def run(engname, handle):
            for waits, fn, key, inc in rec.q[engname]:
                for (k, v) in waits:
                    handle.wait_ge(rec._semh(k), v)
                if fn is not None:
                    fn(handle).then_inc(rec._semh(key), inc)
            rec.q[engname] = []

        with nc.Block() as block:
            @block.tensor
            def _(e):
                run("pe", e)

            @block.vector
            def _(e):
                run("dve", e)

            @block.scalar
            def _(e):
                run("act", e)

            @block.gpsimd
            def _(e):
                run("pool", e)

            @block.sync
            def _(e):
                run("sp", e)


from contextlib import ExitStack
import numpy as np
import concourse.bass as bass
import concourse.mybir as mybir

D = 1024
KC = 8
FH = 2816
JH = 22
EPS = 1e-6


class G:
    pass


_uid = [0]


def uniq(n):
    _uid[0] += 1
    return "sb%d_%s" % (_uid[0], n)


def pool_of(n, name):
    return [Buf("%s%d" % (name, i)) for i in range(n)]


class Rot:
    def __init__(self, tiles):
        self.t = tiles
        self.b = [Buf() for _ in tiles]
        self.i = 0

    def next(self):
        i = self.i
        self.i = (i + 1) % len(self.t)
        return self.t[i], self.b[i]


def setup_globals(nc, es, g, NB):
    g.NB = NB
    sb = lambda n, s, d=F32: es.enter_context(nc.sbuf_tensor(uniq(n), s, d))
    g.modT = sb("modT", [128, 2, 48, NB])
    g.modT_b = Buf("modT")
    g.Amod = sb("Amod", [128, 2, 2, KC, NB])
    g.Amod_b = Buf("Amod")
    g.ident = sb("ident", [128, 128])
    g.ident_bf = sb("ident_bf", [128, 128], BF16)
    g.ones_bf = sb("ones_bf", [128, 128], BF16)
    g.const_b = Buf("const")
    g.epsb = sb("epsb", [128, 1])
    g.ps = [es.enter_context(nc.psum_tensor("ps%d" % i, [128, 512], F32)) for i in range(8)]
    g.psb = [Buf("ps%d" % i) for i in range(8)]
    g.ps_i = 0


def next_ps(g):
    i = g.ps_i
    g.ps_i = (i + 1) % 8
    return g.ps[i], g.psb[i]


def phase_mods(nc, rec, g, dr):
    NB = g.NB
    with ExitStack() as es:
        sb = lambda n, s, d=F32: es.enter_context(nc.sbuf_tensor(uniq(n), s, d))
        cT = sb("cT", [128, KC, NB]); cT_b = Buf()
        condT = sb("condT", [128, KC, NB]); condT_b = Buf()
        adab = sb("adab", [128, 2, 48]); adab_b = Buf()
        gains = sb("gains", [128, 2, 2, KC]); gains_b = Buf()
        wst = [sb("adaw%d" % i, [128, KC, 1024]) for i in range(2)]
        wst_b = [Buf(), Buf()]
        rec.dma("sp", g.ident[:], dr["ident"][:, :], writes=[g.const_b])
        rec.op("pool", lambda e: e.memset(g.ones_bf[:], 1.0), writes=[g.const_b])
        rec.op("pool", lambda e: e.memset(g.epsb[:], EPS), writes=[g.const_b])
        rec.op("dve", lambda e: e.tensor_copy(out=g.ident_bf[:], in_=g.ident[:]), reads=[g.const_b], writes=[g.const_b])
        rec.dma("sp", cT[:], dr["cT"][:, :, :], writes=[cT_b])
        rec.dma("sp", adab[:], dr["ada_bT"][:, :, :], writes=[adab_b])
        rec.dma("sp", gains[:, 0], dr["norm_mixT"][:, :, :], writes=[gains_b])
        rec.dma("sp", gains[:, 1], dr["norm_ffnT"][:, :, :], writes=[gains_b])
        rec.op("act", lambda e: e.activation(out=condT[:], in_=cT[:], func=AF.Silu), reads=[cT_b], writes=[condT_b])
        it = 0
        for l in range(2):
            for pc in range(6):
                w, wb = wst[it % 2], wst_b[it % 2]
                it += 1
                src = dr["ada_w"][l, :, pc * 1024:(pc + 1) * 1024].rearrange("(k p) n -> p k n", p=128)
                for kk in range(KC):
                    rec.dma("sp", w[:, kk, :], src[:, kk, :], writes=[wb])
                for jj in range(8):
                    j = pc * 8 + jj
                    ps, psb = next_ps(g)
                    for kk in range(KC):
                        rec.op("pe", lambda e, ps=ps, w=w, kk=kk, jj=jj: e.matmul(
                            ps[:, 0:NB], lhsT=w[:, kk, jj * 128:(jj + 1) * 128], rhs=condT[:, kk, :],
                            start=(kk == 0), stop=(kk == KC - 1)), reads=[wb, condT_b], writes=[psb])
                    rec.op("dve", lambda e, ps=ps, l=l, j=j: e.tensor_scalar(
                        out=g.modT[:, l, j, :], in0=ps[:, 0:NB], scalar1=adab[:, l, j:j + 1], scalar2=None,
                        op0=ALU.add), reads=[psb, adab_b], writes=[g.modT_b])
        for l in range(2):
            for sub in range(2):
                for kk in range(KC):
                    j = (1 + 3 * sub) * KC + kk
                    rec.op("dve", lambda e, l=l, sub=sub, kk=kk, j=j: e.tensor_scalar(
                        out=g.Amod[:, l, sub, kk, :], in0=g.modT[:, l, j, :], scalar1=1.0,
                        scalar2=gains[:, sub, l, kk:kk + 1], op0=ALU.add, op1=ALU.mult),
                        reads=[g.modT_b, gains_b], writes=[g.Amod_b])
        rec.emit()


def load_cast(rec, st_rot, srcs_dsts, engs=("dve", "pool")):
    for i, (src, dst, db) in enumerate(srcs_dsts):
        st, stb = st_rot.next()
        n = src.shape[-1]
        rec.dma("sp", st[:, 0:n], src, writes=[stb])
        eng = engs[i % len(engs)]
        rec.op(eng, lambda e, st=st, dst=dst, n=n: e.tensor_copy(out=dst, in_=st[:, 0:n]), reads=[stb], writes=[db])


def phase_post(nc, rec, g, dr, l, NTOK, T, last, N=256):
    wout = dr["w_out_even"][0] if l == 0 else dr["w_out_odd"][0]
    XT = dr["XT"].rearrange("(k p) t -> p k t", p=128)
    YT = (dr.get("YT1", dr["YT"]) if l == 1 else dr["YT"]).rearrange("(k p) t -> p k t", p=128)
    ntile = NTOK // N
    with ExitStack() as es:
        sb = lambda n, s, d=F32: es.enter_context(nc.sbuf_tensor(uniq(n), s, d))
        Wo = sb("Wo", [128, KC, D], BF16)
        Wg = sb("Wg", [128, KC, FH], BF16)
        Wu = sb("Wu", [128, KC, FH], BF16)
        Wd = sb("Wd", [128, JH, D], BF16)
        Wb = Buf("W")
        st_rot = Rot([sb("wst%d" % i, [128, FH // 4]) for i in range(2)])
        xs = [sb("x%d" % i, [128, KC, N]) for i in range(2)]
        xbs = [Buf(), Buf()]
        hT = sb("hT", [128, KC, N], BF16); hb = Buf()
        sq = sb("sq", [128, KC, N], BF16); sqb = Buf()
        actT = sb("actT", [128, JH, N], BF16); actb = Buf()
        tr = Rot([sb("tmp%d" % i, [128, N]) for i in range(3)])
        rstd = sb("rstd", [128, N]); rstdb = Buf()
        if last:
            fng = sb("fng", [128, KC]); fngb = Buf()
            rec.dma("sp", fng[:], dr["final_normT"][:, :], writes=[fngb])
            fr = Rot([sb("fin%d" % i, [128, D]) for i in range(1)])
        jobs = []
        for kk in range(KC):
            for hh in range(2):
                cs = slice(hh * 512, (hh + 1) * 512)
                jobs.append((wout[kk * 128:(kk + 1) * 128, cs], Wo[:, kk, cs], Wb))
        for kk in range(KC):
            for hh in range(4):
                cs = slice(hh * (FH // 4), (hh + 1) * (FH // 4))
                jobs.append((dr["ffn_w_gate"][l, kk * 128:(kk + 1) * 128, cs], Wg[:, kk, cs], Wb))
                jobs.append((dr["ffn_w_up"][l, kk * 128:(kk + 1) * 128, cs], Wu[:, kk, cs], Wb))
        for j in range(JH):
            for hh in range(2):
                cs = slice(hh * 512, (hh + 1) * 512)
                jobs.append((dr["ffn_w_down"][l, j * 128:(j + 1) * 128, cs], Wd[:, j, cs], Wb))
        load_cast(rec, st_rot, jobs)

        def stats(x, xb):
            rec.op("act", lambda e: e.activation(out=sq[:], in_=x[:], func=AF.Square), reads=[xb], writes=[sqb])
            ps, psb = next_ps(g)
            for kk in range(KC):
                MM(rec, ps[:, 0:N], g.ones_bf[:], sq[:, kk, :], kk == 0, kk == KC - 1, [sqb, g.const_b], [psb])
            t1, t1b = tr.next()
            ACTF(rec, t1[:], ps[:, 0:N], AF.Sqrt, [psb, g.const_b], [t1b], scale=1.0 / D, bias=g.epsb[:])
            rec.op("dve", lambda e: e.reciprocal(out=rstd[:], in_=t1[:]), reads=[t1b], writes=[rstdb])

        def load_x(i):
            rec.dma("sp", xs[i % 2][:], XT[:, :, i * N:(i + 1) * N], writes=[xbs[i % 2]])

        def load_y(i):
            rec.dma("sp", sq[:], YT[:, :, i * N:(i + 1) * N], writes=[sqb])

        def outproj(i):
            x, xb = xs[i % 2], xbs[i % 2]
            b = (i * N) // T
            for m in range(KC):
                ps, psb = next_ps(g)
                for kk in range(KC):
                    MM(rec, ps[:, 0:N], Wo[:, kk, m * 128:(m + 1) * 128], sq[:, kk, :], kk == 0, kk == KC - 1, [Wb, sqb], [psb])
                STT(rec, x[:, m, :], ps[:, 0:N], g.modT[:, l, 16 + m, b:b + 1], x[:, m, :], ALU.mult, ALU.add, [psb, g.modT_b, xb], [xb])

        def rmsmod(i):
            x, xb = xs[i % 2], xbs[i % 2]
            b = (i * N) // T
            stats(x, xb)
            for m in range(KC):
                t1, t1b = tr.next()
                TT(rec, "dve", t1[:], x[:, m, :], rstd[:], ALU.mult, [xb, rstdb], [t1b])
                ACTF(rec, hT[:, m, :], t1[:], AF.Identity, [t1b, g.Amod_b, g.modT_b], [hb],
                     scale=g.Amod[:, l, 1, m, b:b + 1], bias=g.modT[:, l, 24 + m, b:b + 1])

        def gateup(i):
            for j in range(JH):
                pg, pgb = next_ps(g)
                pu, pub = next_ps(g)
                for kk in range(KC):
                    MM(rec, pg[:, 0:N], Wg[:, kk, j * 128:(j + 1) * 128], hT[:, kk, :], kk == 0, kk == KC - 1, [Wb, hb], [pgb])
                for kk in range(KC):
                    MM(rec, pu[:, 0:N], Wu[:, kk, j * 128:(j + 1) * 128], hT[:, kk, :], kk == 0, kk == KC - 1, [Wb, hb], [pub])
                t1, t1b = tr.next()
                ACTF(rec, t1[:], pg[:, 0:N], AF.Silu, [pgb], [t1b])
                TT(rec, "dve", actT[:, j, :], t1[:], pu[:, 0:N], ALU.mult, [t1b, pub], [actb])

        def down(i):
            x, xb = xs[i % 2], xbs[i % 2]
            b = (i * N) // T
            for m in range(KC):
                ps, psb = next_ps(g)
                for j in range(JH):
                    MM(rec, ps[:, 0:N], Wd[:, j, m * 128:(m + 1) * 128], actT[:, j, :], j == 0, j == JH - 1, [Wb, actb], [psb])
                STT(rec, x[:, m, :], ps[:, 0:N], g.modT[:, l, 40 + m, b:b + 1], x[:, m, :], ALU.mult, ALU.add, [psb, g.modT_b, xb], [xb])

        def store(i):
            x, xb = xs[i % 2], xbs[i % 2]
            if not last:
                rec.dma("sp", XT[:, :, i * N:(i + 1) * N], x[:], reads=[xb])
                return
            stats(x, xb)
            for m in range(KC):
                STT(rec, x[:, m, :], x[:, m, :], fng[:, m:m + 1], rstd[:], ALU.mult, ALU.mult, [xb, rstdb, fngb], [xb])
            for s in range(N // 128):
                f, fb = fr.next()
                for m in range(KC):
                    if m % 4 == 0:
                        ps, psb = next_ps(g)
                    rec.op("pe", lambda e, ps=ps, m=m, s=s, x=x: e.transpose(
                        ps[:, (m % 4) * 128:(m % 4 + 1) * 128], x[:, m, s * 128:(s + 1) * 128], g.ident[:]),
                        reads=[xb, g.const_b], writes=[psb])
                    if m % 4 == 3:
                        evac(rec, m // 4, f[:, (m - 3) * 128:(m + 1) * 128], ps[:, :], [psb], [fb])
                t0 = i * N + s * 128
                rec.dma("sp", dr["out"][t0:t0 + 128, :], f[:], reads=[fb])

        load_x(0)
        load_y(0)
        outproj(0)
        rmsmod(0)
        for i in range(ntile):
            if i + 1 < ntile:
                load_x(i + 1)
                if not last:
                    load_y(i + 1)
            gateup(i)
            if i + 1 < ntile:
                if last:
                    load_y(i + 1)
                outproj(i + 1)
                rmsmod(i + 1)
            down(i)
            store(i)
        rec.emit()


TWO_PI = 2.0 * np.pi
MLA_SCALE = 96.0 ** -0.5


def evac(rec, i, out, in_, reads, writes):
    if i % 2 == 0:
        rec.op("act", lambda e: e.copy(out=out, in_=in_), reads=reads, writes=writes)
    else:
        rec.op("dve", lambda e: e.tensor_copy(out=out, in_=in_), reads=reads, writes=writes)


def rms_stats(rec, g, src, srcb, nch, N, sq, sqb, t1, t1b, rstd, rstdb, dim):
    rec.op("act", lambda e: e.activation(out=sq[:, 0:nch, :], in_=src, func=AF.Square), reads=[srcb], writes=[sqb])
    ps, psb = next_ps(g)
    for kk in range(nch):
        rec.op("pe", lambda e, kk=kk: e.matmul(ps[:, 0:N], lhsT=g.ones_bf[:], rhs=sq[:, kk, :],
                                                start=(kk == 0), stop=(kk == nch - 1)),
               reads=[sqb, g.const_b], writes=[psb])
    rec.op("act", lambda e: e.activation(out=t1[:], in_=ps[:, 0:N], func=AF.Sqrt, scale=1.0 / dim, bias=g.epsb[:]),
           reads=[psb, g.const_b], writes=[t1b])
    rec.op("dve", lambda e: e.reciprocal(out=rstd[:], in_=t1[:]), reads=[t1b], writes=[rstdb])


def phase_pre0(nc, rec, g, dr, NTOK, T, N=512):
    l = 0
    XT = dr["XT"].rearrange("(k p) t -> p k t", p=128)
    RP = dr["RP"].rearrange("(k p) t -> p k t", p=128)
    ntile = NTOK // N
    NS = N // 128
    with ExitStack() as es:
        sb = lambda n, s, d=F32: es.enter_context(nc.sbuf_tensor(uniq(n), s, d))
        Win = sb("Win", [128, KC, 2464], BF16)
        Wsw = sb("Wsw", [128, KC, 32], BF16)
        Wuq = sb("Wuq", [128, 3, 768], BF16)
        Wuqs = sb("Wuqs", [128, 3, 768], BF16)
        Wukv = sb("Wukv", [128, 2, 1024], BF16)
        Wb = Buf("W")
        st_rot = Rot([sb("wst%d" % i, [128, 1232]) for i in range(2)])
        jobs = []
        for kk in range(KC):
            for hh in range(2):
                cs = slice(hh * 1232, (hh + 1) * 1232)
                jobs.append((dr["w_in_even"][0, kk * 128:(kk + 1) * 128, cs], Win[:, kk, cs], Wb))
            jobs.append((dr["w_in_sw"][kk * 128:(kk + 1) * 128, :], Wsw[:, kk, :], Wb))
        for kk in range(3):
            jobs.append((dr["mla_w_uq"][0, kk * 128:(kk + 1) * 128, :], Wuq[:, kk, :], Wb))
            jobs.append((dr["w_uq_sw"][kk * 128:(kk + 1) * 128, :], Wuqs[:, kk, :], Wb))
        for kk in range(2):
            jobs.append((dr["w_ukv_r"][kk * 128:(kk + 1) * 128, :], Wukv[:, kk, :], Wb))
        load_cast(rec, st_rot, jobs)
        ctab = sb("ctab", [128, 8]); ctabb = Buf()
        rec.dma("sp", ctab[:], dr["ctab"][:, :], writes=[ctabb])
        negpi = sb("negpi", [128, 1]);
        rec.op("pool", lambda e: e.memset(negpi[:], -np.pi), writes=[ctabb])
        Cq = sb("Cq", [96, N]); Sq = sb("Sq", [96, N]); trb = Buf()
        rec.op("pool", lambda e: e.memset(Cq[0:64, :], MLA_SCALE), writes=[trb])
        rec.op("pool", lambda e: e.memset(Sq[0:64, :], 0.0), writes=[trb])
        Ck = sb("Ck", [32, N]); Sk = sb("Sk", [32, N])
        posi = sb("posi", [96, N], I32); posib = Buf()
        posf = sb("posf", [96, N]); posfb = Buf()
        ua = sb("ua", [96, N]); uab = Buf()
        ui = sb("ui", [96, N], I32); uib = Buf()
        uf = sb("uf", [96, N]); ufb = Buf()
        um = sb("um", [96, N]); umb = Buf()
        trig = [sb("sinv", [96, N]), sb("cosv", [96, N])]; trigb = [Buf(), Buf()]
        xin = sb("xin", [128, NS, D]); xinb = Buf()
        xT = sb("xT", [128, KC, N]); xTb = Buf()
        sq = sb("sq", [128, KC, N], BF16); sqb = Buf()
        hT = sb("hT", [128, KC, N], BF16); hb = Buf()
        tr = Rot([sb("tmp%d" % i, [128, N]) for i in range(3)])
        rstd = sb("rstd", [128, N]); rstdb = Buf()
        cq = sb("cq", [128, 3, N]); cqb = Buf()
        cqn = sb("cqn", [128, 3, N], BF16); cqnb = Buf()
        ckv = sb("ckv", [128, 2, N]); ckvb = Buf()
        ckvn = sb("ckvn", [128, 2, N], BF16); ckvnb = Buf()
        qo = Rot([sb("qo%d" % i, [96, N], BF16) for i in range(2)])
        ko = Rot([sb("ko%d" % i, [128, N], BF16) for i in range(2)])
        kro = sb("kro", [32, N], BF16); krob = Buf()
        vo = Rot([sb("vo%d" % i, [128, 512], BF16) for i in range(2)])
        rpo = Rot([sb("rpo%d" % i, [128, N]) for i in range(3)])
        ei = 0
        for ti in range(ntile):
            b = (ti * N) // T
            t0 = ti * N
            rec.dma("sp", xin[:], dr["x"][t0:t0 + N, :].rearrange("(s p) d -> p s d", p=128), writes=[xinb])
            for m in range(KC):
                ps, psb = next_ps(g)
                for s in range(NS):
                    rec.op("pe", lambda e, ps=ps, m=m, s=s: e.transpose(
                        ps[:, s * 128:(s + 1) * 128], xin[:, s, m * 128:(m + 1) * 128], g.ident[:]),
                        reads=[xinb, g.const_b], writes=[psb])
                evac(rec, m, xT[:, m, :], ps[:, 0:N], [psb], [xTb])
            rec.dma("sp", XT[:, :, t0:t0 + N], xT[:], reads=[xTb])
            rec.dma("sp", posi[:], dr["pos"][b:b + 1, (t0 % T):(t0 % T) + N].partition_broadcast(96), writes=[posib])
            rec.op("dve", lambda e: e.tensor_copy(out=posf[:], in_=posi[:]), reads=[posib], writes=[posfb])
            for w in range(2):
                off = 0.5 if w == 0 else 0.75
                rec.op("dve", lambda e, off=off: e.tensor_scalar(out=ua[:], in0=posf[:], scalar1=ctab[0:96, 0:1], scalar2=off,
                                                                 op0=ALU.mult, op1=ALU.add), reads=[posfb, ctabb], writes=[uab])
                rec.op("dve", lambda e: e.tensor_copy(out=ui[:], in_=ua[:]), reads=[uab], writes=[uib])
                rec.op("dve", lambda e: e.tensor_copy(out=uf[:], in_=ui[:]), reads=[uib], writes=[ufb])
                rec.op("dve", lambda e: e.tensor_tensor(out=ua[:], in0=ua[:], in1=uf[:], op=ALU.subtract), reads=[uab, ufb], writes=[uab])
                rec.op("dve", lambda e: e.tensor_scalar(out=um[:], in0=ua[:], scalar1=0.0, scalar2=None, op0=ALU.is_lt),
                       reads=[uab], writes=[umb])
                rec.op("dve", lambda e: e.tensor_tensor(out=ua[:], in0=ua[:], in1=um[:], op=ALU.add), reads=[uab, umb], writes=[uab])
                rec.op("act", lambda e, w=w: e.activation(out=trig[w][:], in_=ua[:], func=AF.Sin, scale=TWO_PI, bias=negpi[0:96, :]),
                       reads=[uab, ctabb], writes=[trigb[w]])
            rec.op("dve", lambda e: e.tensor_copy(out=Ck[:], in_=trig[1][0:32, :]), reads=[trigb[1]], writes=[trb])
            rec.op("dve", lambda e: e.tensor_scalar(out=Sk[:], in0=trig[0][0:32, :], scalar1=ctab[0:32, 1:2], scalar2=None, op0=ALU.mult),
                   reads=[trigb[0], ctabb], writes=[trb])
            rec.op("dve", lambda e: e.tensor_scalar(out=Cq[64:96, :], in0=trig[1][64:96, :], scalar1=MLA_SCALE, scalar2=None, op0=ALU.mult),
                   reads=[trigb[1]], writes=[trb])
            rec.op("dve", lambda e: e.tensor_scalar(out=Sq[64:96, :], in0=trig[0][64:96, :], scalar1=ctab[64:96, 2:3], scalar2=None, op0=ALU.mult),
                   reads=[trigb[0], ctabb], writes=[trb])
            t1, t1b = tr.next()
            rms_stats(rec, g, xT[:], xTb, KC, N, sq, sqb, t1, t1b, rstd, rstdb, D)
            for m in range(KC):
                t1, t1b = tr.next()
                rec.op("dve", lambda e, t1=t1, m=m: e.tensor_tensor(out=t1[:], in0=xT[:, m, :], in1=rstd[:], op=ALU.mult),
                       reads=[xTb, rstdb], writes=[t1b])
                rec.op("act", lambda e, t1=t1, m=m, b=b: e.activation(
                    out=hT[:, m, :], in_=t1[:], func=AF.Identity, scale=g.Amod[:, l, 0, m, b:b + 1],
                    bias=g.modT[:, l, 0 + m, b:b + 1]), reads=[t1b, g.Amod_b, g.modT_b], writes=[hb])

            def proj(ps, psb, c0, M, W=Win):
                for kk in range(KC):
                    rec.op("pe", lambda e, kk=kk: e.matmul(ps[0:M, 0:N], lhsT=W[:, kk, c0:c0 + M], rhs=hT[:, kk, :],
                                                            start=(kk == 0), stop=(kk == KC - 1)), reads=[Wb, hb], writes=[psb])
            for c in range(3):
                ps, psb = next_ps(g)
                proj(ps, psb, c * 128, 128)
                evac(rec, c, cq[:, c, :], ps[:, 0:N], [psb], [cqb])
            for c in range(2):
                ps, psb = next_ps(g)
                proj(ps, psb, 384 + c * 128, 128)
                evac(rec, c + 1, ckv[:, c, :], ps[:, 0:N], [psb], [ckvb])
            for (src, srcb, nch, dst, dstb, gcol, dim) in ((cq, cqb, 3, cqn, cqnb, 3, 384), (ckv, ckvb, 2, ckvn, ckvnb, 6, 256)):
                t1, t1b = tr.next()
                rms_stats(rec, g, src[:], srcb, nch, N, sq, sqb, t1, t1b, rstd, rstdb, dim)
                for c in range(nch):
                    rec.op("dve", lambda e, c=c, src=src, dst=dst, gcol=gcol: e.scalar_tensor_tensor(
                        out=dst[:, c, :], in0=src[:, c, :], scalar=ctab[:, gcol + c:gcol + c + 1], in1=rstd[:],
                        op0=ALU.mult, op1=ALU.mult), reads=[srcb, rstdb, ctabb], writes=[dstb])
            ps, psb = next_ps(g)
            proj(ps, psb, 640, 32)
            ps2, psb2 = next_ps(g)
            proj(ps2, psb2, 0, 32, W=Wsw)
            t1, t1b = tr.next()
            t2, t2b = tr.next()
            rec.op("dve", lambda e, t1=t1, ps2=ps2: e.tensor_tensor(out=t1[0:32, :], in0=ps2[0:32, 0:N], in1=Sk[:], op=ALU.mult),
                   reads=[psb2, trb], writes=[t1b])
            rec.op("dve", lambda e, t2=t2, ps=ps: e.tensor_tensor(out=t2[0:32, :], in0=ps[0:32, 0:N], in1=Ck[:], op=ALU.mult),
                   reads=[psb, trb], writes=[t2b])
            rec.op("dve", lambda e, t1=t1, t2=t2: e.tensor_tensor(out=kro[:], in0=t1[0:32, :], in1=t2[0:32, :], op=ALU.add),
                   reads=[t1b, t2b], writes=[krob])
            for h in range(8):
                rec.dma("pool" if h % 2 else "sp", dr["KT"][h, 64:96, t0:t0 + N], kro[:], reads=[krob])
            for h in range(8):
                psA, psAb = next_ps(g)
                psB, psBb = next_ps(g)
                for (ps, psb, W) in ((psA, psAb, Wuq), (psB, psBb, Wuqs)):
                    for kk in range(3):
                        rec.op("pe", lambda e, ps=ps, W=W, kk=kk, h=h: e.matmul(
                            ps[0:96, 0:N], lhsT=W[:, kk, h * 96:(h + 1) * 96], rhs=cqn[:, kk, :],
                            start=(kk == 0), stop=(kk == 2)), reads=[Wb, cqnb], writes=[psb])
                t1, t1b = tr.next()
                t2, t2b = tr.next()
                q, qb = qo.next()
                rec.op("dve", lambda e, t1=t1, psB=psB: e.tensor_tensor(out=t1[0:96, :], in0=psB[0:96, 0:N], in1=Sq[:], op=ALU.mult),
                       reads=[psBb, trb], writes=[t1b])
                rec.op("dve", lambda e, t2=t2, psA=psA: e.tensor_tensor(out=t2[0:96, :], in0=psA[0:96, 0:N], in1=Cq[:], op=ALU.mult),
                       reads=[psAb, trb], writes=[t2b])
                rec.op("pool", lambda e, t1=t1, t2=t2, q=q: e.tensor_tensor(out=q[:], in0=t1[0:96, :], in1=t2[0:96, :], op=ALU.add),
                       reads=[t1b, t2b], writes=[qb])
                rec.dma("sp", dr["QT"][h, :, t0:t0 + N], q[:], reads=[qb])
            for hp in range(4):
                ps, psb = next_ps(g)
                for kk in range(2):
                    rec.op("pe", lambda e, ps=ps, kk=kk, hp=hp: e.matmul(
                        ps[:, 0:N], lhsT=Wukv[:, kk, hp * 128:(hp + 1) * 128], rhs=ckvn[:, kk, :],
                        start=(kk == 0), stop=(kk == 1)), reads=[Wb, ckvnb], writes=[psb])
                k, kb = ko.next()
                evac(rec, hp, k[:], ps[:, 0:N], [psb], [kb])
                rec.dma("sp", dr["KT"][2 * hp, 0:64, t0:t0 + N], k[0:64, :], reads=[kb])
                rec.dma("pool", dr["KT"][2 * hp + 1, 0:64, t0:t0 + N], k[64:128, :], reads=[kb])
            for s in range(NS):
                ps, psb = next_ps(g)
                for kk in range(2):
                    rec.op("pe", lambda e, ps=ps, kk=kk, s=s: e.matmul(
                        ps[:, :], lhsT=ckvn[:, kk, s * 128:(s + 1) * 128], rhs=Wukv[:, kk, 512:1024],
                        start=(kk == 0), stop=(kk == 1)), reads=[Wb, ckvnb], writes=[psb])
                v, vb = vo.next()
                evac(rec, s, v[:], ps[:, :], [psb], [vb])
                rec.dma("sp", dr["V"][t0 + s * 128:t0 + (s + 1) * 128, :], v[:], reads=[vb])
            for c in range(14):
                ps, psb = next_ps(g)
                proj(ps, psb, 672 + c * 128, 128)
                o, ob = rpo.next()
                evac(rec, c, o[:], ps[:, 0:N], [psb], [ob])
                rec.dma("sp" if c % 2 else "pool", RP[:, c, t0:t0 + N], o[:], reads=[ob])
        rec.emit()


def phase_mla(nc, rec, g, dr, NB, T):
    NKB = T // 128
    YT = dr["YT"].rearrange("(k p) t -> p k t", p=128)
    offs = [0]
    for kb in range(NKB):
        offs.append(offs[-1] + (T - kb * 128))
    with ExitStack() as es:
        sb = lambda n, s, d=F32: es.enter_context(nc.sbuf_tensor(uniq(n), s, d))
        qr = Rot([sb("q%d" % i, [96, T], BF16) for i in range(2)])
        kr = Rot([sb("k%d" % i, [96, T], BF16) for i in range(2)])
        va = [sb("va%d" % i, [128, NKB, 65], BF16) for i in range(2)]
        vab = [Buf(), Buf()]
        for i in range(2):
            rec.op("pool", lambda e, i=i: e.memset(va[i][:], 1.0), writes=[vab[i]])
        ptr = Rot([sb("pt%d" % i, [128, offs[-1]], BF16) for i in range(2)])
        mask = sb("mask", [128, 128], BF16); maskb = Buf()
        mstage = sb("mstage", [128, 128])
        rec.dma("sp", mstage[:], dr["cmask"][:, :], writes=[maskb])
        rec.op("dve", lambda e: e.tensor_copy(out=mask[:], in_=mstage[:]), reads=[maskb], writes=[maskb])
        ytm = sb("ytm", [128, NKB, 512]); ytmb = Buf()
        rc = Rot([sb("rc%d" % i, [128, 1]) for i in range(4)])
        ytr = Rot([sb("yts%d" % i, [128, 4, 128], BF16) for i in range(2)])
        it = 0
        for b in range(NB):
            for h in range(8):
                q, qb_ = qr.next()
                k, kb_ = kr.next()
                v, vb_ = va[it % 2], vab[it % 2]
                it += 1
                pt, ptb = ptr.next()
                rec.dma("sp", q[:], dr["QT"][h, :, b * T:(b + 1) * T], writes=[qb_])
                rec.dma("pool", k[:], dr["KT"][h, :, b * T:(b + 1) * T], writes=[kb_])
                rec.dma("sp", v[:, :, 0:64], dr["V"][b * T:(b + 1) * T, h * 64:(h + 1) * 64].rearrange("(k p) d -> p k d", p=128),
                        writes=[vb_])
                for kb in range(NKB):
                    q0 = kb * 128
                    c = q0
                    while c < T:
                        n = min(512, T - c)
                        ps, psb = next_ps(g)
                        rec.op("pe", lambda e, ps=ps, k=k, q=q, q0=q0, c=c, n=n: e.matmul(
                            ps[:, 0:n], lhsT=k[:, q0:q0 + 128], rhs=q[:, c:c + n], start=True, stop=True),
                            reads=[kb_, qb_], writes=[psb])
                        o0 = offs[kb] + (c - q0)
                        rec.op("act", lambda e, ps=ps, pt=pt, o0=o0, n=n: e.activation(out=pt[:, o0:o0 + n], in_=ps[:, 0:n], func=AF.Exp),
                               reads=[psb], writes=[ptb])
                        c += n
                    o0 = offs[kb]
                    rec.op("pool", lambda e, pt=pt, o0=o0: e.tensor_tensor(out=pt[:, o0:o0 + 128], in0=pt[:, o0:o0 + 128], in1=mask[:], op=ALU.mult),
                           reads=[ptb, maskb], writes=[ptb])
                for qb in range(NKB):
                    ps, psb = next_ps(g)
                    for kb in range(qb + 1):
                        o0 = offs[kb] + (qb - kb) * 128
                        rec.op("pe", lambda e, ps=ps, pt=pt, v=v, o0=o0, kb=kb, qb=qb: e.matmul(
                            ps[:, 0:65], lhsT=pt[:, o0:o0 + 128], rhs=v[:, kb, :], start=(kb == 0), stop=(kb == qb)),
                            reads=[ptb, vb_], writes=[psb])
                    r, rb = rc.next()
                    rec.op("dve", lambda e, r=r, ps=ps: e.reciprocal(out=r[:], in_=ps[:, 64:65]), reads=[psb], writes=[rb])
                    rec.op("dve", lambda e, r=r, ps=ps, qb=qb, h=h: e.tensor_scalar(
                        out=ytm[:, qb, h * 64:(h + 1) * 64], in0=ps[:, 0:64], scalar1=r[:, 0:1], scalar2=None, op0=ALU.mult),
                        reads=[psb, rb], writes=[ytmb])
            for qb in range(NKB):
                ps, psb = next_ps(g)
                for c in range(4):
                    rec.op("pe", lambda e, ps=ps, qb=qb, c=c: e.transpose(
                        ps[:, c * 128:(c + 1) * 128], ytm[:, qb, c * 128:(c + 1) * 128], g.ident[:]),
                        reads=[ytmb, g.const_b], writes=[psb])
                yt, ytb = ytr.next()
                evac(rec, qb, yt[:].rearrange("p c t -> p (c t)"), ps[:, :], [psb], [ytb])
                t0 = b * T + qb * 128
                rec.dma("sp", YT[:, 0:4, t0:t0 + 128], yt[:], reads=[ytb])
        rec.emit()


def host_layout(inp):
    f32 = np.float32
    d = {}
    NB = inp["c"].shape[0]
    d["ident"] = np.eye(128, dtype=f32)
    d["cmask"] = np.triu(np.ones((128, 128), dtype=f32))
    d["x"] = np.ascontiguousarray(inp["x"].reshape(-1, 1024))
    d["pos"] = np.ascontiguousarray(inp["positions"].astype(np.int32))
    d["cT"] = np.ascontiguousarray(inp["c"].T.reshape(8, 128, NB).transpose(1, 0, 2))
    d["ada_bT"] = np.ascontiguousarray(inp["ada_b"].reshape(2, 48, 128).transpose(2, 0, 1))
    d["norm_mixT"] = np.ascontiguousarray(inp["norm_mix"].reshape(2, 8, 128).transpose(2, 0, 1))
    d["norm_ffnT"] = np.ascontiguousarray(inp["norm_ffn"].reshape(2, 8, 128).transpose(2, 0, 1))
    d["final_normT"] = np.ascontiguousarray(inp["final_norm"].reshape(8, 128).T)
    for k in ["ada_w", "w_out_even", "w_out_odd", "ffn_w_gate", "ffn_w_up", "ffn_w_down", "w_in_even", "mla_w_uq", "w_in_odd"]:
        d[k] = inp[k]
    wi = inp["w_in_even"][0]
    d["w_in_sw"] = np.ascontiguousarray(np.concatenate([wi[:, 656:672], wi[:, 640:656]], axis=1))
    wq = inp["mla_w_uq"][0].reshape(384, 8, 96)
    d["w_uq_sw"] = np.ascontiguousarray(np.concatenate([wq[:, :, 0:64], wq[:, :, 80:96], wq[:, :, 64:80]], axis=2).reshape(384, 768))
    wkv = inp["mla_w_ukv"][0].reshape(256, 8, 128)
    d["w_ukv_r"] = np.ascontiguousarray(np.concatenate([wkv[:, :, 0:64].reshape(256, 512), wkv[:, :, 64:128].reshape(256, 512)], axis=1))
    ctab = np.zeros((128, 8), dtype=f32)
    invf = (1.0 / (10000.0 ** (np.arange(0, 32, 2, dtype=np.float32) / 32))).astype(np.float32)
    for base in (0, 16, 64, 80):
        ctab[base:base + 16, 0] = invf / np.float32(2 * np.pi)
    ctab[0:16, 1] = -1.0
    ctab[16:32, 1] = 1.0
    ctab[64:80, 2] = -MLA_SCALE
    ctab[80:96, 2] = MLA_SCALE
    ctab[:, 3:6] = inp["mla_q_norm"][0].reshape(3, 128).T
    ctab[:, 6:8] = inp["mla_kv_norm"][0].reshape(2, 128).T
    d["ctab"] = ctab
    su = np.triu(np.ones((128, 128), dtype=f32), 1)
    iu = np.triu(np.ones((128, 128), dtype=f32), 0)
    sl = np.tril(np.ones((128, 128), dtype=f32), -1)
    d["rmasks"] = np.ascontiguousarray(np.concatenate([su, iu, sl], axis=1))
    for k in ("rwkv_ln_w", "rwkv_ln_b"):
        d[k] = inp[k]
    d["hg_lbT"] = np.ascontiguousarray(inp["hg_lb_logits"].reshape(2, 8, 128).transpose(2, 0, 1))
    d["hg_onT"] = np.ascontiguousarray(inp["hg_out_norm"][0].reshape(128, 1))
    blk = np.arange(128) // 32
    d["hmask"] = np.ascontiguousarray(((blk[:, None] == blk[None, :]) & (np.arange(128)[:, None] <= np.arange(128)[None, :])).astype(f32))
    d["rw_mu"] = np.ascontiguousarray(inp["rwkv_mu"][0].reshape(14, 128).T)
    d["rw_par"] = np.ascontiguousarray(np.concatenate(
        [inp[k][0].reshape(4, 128).T for k in ("rwkv_w0", "rwkv_a0", "rwkv_k_k", "rwkv_k_a", "rwkv_r_k")], axis=1))
    for k in ("rwkv_w2", "rwkv_a2", "rwkv_g2"):
        d[k] = inp[k]
    return d


NEG_EM05 = -float(np.exp(-0.5))


def phase_rwkv_prep(nc, rec, g, dr, NTOK, T, N=256):
    RP = dr["RP"].rearrange("(k p) t -> p k t", p=128)
    outs = {k: dr[k].rearrange("(c p) t -> p c t", p=128) for k in ("AT", "BT", "KTt", "RTt", "VT", "RKT")}
    PC = dr["PC"].rearrange("(c p) n -> p c n", p=128)
    ntile = NTOK // N
    NS = N // 128
    with ExitStack() as es:
        sb = lambda n, s, d=F32: es.enter_context(nc.sbuf_tensor(uniq(n), s, d))
        par = sb("par", [128, 64]); parb = Buf()
        rec.dma("sp", par[:, 0:14], dr["rw_mu"][:, :], writes=[parb])
        rec.dma("sp", par[:, 28:48], dr["rw_par"][:, :], writes=[parb])
        rec.op("dve", lambda e: e.tensor_scalar(out=par[:, 14:28], in0=par[:, 0:14], scalar1=-1.0, scalar2=1.0, op0=ALU.mult, op1=ALU.add),
               reads=[parb], writes=[parb])
        rec.op("dve", lambda e: e.tensor_scalar(out=par[:, 48:52], in0=par[:, 40:44], scalar1=-1.0, scalar2=1.0, op0=ALU.mult, op1=ALU.add),
               reads=[parb], writes=[parb])
        wst = sb("wst", [128, 512]); wstb = Buf()
        w2b = sb("w2b", [128, 512], BF16); g2b = sb("g2b", [128, 512], BF16); Wb = Buf()
        rec.dma("sp", wst[0:64, :], dr["rwkv_w2"][0, :, :], writes=[wstb])
        rec.dma("sp", wst[64:128, :], dr["rwkv_a2"][0, :, :], writes=[wstb])
        rec.op("dve", lambda e: e.tensor_copy(out=w2b[:], in_=wst[:]), reads=[wstb], writes=[Wb])
        rec.dma("sp", wst[:, :], dr["rwkv_g2"][0, :, :], reads=[], writes=[wstb])
        rec.op("dve", lambda e: e.tensor_copy(out=g2b[:], in_=wst[:]), reads=[wstb], writes=[Wb])
        bones = sb("bones", [128, 128], BF16)
        rec.op("pool", lambda e: e.memset(bones[:], 0.0), writes=[Wb])
        rec.op("pool", lambda e: e.memset(bones[0:64, 0:64], 1.0), writes=[Wb])
        rec.op("pool", lambda e: e.memset(bones[64:128, 64:128], 1.0), writes=[Wb])
        rmask = sb("rmask", [128, N])
        rec.op("pool", lambda e: e.memset(rmask[:], 1.0), writes=[Wb])
        rec.op("pool", lambda e: e.memset(rmask[:].rearrange("p (c t) -> p c t", t=128)[:, :, 0:1], 0.0), writes=[Wb])
        p = sb("p", [128, 14, N]); pb = Buf()
        psh = sb("psh", [128, 14, N]); pshb = Buf()
        tmpr = Rot([sb("mt%d" % i, [128, N]) for i in range(3)])
        wab = sb("wab", [128, N], BF16); wabb = Buf()
        sgl = sb("sgl", [128, N], BF16); sglb = Buf()
        F4 = lambda n: (sb(n, [128, 4, N]), Buf())
        ld, ldb = F4("ld"); bb, bbb = F4("bb"); epos, eposb = F4("epos"); eneg, enegb = F4("eneg"); eprev, eprevb = F4("eprev")
        aa, aab = F4("aa"); kk, kkb = F4("kk"); kp, kpb = F4("kp"); rn, rnb = F4("rn")
        sqk = sb("sqk", [128, 4, N], BF16); sqkb = Buf()
        ob = {k: (sb("o_" + k, [128, 4, N], BF16), Buf()) for k in outs}
        pco = sb("pco", [128, 4, NS]); pcob = Buf()
        gto = Rot([sb("gto%d" % i, [128, 512], BF16) for i in range(2)])
        for ti in range(ntile):
            t0 = ti * N
            rec.dma("sp", p[:], RP[:, :, t0:t0 + N], writes=[pb])
            if t0 % T == 0:
                rec.op("pool", lambda e: e.memset(psh[:, :, 0:1], 0.0), writes=[pshb])
                rec.dma("pool", psh[:, :, 1:N], RP[:, :, t0:t0 + N - 1], writes=[pshb])
            else:
                rec.dma("pool", psh[:, :, :], RP[:, :, t0 - 1:t0 + N - 1], writes=[pshb])
            for j in range(14):
                t1, t1b = tmpr.next()
                rec.op("act", lambda e, j=j, t1=t1: e.activation(out=t1[:], in_=p[:, j, :], func=AF.Identity, scale=par[:, 14 + j:15 + j]),
                       reads=[pb, parb], writes=[t1b])
                rec.op("dve", lambda e, j=j, t1=t1: e.scalar_tensor_tensor(out=p[:, j, :], in0=psh[:, j, :], scalar=par[:, j:j + 1], in1=t1[:],
                                                                          op0=ALU.mult, op1=ALU.add), reads=[pshb, t1b, parb, pb], writes=[pb])
            rec.op("act", lambda e: e.activation(out=wab[0:64, :], in_=p[0:64, 12, :], func=AF.Tanh), reads=[pb], writes=[wabb])
            rec.op("dve", lambda e: e.tensor_copy(out=wab[64:128, :], in_=p[64:128, 12, :]), reads=[pb], writes=[wabb])
            for c in range(4):
                ps, psb = next_ps(g)
                rec.op("pe", lambda e, ps=ps, c=c: e.matmul(ps[:, 0:N], lhsT=w2b[0:64, c * 128:(c + 1) * 128], rhs=wab[0:64, :], start=True, stop=True),
                       reads=[Wb, wabb], writes=[psb])
                rec.op("act", lambda e, ps=ps, c=c: e.activation(out=ld[:, c, :], in_=ps[:, 0:N], func=AF.Sigmoid, bias=par[:, 28 + c:29 + c]),
                       reads=[psb, parb], writes=[ldb])
            for c in range(4):
                ps, psb = next_ps(g)
                rec.op("pe", lambda e, ps=ps, c=c: e.matmul(ps[:, 0:N], lhsT=w2b[64:128, c * 128:(c + 1) * 128], rhs=wab[64:128, :], start=True, stop=True),
                       reads=[Wb, wabb], writes=[psb])
                rec.op("act", lambda e, ps=ps, c=c: e.activation(out=aa[:, c, :], in_=ps[:, 0:N], func=AF.Sigmoid, bias=par[:, 32 + c:33 + c]),
                       reads=[psb, parb], writes=[aab])
            rec.op("act", lambda e: e.activation(out=sgl[:], in_=p[:, 13, :], func=AF.Sigmoid), reads=[pb], writes=[sglb])
            rec.op("dve", lambda e: e.tensor_scalar(out=ld[:], in0=ld[:], scalar1=NEG_EM05, scalar2=None, op0=ALU.mult), reads=[ldb], writes=[ldb])
            for c in range(4):
                rec.op("dve", lambda e, c=c: e.tensor_tensor_scan(out=bb[:, c, :], data0=rmask[:], data1=ld[:, c, :], initial=0.0,
                                                                  op0=ALU.mult, op1=ALU.add), reads=[ldb, Wb], writes=[bbb])
            rec.op("pool", lambda e: e.tensor_tensor(out=eprev[:], in0=bb[:], in1=ld[:], op=ALU.subtract), reads=[bbb, ldb], writes=[eprevb])
            rec.op("act", lambda e: e.activation(out=epos[:], in_=bb[:], func=AF.Exp), reads=[bbb], writes=[eposb])
            rec.op("act", lambda e: e.activation(out=eneg[:], in_=bb[:], func=AF.Exp, scale=-1.0), reads=[bbb], writes=[enegb])
            rec.op("act", lambda e: e.activation(out=eprev[:], in_=eprev[:], func=AF.Exp), reads=[eprevb], writes=[eprevb])
            for c in range(4):
                rec.op("act", lambda e, c=c: e.activation(out=sqk[:, c, :], in_=p[:, 4 + c, :], func=AF.Square, scale=par[:, 36 + c:37 + c]),
                       reads=[pb, parb], writes=[sqkb])
            for c in range(4):
                ps, psb = next_ps(g)
                rec.op("pe", lambda e, ps=ps, c=c: e.matmul(ps[:, 0:N], lhsT=bones[:], rhs=sqk[:, c, :], start=True, stop=True),
                       reads=[Wb, sqkb], writes=[psb])
                rec.op("act", lambda e, ps=ps, c=c: e.activation(out=rn[:, c, :], in_=ps[:, 0:N], func=AF.Sqrt), reads=[psb], writes=[rnb])
            rec.op("dve", lambda e: e.tensor_scalar(out=rn[:], in0=rn[:], scalar1=1e-12, scalar2=None, op0=ALU.max), reads=[rnb], writes=[rnb])
            rec.op("dve", lambda e: e.reciprocal(out=rn[:], in_=rn[:]), reads=[rnb], writes=[rnb])
            for c in range(4):
                rec.op("dve", lambda e, c=c: e.scalar_tensor_tensor(out=kk[:, c, :], in0=p[:, 4 + c, :], scalar=par[:, 36 + c:37 + c], in1=rn[:, c, :],
                                                                   op0=ALU.mult, op1=ALU.mult), reads=[pb, parb, rnb], writes=[kkb])
                rec.op("dve", lambda e, c=c: e.tensor_scalar(out=kp[:, c, :], in0=aa[:, c, :], scalar1=par[:, 40 + c:41 + c], scalar2=par[:, 48 + c:49 + c],
                                                              op0=ALU.mult, op1=ALU.add), reads=[aab, parb], writes=[kpb])
            rec.op("dve", lambda e: e.tensor_tensor(out=kp[:], in0=kp[:], in1=p[:, 4:8, :], op=ALU.mult), reads=[kpb, pb], writes=[kpb])
            o, obb = ob["AT"]
            rec.op("dve", lambda e, o=o: e.scalar_tensor_tensor(out=o[:], in0=kk[:], scalar=-1.0, in1=eprev[:], op0=ALU.mult, op1=ALU.mult),
                   reads=[kkb, eprevb], writes=[obb])
            o, obb = ob["BT"]
            rec.op("pool", lambda e: e.tensor_tensor(out=kk[:], in0=kk[:], in1=aa[:], op=ALU.mult), reads=[kkb, aab], writes=[kkb])
            rec.op("dve", lambda e, o=o: e.tensor_tensor(out=o[:], in0=kk[:], in1=eneg[:], op=ALU.mult), reads=[kkb, enegb], writes=[obb])
            o, obb = ob["KTt"]
            rec.op("pool", lambda e, o=o: e.tensor_tensor(out=o[:], in0=kp[:], in1=eneg[:], op=ALU.mult), reads=[kpb, enegb], writes=[obb])
            o, obb = ob["RTt"]
            rec.op("dve", lambda e, o=o: e.tensor_tensor(out=o[:], in0=p[:, 0:4, :], in1=epos[:], op=ALU.mult), reads=[pb, eposb], writes=[obb])
            o, obb = ob["VT"]
            rec.op("act", lambda e, o=o: e.copy(out=o[:], in_=p[:, 8:12, :]), reads=[pb], writes=[obb])
            o, obb = ob["RKT"]
            for c in range(4):
                rec.op("dve", lambda e, o=o, c=c: e.scalar_tensor_tensor(out=o[:, c, :], in0=p[:, c, :], scalar=par[:, 44 + c:45 + c], in1=kp[:, c, :],
                                                                        op0=ALU.mult, op1=ALU.mult), reads=[pb, parb, kpb], writes=[obb])
            rec.op("pool", lambda e: e.tensor_copy(out=pco[:], in_=epos[:].rearrange("p c (s t) -> p c s t", t=128)[:, :, :, 127]),
                   reads=[eposb], writes=[pcob])
            rec.dma("sp", PC[:, :, t0 // 128:t0 // 128 + NS], pco[:], reads=[pcob])
            for i, k in enumerate(outs):
                o, obb = ob[k]
                rec.dma("sp" if i % 2 == 0 else "pool", outs[k][:, :, t0:t0 + N], o[:], reads=[obb])
            for s in range(NS):
                ps, psb = next_ps(g)
                rec.op("pe", lambda e, ps=ps, s=s: e.matmul(ps[:, :], lhsT=sgl[:, s * 128:(s + 1) * 128], rhs=g2b[:], start=True, stop=True),
                       reads=[sglb, Wb], writes=[psb])
                go, gob = gto.next()
                evac(rec, s, go[:], ps[:, :], [psb], [gob])
                rec.dma("sp", dr["G_tm"][t0 + s * 128:t0 + (s + 1) * 128, :], go[:], reads=[gob])
        rec.emit()


def MM(rec, out, lhsT, rhs, start, stop, reads, writes):
    rec.op("pe", lambda e: e.matmul(out, lhsT=lhsT, rhs=rhs, start=start, stop=stop), reads=reads, writes=writes)


def CP(rec, eng, out, in_, reads, writes):
    if eng == "act":
        rec.op("act", lambda e: e.copy(out=out, in_=in_), reads=reads, writes=writes)
    else:
        rec.op(eng, lambda e: e.tensor_copy(out=out, in_=in_), reads=reads, writes=writes)


def TT(rec, eng, out, in0, in1, op, reads, writes):
    rec.op(eng, lambda e: e.tensor_tensor(out=out, in0=in0, in1=in1, op=op), reads=reads, writes=writes)


def TS(rec, eng, out, in0, s1, op0, reads, writes, s2=None, op1=None):
    if op1 is None:
        rec.op(eng, lambda e: e.tensor_scalar(out=out, in0=in0, scalar1=s1, scalar2=None, op0=op0), reads=reads, writes=writes)
    else:
        rec.op(eng, lambda e: e.tensor_scalar(out=out, in0=in0, scalar1=s1, scalar2=s2, op0=op0, op1=op1), reads=reads, writes=writes)


def STT(rec, out, in0, scalar, in1, op0, op1, reads, writes):
    rec.op("dve", lambda e: e.scalar_tensor_tensor(out=out, in0=in0, scalar=scalar, in1=in1, op0=op0, op1=op1), reads=reads, writes=writes)


def ACTF(rec, out, in_, func, reads, writes, scale=1.0, bias=None):
    if bias is None:
        rec.op("act", lambda e: e.activation(out=out, in_=in_, func=func, scale=scale), reads=reads, writes=writes)
    else:
        rec.op("act", lambda e: e.activation(out=out, in_=in_, func=func, scale=scale, bias=bias), reads=reads, writes=writes)


def phase_rwkv_main(nc, rec, g, dr, NB, T):
    import os
    NCH = int(os.environ.get("RW_NCH", T // 128))
    YT = dr["YT"].rearrange("(k p) t -> p k t", p=128)
    src = {k: dr[k].rearrange("(h k) t -> k h t", k=64) for k in ("AT", "BT", "KTt", "RTt", "VT", "RKT")}
    PCd = dr["PC"].rearrange("(h k) n -> k h n", k=64)
    NH = 8
    NHL = int(os.environ.get("RW_NH", "8"))
    LIM = int(os.environ.get("RW_LIM", "9"))
    with ExitStack() as es:
        sb = lambda n, s, d=F32: es.enter_context(nc.sbuf_tensor(uniq(n), s, d))
        cb = g.const_b
        masks = sb("masks", [128, 512]); mb = Buf()
        mlow = sb("mlow", [128, 128])
        rec.dma("sp", masks[:, 0:256], dr["rmasks"][:, 0:256], writes=[mb])
        rec.dma("sp", masks[:, 256:512], dr["rmasks"][:, 0:256], writes=[mb])
        rec.dma("sp", mlow[:], dr["rmasks"][:, 256:384], writes=[mb])
        lnw = sb("lnw", [128, 512]); lnb = sb("lnb", [128, 512]); lnbuf = Buf()
        rec.dma("sp", lnw[:], dr["rwkv_ln_w"][0:1, :].partition_broadcast(128), writes=[lnbuf])
        rec.dma("sp", lnb[:], dr["rwkv_ln_b"][0:1, :].partition_broadcast(128), writes=[lnbuf])
        eps2 = sb("eps2", [128, 1])
        rec.op("pool", lambda e: e.memset(eps2[:], 64e-5), writes=[lnbuf])
        NBUF = 3
        ARt = [sb("AR%d" % i, [64, NH, 2, 128], BF16) for i in range(NBUF)]
        Btt = [sb("Bt%d" % i, [64, NH, 128], BF16) for i in range(NBUF)]
        Ktt = [sb("Kt%d" % i, [64, NH, 128], BF16) for i in range(NBUF)]
        Vtt = [sb("Vt%d" % i, [64, NH, 128], BF16) for i in range(NBUF)]
        RKt = [sb("RK%d" % i, [64, NH, 128], BF16) for i in range(NBUF)]
        Gtm = [sb("Gtm%d" % i, [128, 512], BF16) for i in range(NBUF)]
        inb = [Buf() for _ in range(NBUF)]
        PCs = sb("PCs", [64, NH, NCH]); PCb = Buf()
        P2 = 2
        TM = [[sb("TM%d_%d" % (h, i), [128, 256], BF16) for i in range(P2)] for h in range(NH)]
        rks = [[sb("rks%d_%d" % (h, i), [128, 2]) for i in range(P2)] for h in range(NH)]
        M12 = [[sb("M12%d_%d" % (h, i), [128, 512], BF16) for i in range(P2)] for h in range(NH)]
        F32R = mybir.dt.float32r
        L0 = [[sb("L0%d_%d" % (h, i), [128, 128], F32R) for i in range(P2)] for h in range(NH)]
        LTr = [[sb("LTr%d_%d" % (h, i), [128, 128], F32R) for i in range(P2)] for h in range(NH)]
        LPW = [[sb("LPW%d_%d" % (p2, i), [128, 2, 256], F32R) for i in range(2)] for p2 in range(NH // 2)]
        Lpw = [[LPW[h // 2][i][:, h % 2, :] for i in range(2)] for h in range(NH)]
        XG = [[sb("XG%d_%d" % (g2, i), [128, 4, 128], F32R) for i in range(P2)] for g2 in range(NH // 4)]
        Xf = [[XG[h // 4][i][:, h % 4, :] for i in range(P2)] for h in range(NH)]
        XB = [[sb("XB%d_%d" % (g2, i), [128, 4, 128], BF16) for i in range(P2)] for g2 in range(NH // 4)]
        Xb = [[XB[h // 4][i][:, h % 4, :] for i in range(P2)] for h in range(NH)]
        Gs = [[sb("Gs%d_%d" % (h, i), [64, 64], BF16) for i in range(P2)] for h in range(NH)]
        RhT = [[sb("RhT%d_%d" % (h, i), [64, 128], BF16) for i in range(P2)] for h in range(NH)]
        hb = [[{k: Buf() for k in ("TM", "rks", "M12", "L0", "X", "Xb", "G", "Rh")} for i in range(P2)] for h in range(NH)]
        Lpb2 = [[Buf() for i in range(2)] for p2 in range(NH // 2)]
        Lpb = [[Lpb2[h // 2][i] for i in range(2)] for h in range(NH)]
        gX = [[Buf() for i in range(P2)] for g2 in range(NH // 4)]
        gXb = [[Buf() for i in range(P2)] for g2 in range(NH // 4)]
        for h in range(NH):
            for i in range(P2):
                hb[h][i]["X"] = gX[h // 4][i]
                hb[h][i]["Xb"] = gXb[h // 4][i]
        Sf = [sb("Sf%d" % h, [64, 64]) for h in range(NH)]
        Sb_ = [sb("Sb%d" % h, [64, 64], BF16) for h in range(NH)]
        St = [sb("St%d" % h, [64, 64]) for h in range(NH)]
        Sfb = [Buf() for h in range(NH)]; Sbb = [Buf() for h in range(NH)]; Stb = [Buf() for h in range(NH)]
        Ytm = [sb("Ytm%d" % i, [128, 512]) for i in range(2)]; Ytmb = [Buf(), Buf()]
        ysq = sb("ysq", [128, 512]); ysqb = Buf()
        st = sb("gnst", [128, 5, 8]); stb = Buf()
        yto = Rot([sb("yto%d" % i, [128, 512], BF16) for i in range(2)])
        gi = 0
        for b in range(NB):
            rec.dma("sp", PCs[:], PCd[:, :, b * NCH:(b + 1) * NCH], writes=[PCb])
            for h in range(NH):
                rec.op("pool", lambda e, h=h: e.memset(Sf[h][:], 0.0), writes=[Sfb[h]])
                rec.op("pool", lambda e, h=h: e.memset(Sb_[h][:], 0.0), writes=[Sbb[h]])
            ctx = {}

            def front(c):
                nonlocal gi
                bi = gi % NBUF
                pi = gi % P2
                gi += 1
                tk = slice(b * T + c * 128, b * T + (c + 1) * 128)
                ib = inb[bi]
                AR, Bt, Kt, Vt, RK, Gt = ARt[bi], Btt[bi], Ktt[bi], Vtt[bi], RKt[bi], Gtm[bi]
                rec.dma("sp", AR[:, :, 0, :], src["AT"][:, :, tk], writes=[ib])
                rec.dma(os.environ.get("RW_DQ", "pool"), AR[:, :, 1, :], src["RTt"][:, :, tk], writes=[ib])
                rec.dma("sp", Bt[:], src["BT"][:, :, tk], writes=[ib])
                rec.dma(os.environ.get("RW_DQ", "pool"), Kt[:], src["KTt"][:, :, tk], writes=[ib])
                rec.dma("sp", Vt[:], src["VT"][:, :, tk], writes=[ib])
                rec.dma(os.environ.get("RW_DQ", "pool"), RK[:], src["RKT"][:, :, tk], writes=[ib])
                rec.dma("sp", Gt[:], dr["G_tm"][tk, :], writes=[ib])
                Y = Ytm[pi]; Yb = Ytmb[pi]
                idb = g.ident_bf
                for h in range(NHL if LIM >= 1 else 0):
                    B = hb[h][pi]
                    pA, pAb = next_ps(g)
                    MM(rec, pA[:, 0:64], Bt[:, h, :], idb[0:64, 0:64], True, True, [ib, cb], [pAb])
                    MM(rec, pA[:, 64:128], Kt[:, h, :], idb[0:64, 0:64], True, True, [ib, cb], [pAb])
                    MM(rec, pA[:, 128:192], Vt[:, h, :], idb[0:64, 0:64], True, True, [ib, cb], [pAb])
                    MM(rec, pA[:, 192:256], AR[:, h, 0, :], idb[0:64, 0:64], True, True, [ib, cb], [pAb])
                    MM(rec, pA[:, 256:320], RK[:, h, :], g.ones_bf[0:64, 0:64], True, True, [ib, cb], [pAb])
                    CP(rec, "act", TM[h][pi][:], pA[:, 0:256], [pAb], [B["TM"]])
                    CP(rec, "act", rks[h][pi][:], pA[:, 256:258], [pAb], [B["rks"]])
                    pL, pLb = next_ps(g)
                    MM(rec, pL[:, 0:128], AR[:, h, 0, :], Bt[:, h, :], True, True, [ib], [pLb])
                    TT(rec, "dve", L0[h][pi][:], pL[:, 0:128], mlow[:], ALU.mult, [pLb, mb], [B["L0"]])
                    if LIM < 2:
                        continue
                    pB, pBb = next_ps(g)
                    arh = AR[:, h, :, :].rearrange("k a t -> k (a t)")
                    MM(rec, pB[:, 0:256], Bt[:, h, :], arh, True, True, [ib], [pBb])
                    MM(rec, pB[:, 256:512], Kt[:, h, :], arh, True, True, [ib], [pBb])
                    TT(rec, "dve", M12[h][pi][:], pB[:, :], masks[:], ALU.mult, [pBb, mb], [B["M12"]])
                    TT(rec, "dve", LTr[h][pi][:], pB[:, 0:128], masks[:, 0:128], ALU.mult, [pBb, mb], [B["L0"]])
                for h in range(NH if LIM >= 3 else 0):
                    B = hb[h][pi]
                    p3, p3b = next_ps(g)
                    MM(rec, p3[:, 0:64], M12[h][pi][:, 256:384], TM[h][pi][:, 128:192], True, True, [B["M12"], B["TM"]], [p3b])
                    CP(rec, "dve", Xf[h][pi][:, 64:128], p3[:, 0:64], [p3b], [B["X"]])
                    CP(rec, "pool", Xf[h][pi][:, 0:64], TM[h][pi][:, 192:256], [B["TM"]], [B["X"]])
                def lops(h, i):
                    if i == 0:
                        return LTr[h][pi][:], L0[h][pi][:], [hb[h][pi]["L0"]]
                    cur = Lpw[h][(i - 1) % 2]
                    return cur[:, 128:256], cur[:, 0:128], [Lpb[h][(i - 1) % 2]]
                for i in range(7 if LIM >= 4 else 0):
                    for g2 in range(2):
                        px, pxb = next_ps(g)
                        for hq in range(4):
                            h = g2 * 4 + hq
                            LT_ap, L_ap, lreads = lops(h, i)
                            MM(rec, px[:, hq * 128:(hq + 1) * 128], LT_ap, Xf[h][pi], True, True, lreads + [gX[g2][pi]], [pxb])
                        if i < 6:
                            for pp in range(2):
                                p2 = g2 * 2 + pp
                                pc, pcb = next_ps(g)
                                for hp in range(2):
                                    h = p2 * 2 + hp
                                    LT_ap, L_ap, lreads = lops(h, i)
                                    MM(rec, pc[:, hp * 256:hp * 256 + 128], LT_ap, L_ap, True, True, lreads, [pcb])
                                    MM(rec, pc[:, hp * 256 + 128:hp * 256 + 256], L_ap, LT_ap, True, True, lreads, [pcb])
                                CP(rec, "act", LPW[p2][i % 2][:].rearrange("p a c -> p (a c)"), pc[:, :], [pcb], [Lpb2[p2][i % 2]])
                        xflat = XG[g2][pi][:].rearrange("p a c -> p (a c)")
                        TT(rec, "dve", xflat, xflat.bitcast(F32), px[:, :], ALU.add, [gX[g2][pi], pxb], [gX[g2][pi]])
                        if i == 6:
                            CP(rec, "pool", XB[g2][pi][:].rearrange("p a c -> p (a c)"), xflat.bitcast(F32), [gX[g2][pi]], [gXb[g2][pi]])
                for h in range(NH if LIM >= 5 else 0):
                    B = hb[h][pi]
                    p5, p5b = next_ps(g)
                    MM(rec, p5[0:64, 0:64], Xb[h][pi][:, 0:64], TM[h][pi][:, 0:64], True, True, [B["Xb"], B["TM"]], [p5b])
                    CP(rec, "act", Gs[h][pi][:], p5[0:64, 0:64], [p5b], [B["G"]])
                    p5r, p5rb = next_ps(g)
                    MM(rec, p5r[0:64, 0:128], Xb[h][pi][:, 0:64], M12[h][pi][:, 128:256], True, True, [B["Xb"], B["M12"]], [p5rb])
                    TT(rec, "dve", RhT[h][pi][:], p5r[0:64, 0:128], AR[:, h, 1, :], ALU.add, [p5rb, ib], [B["Rh"]])
                ctx[c] = (bi, pi, tk, ib, AR, Gt, Y, Yb)

            def back(c):
                bi, pi, tk, ib, AR, Gt, Y, Yb = ctx.pop(c)
                for h in range(NH if LIM >= 6 else 0):
                    B = hb[h][pi]
                    p6, p6b = next_ps(g)
                    U = Xb[h][pi][:, 64:128]
                    Vm = TM[h][pi][:, 128:192]
                    MM(rec, p6[:, 0:64], M12[h][pi][:, 128:256], U, True, False, [B["M12"], B["Xb"]], [p6b])
                    MM(rec, p6[:, 0:64], M12[h][pi][:, 384:512], Vm, False, False, [B["M12"], B["TM"]], [p6b])
                    MM(rec, p6[:, 0:64], RhT[h][pi][:], Sb_[h][:], False, True, [B["Rh"], Sbb[h]], [p6b])
                    p7, p7b = next_ps(g)
                    MM(rec, p7[0:64, 0:64], TM[h][pi][:, 0:64], U, True, False, [B["TM"], B["Xb"]], [p7b])
                    MM(rec, p7[0:64, 0:64], TM[h][pi][:, 64:128], Vm, False, False, [B["TM"]], [p7b])
                    MM(rec, p7[0:64, 0:64], Gs[h][pi][:], Sb_[h][:], False, True, [B["G"], Sbb[h]], [p7b])
                    CP(rec, "act", Y[:, h * 64:(h + 1) * 64], p6[:, 0:64], [p6b], [Yb])
                    TT(rec, "dve", St[h][:], Sf[h][:], p7[0:64, 0:64], ALU.add, [Sfb[h], p7b], [Stb[h]])
                    TS(rec, "pool", Sf[h][:], St[h][:], PCs[:, h, c:c + 1], ALU.mult, [Stb[h], PCb], [Sfb[h]])
                    ACTF(rec, Sb_[h][:], St[h][:], AF.Identity, [Stb[h], PCb], [Sbb[h]], scale=PCs[:, h, c:c + 1])
                if LIM < 7:
                    return
                Y3 = Y[:].rearrange("p (h v) -> p h v", v=64)
                rec.op("dve", lambda e, Y3=Y3: e.tensor_reduce(out=st[:, 0, :], in_=Y3, op=ALU.add, axis=AX.X), reads=[Yb], writes=[stb])
                ACTF(rec, ysq[:], Y[:], AF.Square, [Yb], [ysqb])
                rec.op("dve", lambda e: e.tensor_reduce(out=st[:, 1, :], in_=ysq[:].rearrange("p (h v) -> p h v", v=64), op=ALU.add, axis=AX.X),
                       reads=[ysqb], writes=[stb])
                TS(rec, "dve", st[:, 2, :], st[:, 0, :], 1.0 / 64, ALU.mult, [stb], [stb])
                TT(rec, "dve", st[:, 3, :], st[:, 2, :], st[:, 2, :], ALU.mult, [stb], [stb])
                STT(rec, st[:, 3, :], st[:, 1, :], 1.0 / 64, st[:, 3, :], ALU.mult, ALU.subtract, [stb], [stb])
                ACTF(rec, st[:, 3, :], st[:, 3, :], AF.Sqrt, [stb, lnbuf], [stb], bias=eps2[:])
                rec.op("dve", lambda e: e.reciprocal(out=st[:, 4, :], in_=st[:, 3, :]), reads=[stb], writes=[stb])
                for h in range(NH):
                    TS(rec, "dve" if h % 2 else "pool", Y[:, h * 64:(h + 1) * 64], Y[:, h * 64:(h + 1) * 64], st[:, 2, h:h + 1], ALU.subtract,
                       [Yb, stb], [Yb], s2=st[:, 4, h:h + 1], op1=ALU.mult)
                TT(rec, "pool", Y[:], Y[:], lnw[:], ALU.mult, [Yb, lnbuf], [Yb])
                TT(rec, "pool", Y[:], Y[:], lnb[:], ALU.add, [Yb, lnbuf], [Yb])
                for h in range(NH):
                    STT(rec, Y[:, h * 64:(h + 1) * 64], TM[h][pi][:, 128:192], rks[h][pi][:, 0:1], Y[:, h * 64:(h + 1) * 64],
                        ALU.mult, ALU.add, [hb[h][pi]["TM"], hb[h][pi]["rks"], Yb], [Yb])
                TT(rec, "dve", Y[:], Y[:], Gt[:], ALU.mult, [Yb, ib], [Yb])
                pT, pTb = next_ps(g)
                for cc in range(4):
                    rec.op("pe", lambda e, pT=pT, cc=cc, Y=Y: e.transpose(pT[:, cc * 128:(cc + 1) * 128], Y[:, cc * 128:(cc + 1) * 128], g.ident[:]),
                           reads=[Yb, cb], writes=[pTb])
                yo, yob = yto.next()
                CP(rec, "act", yo[:], pT[:, :], [pTb], [yob])
                rec.dma("sp", YT[:, 4:8, tk], yo[:].rearrange("p (c t) -> p c t", t=128), reads=[yob])

            for c in range(NCH + 1):
                if c < NCH:
                    front(c)
                if c >= 1:
                    back(c - 1)
        rec.emit()


def phase_pre1(nc, rec, g, dr, NTOK, T, N=256):
    l = 1
    XT = dr["XT"].rearrange("(k p) t -> p k t", p=128)
    fo = {k: dr[k].rearrange("(h p) t -> p h t", p=128) for k in ("QTl", "KTl", "KHl", "SGT")}
    PCh = dr["PCh"].rearrange("(h p) n -> p h n", p=128)
    ntile = NTOK // N
    NS = N // 128
    NC32 = 8 * N // 32
    with ExitStack() as es:
        sb = lambda n, s, d=F32: es.enter_context(nc.sbuf_tensor(uniq(n), s, d))
        W = sb("W1", [128, KC, 4096], BF16); Wb = Buf()
        st_rot = Rot([sb("wst%d" % i, [128, 1024]) for i in range(2)])
        jobs = []
        for kk in range(KC):
            for q4 in range(4):
                cs = slice(q4 * 1024, (q4 + 1) * 1024)
                jobs.append((dr["w_in_odd"][0, kk * 128:(kk + 1) * 128, cs], W[:, kk, cs], Wb))
        load_cast(rec, st_rot, jobs)
        lbl = sb("lbl", [128, 2, 8]); lbb = Buf()
        lb = sb("lb", [128, 8]); oml = sb("oml", [128, 8])
        rec.dma("sp", lbl[:], dr["hg_lbT"][:, :, :], writes=[lbb])
        TT(rec, "dve", lb[:], lbl[:, 1, :], lbl[:, 0, :], ALU.subtract, [lbb], [lbb])
        ACTF(rec, lb[:], lb[:], AF.Sigmoid, [lbb], [lbb])
        TS(rec, "dve", oml[:], lb[:], -1.0, ALU.mult, [lbb], [lbb], s2=1.0, op1=ALU.add)
        m32 = sb("m32", [128, 8 * N])
        rec.op("pool", lambda e: e.memset(m32[:], 1.0), writes=[lbb])
        rec.op("pool", lambda e: e.memset(m32[:].rearrange("p (c t) -> p c t", t=32)[:, :, 0:1], 0.0), writes=[lbb])
        x = sb("x", [128, KC, N]); xb = Buf()
        sq = sb("sq", [128, KC, N], BF16); sqb = Buf()
        hT = sb("hT", [128, KC, N], BF16); hb = Buf()
        tr = Rot([sb("tmp%d" % i, [128, N]) for i in range(3)])
        rstd = sb("rstd", [128, N]); rstdb = Buf()
        raw = sb("raw", [128, 24, N]); rawb = Buf()
        lf = sb("lf", [128, 8 * N]); lfb = Buf()
        bt = sb("bt", [128, 8 * N]); btb = Buf()
        ept = sb("ept", [128, 8 * N]); epb = Buf()
        ent = sb("ent", [128, 8 * N]); enb = Buf()
        ect = sb("ect", [128, 8 * N]); ecb = Buf()
        pct = sb("pct", [128, NC32]); pcb_ = Buf()
        ob = {k: (sb("o_" + k, [128, 8, N], BF16), Buf()) for k in fo}
        ito = Rot([sb("ito%d" % i, [128, 1024], BF16) for i in range(2)])
        for ti in range(ntile):
            b = (ti * N) // T
            t0 = ti * N
            rec.dma("sp", x[:], XT[:, :, t0:t0 + N], writes=[xb])
            t1, t1b = tr.next()
            rms_stats(rec, g, x[:], xb, KC, N, sq, sqb, t1, t1b, rstd, rstdb, D)
            for m in range(KC):
                t1, t1b = tr.next()
                TT(rec, "dve", t1[:], x[:, m, :], rstd[:], ALU.mult, [xb, rstdb], [t1b])
                ACTF(rec, hT[:, m, :], t1[:], AF.Identity, [t1b, g.Amod_b, g.modT_b], [hb],
                     scale=g.Amod[:, l, 0, m, b:b + 1], bias=g.modT[:, l, 0 + m, b:b + 1])
            for j in range(24):
                grp, h = j // 8, j % 8
                c0 = (0, 1024, 3072)[grp] + h * 128
                ps, psb = next_ps(g)
                for kk in range(KC):
                    MM(rec, ps[:, 0:N], W[:, kk, c0:c0 + 128], hT[:, kk, :], kk == 0, kk == KC - 1, [Wb, hb], [psb])
                evac(rec, j, raw[:, j, :], ps[:, 0:N], [psb], [rawb])
            rq = raw[:, 0:8, :].rearrange("p h t -> p (h t)")
            rf = raw[:, 8:16, :].rearrange("p h t -> p (h t)")
            rg = raw[:, 16:24, :].rearrange("p h t -> p (h t)")
            ACTF(rec, rq, rq, AF.Silu, [rawb], [rawb])
            o, obb = ob["SGT"]
            ACTF(rec, o[:].rearrange("p h t -> p (h t)"), rg, AF.Silu, [rawb], [obb])
            ACTF(rec, rf, rf, AF.Sigmoid, [rawb], [rawb])
            for h in range(8):
                TS(rec, "pool" if h % 2 else "dve", raw[:, 8 + h, :], raw[:, 8 + h, :], oml[:, h:h + 1], ALU.mult, [rawb, lbb], [rawb],
                   s2=lb[:, h:h + 1], op1=ALU.add)
            ACTF(rec, lf[:], rf, AF.Ln, [rawb], [lfb])
            TS(rec, "pool", rf, rf, -1.0, ALU.mult, [rawb], [rawb], s2=1.0, op1=ALU.add)
            rec.op("dve", lambda e: e.tensor_tensor_scan(out=bt[:], data0=m32[:], data1=lf[:], initial=0.0, op0=ALU.mult, op1=ALU.add),
                   reads=[lfb, lbb], writes=[btb])
            b3 = bt[:].rearrange("p (c t) -> p c t", t=32)
            ACTF(rec, ept[:], bt[:], AF.Exp, [btb], [epb])
            ACTF(rec, ent[:], bt[:], AF.Exp, [btb], [enb], scale=-1.0)
            TT(rec, "dve", ect[:].rearrange("p (c t) -> p c t", t=32), b3[:, :, 31:32].broadcast_to([128, NC32, 32]), b3, ALU.subtract,
               [btb], [ecb])
            ACTF(rec, ect[:], ect[:], AF.Exp, [ecb], [ecb])
            ACTF(rec, pct[:], b3[:, :, 31], AF.Exp, [btb], [pcb_])
            o, obb = ob["QTl"]
            TT(rec, "dve", o[:].rearrange("p h t -> p (h t)"), rq, ept[:], ALU.mult, [rawb, epb], [obb])
            o, obb = ob["KTl"]
            TT(rec, "pool", o[:].rearrange("p h t -> p (h t)"), rf, ent[:], ALU.mult, [rawb, enb], [obb])
            o, obb = ob["KHl"]
            TT(rec, "dve", o[:].rearrange("p h t -> p (h t)"), rf, ect[:], ALU.mult, [rawb, ecb], [obb])
            for i, k in enumerate(fo):
                o, obb = ob[k]
                rec.dma("sp" if i % 2 == 0 else "pool", fo[k][:, :, t0:t0 + N], o[:], reads=[obb])
            rec.dma("sp", PCh[:, :, t0 // 32:(t0 + N) // 32], pct[:].rearrange("p (h c) -> p h c", h=8), reads=[pcb_])
            for s in range(NS):
                io, iob = ito.next()
                for n in range(2):
                    ps, psb = next_ps(g)
                    for kk in range(KC):
                        MM(rec, ps[:, :], hT[:, kk, s * 128:(s + 1) * 128], W[:, kk, 2048 + n * 512:2048 + (n + 1) * 512],
                           kk == 0, kk == KC - 1, [Wb, hb], [psb])
                    evac(rec, n, io[:, n * 512:(n + 1) * 512], ps[:, :], [psb], [iob])
                rec.dma("sp", dr["I_tm"][t0 + s * 128:t0 + (s + 1) * 128, :], io[:], reads=[iob])
        rec.emit()


def phase_hgrn(nc, rec, g, dr, NB, T, N=512):
    NG = N // 128
    fi = {k: dr[k].rearrange("(h p) t -> p h t", p=128) for k in ("QTl", "KTl", "KHl", "SGT")}
    PCh = dr["PCh"].rearrange("(h p) n -> p h n", p=128)
    YT = dr.get("YT1", dr["YT"]).rearrange("(h p) t -> p h t", p=128)
    NH = 8
    import os
    HL = int(os.environ.get("HG_LIM", "9"))
    with ExitStack() as es:
        sb = lambda n, s, d=F32: es.enter_context(nc.sbuf_tensor(uniq(n), s, d))
        cb = g.const_b
        bmask = sb("bmask", [128, 128]); mb = Buf()
        rec.dma("sp", bmask[:], dr["hmask"][:, :], writes=[mb])
        onorm = sb("onorm", [128, 1])
        rec.dma("sp", onorm[:], dr["hg_onT"][:, :], writes=[mb])
        NBUF = 2
        ft = {k: [sb("f_%s%d" % (k, i), [128, NH, N], BF16) for i in range(NBUF)] for k in fi}
        itm = [sb("itm%d" % i, [128, NG, 1024], BF16) for i in range(NBUF)]
        pcs = [sb("pcs%d" % i, [128, NH, N // 32]) for i in range(NBUF)]
        inb = [Buf() for _ in range(NBUF)]
        yt = [sb("yt%d" % i, [128, NH, N], BF16) for i in range(2)]; ytb = [Buf(), Buf()]
        Sf = [sb("Sf%d" % h, [128, 128]) for h in range(NH)]
        Sb_ = [sb("Sbf%d" % h, [128, 128], BF16) for h in range(NH)]
        Sfb = [Buf() for h in range(NH)]; Sbb = [Buf() for h in range(NH)]
        khm = [[sb("khm%d_%d" % (h, i), [128, 128], BF16) for i in range(2)] for h in range(NH)]
        attm = [[sb("attm%d_%d" % (h, i), [128, 128], BF16) for i in range(2)] for h in range(NH)]
        khb = [[Buf() for i in range(2)] for h in range(NH)]
        atb = [[Buf() for i in range(2)] for h in range(NH)]
        oT = [sb("oT%d" % h, [128, 128]) for h in range(NH)]; oTb = [Buf() for h in range(NH)]
        osq = [sb("osq%d" % h, [128, 128], BF16) for h in range(NH)]; osqb = [Buf() for h in range(NH)]
        sd = [sb("sd%d" % h, [128, 128]) for h in range(NH)]; sdb = [Buf() for h in range(NH)]
        gi = 0
        ti_glob = 0
        for b in range(NB):
            for h in range(NH):
                rec.op("pool", lambda e, h=h: e.memset(Sf[h][:], 0.0), writes=[Sfb[h]])
                rec.op("pool", lambda e, h=h: e.memset(Sb_[h][:], 0.0), writes=[Sbb[h]])
            for tt in range(T // N):
                bi = ti_glob % NBUF
                ti_glob += 1
                t0 = b * T + tt * N
                ib = inb[bi]
                for i, k in enumerate(fi):
                    rec.dma("sp" if i % 2 == 0 else "pool", ft[k][bi][:], fi[k][:, :, t0:t0 + N], writes=[ib])
                rec.dma("sp", itm[bi][:], dr["I_tm"][t0:t0 + N, :].rearrange("(g p) d -> p g d", p=128), writes=[ib])
                rec.dma("sp", pcs[bi][:], PCh[:, :, t0 // 32:(t0 + N) // 32], writes=[ib])
                Q, K, KH, SG, IT, PC = ft["QTl"][bi], ft["KTl"][bi], ft["KHl"][bi], ft["SGT"][bi], itm[bi], pcs[bi]
                Yt, Ytb = yt[bi], ytb[bi]
                for gq in range(NG):
                    pi = gi % 2
                    gi += 1
                    gs = slice(gq * 128, (gq + 1) * 128)
                    for h in range(NH):
                        pK, pKb = next_ps(g)
                        MM(rec, pK[:, 0:128], KH[:, h, gs], g.ident_bf[:], True, True, [ib, cb], [pKb])
                        CP(rec, "act", khm[h][pi][:], pK[:, 0:128], [pKb], [khb[h][pi]])
                        pA, pAb = next_ps(g)
                        MM(rec, pA[:, 0:128], K[:, h, gs], Q[:, h, gs], True, True, [ib], [pAb])
                        TT(rec, "dve", attm[h][pi][:], pA[:, 0:128], bmask[:], ALU.mult, [pAb, mb], [atb[h][pi]])
                    for half in range(2 if HL >= 2 else 0):
                        hs = range(half * 4, half * 4 + 4)
                        pS = {}; pO = {}
                        pSj = [next_ps(g) for j in range(4)]
                        for h in hs:
                            hl = h - half * 4
                            for j in range(4):
                                r0, r1 = (32 * j, 32 * j + 32) if j < 3 else (64, 128)
                                MM(rec, pSj[j][0][:, hl * 128:(hl + 1) * 128], khm[h][pi][r0:r1, :],
                                   IT[r0:r1, gq, h * 128:(h + 1) * 128], True, True, [khb[h][pi], ib], [pSj[j][1]])
                        if HL < 3:
                            continue
                        for h in hs:
                            pO[h] = next_ps(g)
                            MM(rec, pO[h][0][:, 0:128], IT[:, gq, h * 128:(h + 1) * 128], attm[h][pi][:], True, False, [ib, atb[h][pi]], [pO[h][1]])
                        for j in range(4 if HL >= 4 else 0):
                            for h in hs:
                                c0 = gq * 128 + 32 * j
                                MM(rec, pO[h][0][:, 32 * j:32 * j + 32], Sb_[h][:], Q[:, h, c0:c0 + 32], False, j == 3, [Sbb[h], ib], [pO[h][1]])
                                ch = gq * 4 + j
                                hl = h - half * 4
                                STT(rec, Sf[h][:], Sf[h][:], PC[:, h, ch:ch + 1], pSj[j][0][:, hl * 128:(hl + 1) * 128], ALU.mult, ALU.add,
                                    [Sfb[h], ib, pSj[j][1]], [Sfb[h]])
                                if j == 3:
                                    TT(rec, "dve", Sf[h][:], Sf[h][:], pSj[2][0][:, hl * 128:(hl + 1) * 128], ALU.subtract, [Sfb[h], pSj[2][1]], [Sfb[h]])
                                CP(rec, "act", Sb_[h][:], Sf[h][:], [Sfb[h]], [Sbb[h]])
                        for h in (hs if HL >= 5 else []):
                            CP(rec, "act", oT[h][:], pO[h][0][:, 0:128], [pO[h][1]], [oTb[h]])
                            ACTF(rec, osq[h][:], oT[h][:], AF.Square, [oTb[h]], [osqb[h]])
                            pZ, pZb = next_ps(g)
                            MM(rec, pZ[:, 0:128], g.ones_bf[:], osq[h][:], True, True, [cb, osqb[h]], [pZb])
                            ACTF(rec, sd[h][:], pZ[:, 0:128], AF.Ln, [pZb, cb], [sdb[h]], scale=1.0 / 128, bias=g.epsb[:])
                            ACTF(rec, sd[h][:], sd[h][:], AF.Exp, [sdb[h]], [sdb[h]], scale=-0.5)
                            STT(rec, oT[h][:], oT[h][:], onorm[:, 0:1], sd[h][:], ALU.mult, ALU.mult, [oTb[h], sdb[h], mb], [oTb[h]])
                            TT(rec, "dve", Yt[:, h, gs], oT[h][:], SG[:, h, gs], ALU.mult, [oTb[h], ib], [Ytb])
                rec.dma("sp", YT[:, :, t0:t0 + N], Yt[:], reads=[Ytb])
        rec.emit()


SCRATCH = None


def build_program(NB, T, ext_scratch=False):
    NTOK = NB * T
    nc = bass.Bass("TRN2", target_bir_lowering=False)
    dr = {}
    shapes = input_shapes(NB, T)
    for k, (shp, dt) in shapes.items():
        dr[k] = nc.dram_tensor(k, list(shp), dt, kind="ExternalInput").ap()
    dr["out"] = nc.dram_tensor("out", [NTOK, 1024], F32, kind="ExternalOutput").ap()
    kind = "ExternalOutput" if ext_scratch else "Internal"

    def scr(name, shape, dt):
        dr[name] = nc.dram_tensor(name, shape, dt, kind=kind).ap()
    scr("XT", [1024, NTOK], F32)
    scr("RP", [1792, NTOK], F32)
    scr("QT", [8, 96, NTOK], BF16)
    scr("KT", [8, 96, NTOK], BF16)
    scr("V", [NTOK, 512], BF16)
    scr("YT", [1024, NTOK], BF16)
    for k in ("AT", "BT", "KTt", "RTt", "VT", "RKT"):
        scr(k, [512, NTOK], BF16)
    scr("G_tm", [NTOK, 512], BF16)
    scr("PC", [512, NTOK // 128], F32)
    for k in ("QTl", "KTl", "KHl", "SGT"):
        scr(k, [1024, NTOK], BF16)
    scr("I_tm", [NTOK, 1024], BF16)
    scr("PCh", [1024, NTOK // 32], F32)
    if ext_scratch:
        scr("YT1", [1024, NTOK], BF16)
    g = G()
    with ExitStack() as es:
        rec = Rec(nc, es)
        setup_globals(nc, es, g, NB)
        phase_mods(nc, rec, g, dr)
        phase_pre0(nc, rec, g, dr, NTOK, T)
        phase_mla(nc, rec, g, dr, NB, T)
        phase_rwkv_prep(nc, rec, g, dr, NTOK, T)
        phase_rwkv_main(nc, rec, g, dr, NB, T)
        phase_post(nc, rec, g, dr, 0, NTOK, T, last=False)
        phase_pre1(nc, rec, g, dr, NTOK, T)
        phase_hgrn(nc, rec, g, dr, NB, T)
        phase_post(nc, rec, g, dr, 1, NTOK, T, last=True)
    return nc, rec


def input_shapes(NB, T):
    NTOK = NB * T
    f = F32
    return {
        "ident": ((128, 128), f), "cmask": ((128, 128), f), "x": ((NTOK, 1024), f), "pos": ((NB, T), I32),
        "cT": ((128, 8, NB), f), "ada_bT": ((128, 2, 48), f), "norm_mixT": ((128, 2, 8), f), "norm_ffnT": ((128, 2, 8), f),
        "final_normT": ((128, 8), f), "ada_w": ((2, 1024, 6144), f), "w_out_even": ((1, 1024, 1024), f),
        "w_out_odd": ((1, 1024, 1024), f), "ffn_w_gate": ((2, 1024, 2816), f), "ffn_w_up": ((2, 1024, 2816), f),
        "ffn_w_down": ((2, 2816, 1024), f), "w_in_even": ((1, 1024, 2464), f), "mla_w_uq": ((1, 384, 768), f),
        "w_in_odd": ((1, 1024, 4096), f), "w_in_sw": ((1024, 32), f), "w_uq_sw": ((384, 768), f), "w_ukv_r": ((256, 1024), f),
        "ctab": ((128, 8), f), "rmasks": ((128, 384), f), "rwkv_ln_w": ((1, 512), f), "rwkv_ln_b": ((1, 512), f),
        "hg_lbT": ((128, 2, 8), f), "hg_onT": ((128, 1), f), "hmask": ((128, 128), f), "rw_mu": ((128, 14), f),
        "rw_par": ((128, 20), f), "rwkv_w2": ((1, 64, 512), f), "rwkv_a2": ((1, 64, 512), f), "rwkv_g2": ((1, 128, 512), f),
    }


_CACHE = {}


def kernel(**inputs):
    from concourse.bass_utils import run_bass_kernel_spmd
    NB, T, NCORE = 4, 2048, 8
    inputs = {k: np.asarray(v) for k, v in inputs.items()}
    if "nc" not in _CACHE:
        _CACHE["nc"] = build_program(NB, T)[0]
    nc = _CACHE["nc"]
    shared = None
    in_maps = []
    for i in range(NCORE):
        sl = slice(i * NB, (i + 1) * NB)
        inp = dict(inputs)
        inp["x"] = inputs["x"][sl]
        inp["c"] = inputs["c"][sl]
        inp["positions"] = inputs["positions"][sl]
        if shared is None:
            shared = host_layout(inp)
            d = dict(shared)
        else:
            d = dict(shared)
            d["x"] = np.ascontiguousarray(inp["x"].reshape(-1, 1024))
            d["pos"] = np.ascontiguousarray(inp["positions"].astype(np.int32))
            d["cT"] = np.ascontiguousarray(inp["c"].T.reshape(8, 128, NB).transpose(1, 0, 2))
        in_maps.append(d)
    res = run_bass_kernel_spmd(nc, in_maps, core_ids=list(range(NCORE)))
    outs = [np.asarray(r["out"]).reshape(NB, T, 1024) for r in res.results]
    return np.concatenate(outs, axis=0).astype(np.float32)
```

```python
import numpy as np
import concourse.bass as bass
import concourse.mybir as mybir

F32 = mybir.dt.float32
BF16 = mybir.dt.bfloat16
I32 = mybir.dt.int32
AF = mybir.ActivationFunctionType
ALU = mybir.AluOpType
AX = mybir.AxisListType

ENGS = ("pe", "dve", "act", "pool", "sp")
NSLOT = 8


class Buf:
    __slots__ = ("name", "w", "r")

    def __init__(self, name=""):
        self.name = name
        self.w = None
        self.r = {}


class Rec:
    def __init__(self, nc, es, same_eng_sync=True):
        self.nc = nc
        self.sem = {}
        for e in ENGS:
            self.sem[e] = es.enter_context(nc.semaphore("s_" + e))
        self.cnt = {e: 0 for e in ENGS}
        self.q = {e: [] for e in ENGS}
        self.seen = {e: {} for e in ENGS}
        self.slots = {}
        for e in ("sp", "act", "pool"):
            self.slots[e] = [[es.enter_context(nc.semaphore("d_%s%d" % (e, i))), 0] for i in range(NSLOT)]
        self.slot_i = {e: 0 for e in self.slots}
        self.same = same_eng_sync
        self.nops = 0

    def _need(self, eng, ev, waits):
        if ev is None:
            return
        key, val, src = ev
        if src == eng and key[0] == "e":
            if eng == "pe" or not self.same:
                return
        if self.seen[eng].get(key, 0) >= val:
            return
        self.seen[eng][key] = val
        waits.append((key, val))

    def _deps(self, eng, reads, writes):
        waits = []
        for b in reads:
            self._need(eng, b.w, waits)
        for b in writes:
            self._need(eng, b.w, waits)
            for ev in b.r.values():
                self._need(eng, ev, waits)
        return waits

    def _mark(self, ev, reads, writes):
        for b in reads:
            b.r[ev[0]] = ev
        for b in writes:
            b.w = ev
            b.r = {}

    def op(self, eng, fn, reads=(), writes=()):
        waits = self._deps(eng, reads, writes)
        self.cnt[eng] += 1
        ev = (("e", eng), self.cnt[eng], eng)
        self.q[eng].append((waits, fn, ("e", eng), 1))
        self._mark(ev, reads, writes)
        self.nops += 1

    def dma(self, eng, out, in_, reads=(), writes=()):
        sl = self.slots[eng]
        i = self.slot_i[eng]
        self.slot_i[eng] = (i + 1) % NSLOT
        key = ("d", eng, i)
        waits = []
        if sl[i][1] > 0:
            self._need(eng, (key, sl[i][1], "dma"), waits)
        waits += self._deps(eng, reads, writes)
        sl[i][1] += 16
        ev = (key, sl[i][1], "dma")
        self.q[eng].append((waits, lambda e, o=out, s=in_: e.dma_start(out=o, in_=s), key, 16))
        self._mark(ev, reads, writes)
        self.nops += 1

    def _semh(self, key):
        if key[0] == "e":
            return self.sem[key[1]]
        return self.slots[key[1]][key[2]][0]

    def drain_dmas(self):
        for e in self.slots:
            for i, (s, v) in enumerate(self.slots[e]):
                if v > 0:
                    key = ("d", e, i)
                    if self.seen[e].get(key, 0) < v:
                        self.seen[e][key] = v
                        self.q[e].append(([(key, v)], None, None, 0))

    def emit(self):
        self.drain_dmas()
        nc = self.nc
        rec = self

        def run(engname, handle):
            for waits, fn, key, inc in rec.q[engname]:
                for (k, v) in waits:
                    handle.wait_ge(rec._semh(k), v)
                if fn is not None:
                    fn(handle).then_inc(rec._semh(key), inc)
            rec.q[engname] = []

        with nc.Block() as block:
            @block.tensor
            def _(e):
                run("pe", e)

            @block.vector
            def _(e):
                run("dve", e)

            @block.scalar
            def _(e):
                run("act", e)

            @block.gpsimd
            def _(e):
                run("pool", e)

            @block.sync
            def _(e):
                run("sp", e)


from contextlib import ExitStack
import numpy as np
import concourse.bass as bass
import concourse.mybir as mybir

D = 1024
KC = 8
FH = 2816
JH = 22
EPS = 1e-6


class G:
    pass


_uid = [0]


def uniq(n):
    _uid[0] += 1
    return "sb%d_%s" % (_uid[0], n)


def pool_of(n, name):
    return [Buf("%s%d" % (name, i)) for i in range(n)]


class Rot:
    def __init__(self, tiles):
        self.t = tiles
        self.b = [Buf() for _ in tiles]
        self.i = 0

    def next(self):
        i = self.i
        self.i = (i + 1) % len(self.t)
        return self.t[i], self.b[i]


def setup_globals(nc, es, g, NB):
    g.NB = NB
    sb = lambda n, s, d=F32: es.enter_context(nc.sbuf_tensor(uniq(n), s, d))
    g.modT = sb("modT", [128, 2, 48, NB])
    g.modT_b = Buf("modT")
    g.Amod = sb("Amod", [128, 2, 2, KC, NB])
    g.Amod_b = Buf("Amod")
    g.ident = sb("ident", [128, 128])
    g.ident_bf = sb("ident_bf", [128, 128], BF16)
    g.ones_bf = sb("ones_bf", [128, 128], BF16)
    g.const_b = Buf("const")
    g.epsb = sb("epsb", [128, 1])
    g.ps = [es.enter_context(nc.psum_tensor("ps%d" % i, [128, 512], F32)) for i in range(8)]
    g.psb = [Buf("ps%d" % i) for i in range(8)]
    g.ps_i = 0


def next_ps(g):
    i = g.ps_i
    g.ps_i = (i + 1) % 8
    return g.ps[i], g.psb[i]


def phase_mods(nc, rec, g, dr):
    NB = g.NB
    with ExitStack() as es:
        sb = lambda n, s, d=F32: es.enter_context(nc.sbuf_tensor(uniq(n), s, d))
        cT = sb("cT", [128, KC, NB]); cT_b = Buf()
        condT = sb("condT", [128, KC, NB]); condT_b = Buf()
        adab = sb("adab", [128, 2, 48]); adab_b = Buf()
        gains = sb("gains", [128, 2, 2, KC]); gains_b = Buf()
        wst = [sb("adaw%d" % i, [128, KC, 1024]) for i in range(2)]
        wst_b = [Buf(), Buf()]
        rec.dma("sp", g.ident[:], dr["ident"][:, :], writes=[g.const_b])
        rec.op("pool", lambda e: e.memset(g.ones_bf[:], 1.0), writes=[g.const_b])
        rec.op("pool", lambda e: e.memset(g.epsb[:], EPS), writes=[g.const_b])
        rec.op("dve", lambda e: e.tensor_copy(out=g.ident_bf[:], in_=g.ident[:]), reads=[g.const_b], writes=[g.const_b])
        rec.dma("sp", cT[:], dr["cT"][:, :, :], writes=[cT_b])
        rec.dma("sp", adab[:], dr["ada_bT"][:, :, :], writes=[adab_b])
        rec.dma("sp", gains[:, 0], dr["norm_mixT"][:, :, :], writes=[gains_b])
        rec.dma("sp", gains[:, 1], dr["norm_ffnT"][:, :, :], writes=[gains_b])
        rec.op("act", lambda e: e.activation(out=condT[:], in_=cT[:], func=AF.Silu), reads=[cT_b], writes=[condT_b])
        it = 0
        for l in range(2):
            for pc in range(6):
                w, wb = wst[it % 2], wst_b[it % 2]
                it += 1
                src = dr["ada_w"][l, :, pc * 1024:(pc + 1) * 1024].rearrange("(k p) n -> p k n", p=128)
                for kk in range(KC):
                    rec.dma("sp", w[:, kk, :], src[:, kk, :], writes=[wb])
                for jj in range(8):
                    j = pc * 8 + jj
                    ps, psb = next_ps(g)
                    for kk in range(KC):
                        rec.op("pe", lambda e, ps=ps, w=w, kk=kk, jj=jj: e.matmul(
                            ps[:, 0:NB], lhsT=w[:, kk, jj * 128:(jj + 1) * 128], rhs=condT[:, kk, :],
                            start=(kk == 0), stop=(kk == KC - 1)), reads=[wb, condT_b], writes=[psb])
                    rec.op("dve", lambda e, ps=ps, l=l, j=j: e.tensor_scalar(
                        out=g.modT[:, l, j, :], in0=ps[:, 0:NB], scalar1=adab[:, l, j:j + 1], scalar2=None,
                        op0=ALU.add), reads=[psb, adab_b], writes=[g.modT_b])
        for l in range(2):
            for sub in range(2):
                for kk in range(KC):
                    j = (1 + 3 * sub) * KC + kk
                    rec.op("dve", lambda e, l=l, sub=sub, kk=kk, j=j: e.tensor_scalar(
                        out=g.Amod[:, l, sub, kk, :], in0=g.modT[:, l, j, :], scalar1=1.0,
                        scalar2=gains[:, sub, l, kk:kk + 1], op0=ALU.add, op1=ALU.mult),
                        reads=[g.modT_b, gains_b], writes=[g.Amod_b])
        rec.emit()


def load_cast(rec, st_rot, srcs_dsts, engs=("dve", "pool")):
    for i, (src, dst, db) in enumerate(srcs_dsts):
        st, stb = st_rot.next()
        n = src.shape[-1]
        rec.dma("sp", st[:, 0:n], src, writes=[stb])
        eng = engs[i % len(engs)]
        rec.op(eng, lambda e, st=st, dst=dst, n=n: e.tensor_copy(out=dst, in_=st[:, 0:n]), reads=[stb], writes=[db])


def phase_post(nc, rec, g, dr, l, NTOK, T, last, N=256):
    wout = dr["w_out_even"][0] if l == 0 else dr["w_out_odd"][0]
    XT = dr["XT"].rearrange("(k p) t -> p k t", p=128)
    YT = (dr.get("YT1", dr["YT"]) if l == 1 else dr["YT"]).rearrange("(k p) t -> p k t", p=128)
    ntile = NTOK // N
    with ExitStack() as es:
        sb = lambda n, s, d=F32: es.enter_context(nc.sbuf_tensor(uniq(n), s, d))
        Wo = sb("Wo", [128, KC, D], BF16)
        Wg = sb("Wg", [128, KC, FH], BF16)
        Wu = sb("Wu", [128, KC, FH], BF16)
        Wd = sb("Wd", [128, JH, D], BF16)
        Wb = Buf("W")
        st_rot = Rot([sb("wst%d" % i, [128, FH // 4]) for i in range(2)])
        xs = [sb("x%d" % i, [128, KC, N]) for i in range(2)]
        xbs = [Buf(), Buf()]
        hT = sb("hT", [128, KC, N], BF16); hb = Buf()
        sq = sb("sq", [128, KC, N], BF16); sqb = Buf()
        actT = sb("actT", [128, JH, N], BF16); actb = Buf()
        tr = Rot([sb("tmp%d" % i, [128, N]) for i in range(3)])
        rstd = sb("rstd", [128, N]); rstdb = Buf()
        if last:
            fng = sb("fng", [128, KC]); fngb = Buf()
            rec.dma("sp", fng[:], dr["final_normT"][:, :], writes=[fngb])
            fr = Rot([sb("fin%d" % i, [128, D]) for i in range(1)])
        jobs = []
        for kk in range(KC):
            for hh in range(2):
                cs = slice(hh * 512, (hh + 1) * 512)
                jobs.append((wout[kk * 128:(kk + 1) * 128, cs], Wo[:, kk, cs], Wb))
        for kk in range(KC):
            for hh in range(4):
                cs = slice(hh * (FH // 4), (hh + 1) * (FH // 4))
                jobs.append((dr["ffn_w_gate"][l, kk * 128:(kk + 1) * 128, cs], Wg[:, kk, cs], Wb))
                jobs.append((dr["ffn_w_up"][l, kk * 128:(kk + 1) * 128, cs], Wu[:, kk, cs], Wb))
        for j in range(JH):
            for hh in range(2):
                cs = slice(hh * 512, (hh + 1) * 512)
                jobs.append((dr["ffn_w_down"][l, j * 128:(j + 1) * 128, cs], Wd[:, j, cs], Wb))
        load_cast(rec, st_rot, jobs)

        def stats(x, xb):
            rec.op("act", lambda e: e.activation(out=sq[:], in_=x[:], func=AF.Square), reads=[xb], writes=[sqb])
            ps, psb = next_ps(g)
            for kk in range(KC):
                MM(rec, ps[:, 0:N], g.ones_bf[:], sq[:, kk, :], kk == 0, kk == KC - 1, [sqb, g.const_b], [psb])
            t1, t1b = tr.next()
            ACTF(rec, t1[:], ps[:, 0:N], AF.Sqrt, [psb, g.const_b], [t1b], scale=1.0 / D, bias=g.epsb[:])
            rec.op("dve", lambda e: e.reciprocal(out=rstd[:], in_=t1[:]), reads=[t1b], writes=[rstdb])

        def load_x(i):
            rec.dma("sp", xs[i % 2][:], XT[:, :, i * N:(i + 1) * N], writes=[xbs[i % 2]])

        def load_y(i):
            rec.dma("sp", sq[:], YT[:, :, i * N:(i + 1) * N], writes=[sqb])

        def outproj(i):
            x, xb = xs[i % 2], xbs[i % 2]
            b = (i * N) // T
            for m in range(KC):
                ps, psb = next_ps(g)
                for kk in range(KC):
                    MM(rec, ps[:, 0:N], Wo[:, kk, m * 128:(m + 1) * 128], sq[:, kk, :], kk == 0, kk == KC - 1, [Wb, sqb], [psb])
                STT(rec, x[:, m, :], ps[:, 0:N], g.modT[:, l, 16 + m, b:b + 1], x[:, m, :], ALU.mult, ALU.add, [psb, g.modT_b, xb], [xb])

        def rmsmod(i):
            x, xb = xs[i % 2], xbs[i % 2]
            b = (i * N) // T
            stats(x, xb)
            for m in range(KC):
                t1, t1b = tr.next()
                TT(rec, "dve", t1[:], x[:, m, :], rstd[:], ALU.mult, [xb, rstdb], [t1b])
                ACTF(rec, hT[:, m, :], t1[:], AF.Identity, [t1b, g.Amod_b, g.modT_b], [hb],
                     scale=g.Amod[:, l, 1, m, b:b + 1], bias=g.modT[:, l, 24 + m, b:b + 1])

        def gateup(i):
            for j in range(JH):
                pg, pgb = next_ps(g)
                pu, pub = next_ps(g)
                for kk in range(KC):
                    MM(rec, pg[:, 0:N], Wg[:, kk, j * 128:(j + 1) * 128], hT[:, kk, :], kk == 0, kk == KC - 1, [Wb, hb], [pgb])
                for kk in range(KC):
                    MM(rec, pu[:, 0:N], Wu[:, kk, j * 128:(j + 1) * 128], hT[:, kk, :], kk == 0, kk == KC - 1, [Wb, hb], [pub])
                t1, t1b = tr.next()
                ACTF(rec, t1[:], pg[:, 0:N], AF.Silu, [pgb], [t1b])
                TT(rec, "dve", actT[:, j, :], t1[:], pu[:, 0:N], ALU.mult, [t1b, pub], [actb])

        def down(i):
            x, xb = xs[i % 2], xbs[i % 2]
            b = (i * N) // T
            for m in range(KC):
                ps, psb = next_ps(g)
                for j in range(JH):
                    MM(rec, ps[:, 0:N], Wd[:, j, m * 128:(m + 1) * 128], actT[:, j, :], j == 0, j == JH - 1, [Wb, actb], [psb])
                STT(rec, x[:, m, :], ps[:, 0:N], g.modT[:, l, 40 + m, b:b + 1], x[:, m, :], ALU.mult, ALU.add, [psb, g.modT_b, xb], [xb])

        def store(i):
            x, xb = xs[i % 2], xbs[i % 2]
            if not last:
                rec.dma("sp", XT[:, :, i * N:(i + 1) * N], x[:], reads=[xb])
                return
            stats(x, xb)
            for m in range(KC):
                STT(rec, x[:, m, :], x[:, m, :], fng[:, m:m + 1], rstd[:], ALU.mult, ALU.mult, [xb, rstdb, fngb], [xb])
            for s in range(N // 128):
                f, fb = fr.next()
                for m in range(KC):
                    if m % 4 == 0:
                        ps, psb = next_ps(g)
                    rec.op("pe", lambda e, ps=ps, m=m, s=s, x=x: e.transpose(
                        ps[:, (m % 4) * 128:(m % 4 + 1) * 128], x[:, m, s * 128:(s + 1) * 128], g.ident[:]),
                        reads=[xb, g.const_b], writes=[psb])
                    if m % 4 == 3:
                        evac(rec, m // 4, f[:, (m - 3) * 128:(m + 1) * 128], ps[:, :], [psb], [fb])
                t0 = i * N + s * 128
                rec.dma("sp", dr["out"][t0:t0 + 128, :], f[:], reads=[fb])

        load_x(0)
        load_y(0)
        outproj(0)
        rmsmod(0)
        for i in range(ntile):
            if i + 1 < ntile:
                load_x(i + 1)
                if not last:
                    load_y(i + 1)
            gateup(i)
            if i + 1 < ntile:
                if last:
                    load_y(i + 1)
                outproj(i + 1)
                rmsmod(i + 1)
            down(i)
            store(i)
        rec.emit()


TWO_PI = 2.0 * np.pi
MLA_SCALE = 96.0 ** -0.5


def evac(rec, i, out, in_, reads, writes):
    if i % 2 == 0:
        rec.op("act", lambda e: e.copy(out=out, in_=in_), reads=reads, writes=writes)
    else:
        rec.op("dve", lambda e: e.tensor_copy(out=out, in_=in_), reads=reads, writes=writes)


def rms_stats(rec, g, src, srcb, nch, N, sq, sqb, t1, t1b, rstd, rstdb, dim):
    rec.op("act", lambda e: e.activation(out=sq[:, 0:nch, :], in_=src, func=AF.Square), reads=[srcb], writes=[sqb])
    ps, psb = next_ps(g)
    for kk in range(nch):
        rec.op("pe", lambda e, kk=kk: e.matmul(ps[:, 0:N], lhsT=g.ones_bf[:], rhs=sq[:, kk, :],
                                                start=(kk == 0), stop=(kk == nch - 1)),
               reads=[sqb, g.const_b], writes=[psb])
    rec.op("act", lambda e: e.activation(out=t1[:], in_=ps[:, 0:N], func=AF.Sqrt, scale=1.0 / dim, bias=g.epsb[:]),
           reads=[psb, g.const_b], writes=[t1b])
    rec.op("dve", lambda e: e.reciprocal(out=rstd[:], in_=t1[:]), reads=[t1b], writes=[rstdb])


def phase_pre0(nc, rec, g, dr, NTOK, T, N=512):
    l = 0
    XT = dr["XT"].rearrange("(k p) t -> p k t", p=128)
    RP = dr["RP"].rearrange("(k p) t -> p k t", p=128)
    ntile = NTOK // N
    NS = N // 128
    with ExitStack() as es:
        sb = lambda n, s, d=F32: es.enter_context(nc.sbuf_tensor(uniq(n), s, d))
        Win = sb("Win", [128, KC, 2464], BF16)
        Wsw = sb("Wsw", [128, KC, 32], BF16)
        Wuq = sb("Wuq", [128, 3, 768], BF16)
        Wuqs = sb("Wuqs", [128, 3, 768], BF16)
        Wukv = sb("Wukv", [128, 2, 1024], BF16)
        Wb = Buf("W")
        st_rot = Rot([sb("wst%d" % i, [128, 1232]) for i in range(2)])
        jobs = []
        for kk in range(KC):
            for hh in range(2):
                cs = slice(hh * 1232, (hh + 1) * 1232)
                jobs.append((dr["w_in_even"][0, kk * 128:(kk + 1) * 128, cs], Win[:, kk, cs], Wb))
            jobs.append((dr["w_in_sw"][kk * 128:(kk + 1) * 128, :], Wsw[:, kk, :], Wb))
        for kk in range(3):
            jobs.append((dr["mla_w_uq"][0, kk * 128:(kk + 1) * 128, :], Wuq[:, kk, :], Wb))
            jobs.append((dr["w_uq_sw"][kk * 128:(kk + 1) * 128, :], Wuqs[:, kk, :], Wb))
        for kk in range(2):
            jobs.append((dr["w_ukv_r"][kk * 128:(kk + 1) * 128, :], Wukv[:, kk, :], Wb))
        load_cast(rec, st_rot, jobs)
        ctab = sb("ctab", [128, 8]); ctabb = Buf()
        rec.dma("sp", ctab[:], dr["ctab"][:, :], writes=[ctabb])
        negpi = sb("negpi", [128, 1]);
        rec.op("pool", lambda e: e.memset(negpi[:], -np.pi), writes=[ctabb])
        Cq = sb("Cq", [96, N]); Sq = sb("Sq", [96, N]); trb = Buf()
        rec.op("pool", lambda e: e.memset(Cq[0:64, :], MLA_SCALE), writes=[trb])
        rec.op("pool", lambda e: e.memset(Sq[0:64, :], 0.0), writes=[trb])
        Ck = sb("Ck", [32, N]); Sk = sb("Sk", [32, N])
        posi = sb("posi", [96, N], I32); posib = Buf()
        posf = sb("posf", [96, N]); posfb = Buf()
        ua = sb("ua", [96, N]); uab = Buf()
        ui = sb("ui", [96, N], I32); uib = Buf()
        uf = sb("uf", [96, N]); ufb = Buf()
        um = sb("um", [96, N]); umb = Buf()
        trig = [sb("sinv", [96, N]), sb("cosv", [96, N])]; trigb = [Buf(), Buf()]
        xin = sb("xin", [128, NS, D]); xinb = Buf()
        xT = sb("xT", [128, KC, N]); xTb = Buf()
        sq = sb("sq", [128, KC, N], BF16); sqb = Buf()
        hT = sb("hT", [128, KC, N], BF16); hb = Buf()
        tr = Rot([sb("tmp%d" % i, [128, N]) for i in range(3)])
        rstd = sb("rstd", [128, N]); rstdb = Buf()
        cq = sb("cq", [128, 3, N]); cqb = Buf()
        cqn = sb("cqn", [128, 3, N], BF16); cqnb = Buf()
        ckv = sb("ckv", [128, 2, N]); ckvb = Buf()
        ckvn = sb("ckvn", [128, 2, N], BF16); ckvnb = Buf()
        qo = Rot([sb("qo%d" % i, [96, N], BF16) for i in range(2)])
        ko = Rot([sb("ko%d" % i, [128, N], BF16) for i in range(2)])
        kro = sb("kro", [32, N], BF16); krob = Buf()
        vo = Rot([sb("vo%d" % i, [128, 512], BF16) for i in range(2)])
        rpo = Rot([sb("rpo%d" % i, [128, N]) for i in range(3)])
        ei = 0
        for ti in range(ntile):
            b = (ti * N) // T
            t0 = ti * N
            rec.dma("sp", xin[:], dr["x"][t0:t0 + N, :].rearrange("(s p) d -> p s d", p=128), writes=[xinb])
            for m in range(KC):
                ps, psb = next_ps(g)
                for s in range(NS):
                    rec.op("pe", lambda e, ps=ps, m=m, s=s: e.transpose(
                        ps[:, s * 128:(s + 1) * 128], xin[:, s, m * 128:(m + 1) * 128], g.ident[:]),
                        reads=[xinb, g.const_b], writes=[psb])
                evac(rec, m, xT[:, m, :], ps[:, 0:N], [psb], [xTb])
            rec.dma("sp", XT[:, :, t0:t0 + N], xT[:], reads=[xTb])
            rec.dma("sp", posi[:], dr["pos"][b:b + 1, (t0 % T):(t0 % T) + N].partition_broadcast(96), writes=[posib])
            rec.op("dve", lambda e: e.tensor_copy(out=posf[:], in_=posi[:]), reads=[posib], writes=[posfb])
            for w in range(2):
                off = 0.5 if w == 0 else 0.75
                rec.op("dve", lambda e, off=off: e.tensor_scalar(out=ua[:], in0=posf[:], scalar1=ctab[0:96, 0:1], scalar2=off,
                                                                 op0=ALU.mult, op1=ALU.add), reads=[posfb, ctabb], writes=[uab])
                rec.op("dve", lambda e: e.tensor_copy(out=ui[:], in_=ua[:]), reads=[uab], writes=[uib])
                rec.op("dve", lambda e: e.tensor_copy(out=uf[:], in_=ui[:]), reads=[uib], writes=[ufb])
                rec.op("dve", lambda e: e.tensor_tensor(out=ua[:], in0=ua[:], in1=uf[:], op=ALU.subtract), reads=[uab, ufb], writes=[uab])
                rec.op("dve", lambda e: e.tensor_scalar(out=um[:], in0=ua[:], scalar1=0.0, scalar2=None, op0=ALU.is_lt),
                       reads=[uab], writes=[umb])
                rec.op("dve", lambda e: e.tensor_tensor(out=ua[:], in0=ua[:], in1=um[:], op=ALU.add), reads=[uab, umb], writes=[uab])
                rec.op("act", lambda e, w=w: e.activation(out=trig[w][:], in_=ua[:], func=AF.Sin, scale=TWO_PI, bias=negpi[0:96, :]),
                       reads=[uab, ctabb], writes=[trigb[w]])
            rec.op("dve", lambda e: e.tensor_copy(out=Ck[:], in_=trig[1][0:32, :]), reads=[trigb[1]], writes=[trb])
            rec.op("dve", lambda e: e.tensor_scalar(out=Sk[:], in0=trig[0][0:32, :], scalar1=ctab[0:32, 1:2], scalar2=None, op0=ALU.mult),
                   reads=[trigb[0], ctabb], writes=[trb])
            rec.op("dve", lambda e: e.tensor_scalar(out=Cq[64:96, :], in0=trig[1][64:96, :], scalar1=MLA_SCALE, scalar2=None, op0=ALU.mult),
                   reads=[trigb[1]], writes=[trb])
            rec.op("dve", lambda e: e.tensor_scalar(out=Sq[64:96, :], in0=trig[0][64:96, :], scalar1=ctab[64:96, 2:3], scalar2=None, op0=ALU.mult),
                   reads=[trigb[0], ctabb], writes=[trb])
            t1, t1b = tr.next()
            rms_stats(rec, g, xT[:], xTb, KC, N, sq, sqb, t1, t1b, rstd, rstdb, D)
            for m in range(KC):
                t1, t1b = tr.next()
                rec.op("dve", lambda e, t1=t1, m=m: e.tensor_tensor(out=t1[:], in0=xT[:, m, :], in1=rstd[:], op=ALU.mult),
                       reads=[xTb, rstdb], writes=[t1b])
                rec.op("act", lambda e, t1=t1, m=m, b=b: e.activation(
                    out=hT[:, m, :], in_=t1[:], func=AF.Identity, scale=g.Amod[:, l, 0, m, b:b + 1],
                    bias=g.modT[:, l, 0 + m, b:b + 1]), reads=[t1b, g.Amod_b, g.modT_b], writes=[hb])

            def proj(ps, psb, c0, M, W=Win):
                for kk in range(KC):
                    rec.op("pe", lambda e, kk=kk: e.matmul(ps[0:M, 0:N], lhsT=W[:, kk, c0:c0 + M], rhs=hT[:, kk, :],
                                                            start=(kk == 0), stop=(kk == KC - 1)), reads=[Wb, hb], writes=[psb])
            for c in range(3):
                ps, psb = next_ps(g)
                proj(ps, psb, c * 128, 128)
                evac(rec, c, cq[:, c, :], ps[:, 0:N], [psb], [cqb])
            for c in range(2):
                ps, psb = next_ps(g)
                proj(ps, psb, 384 + c * 128, 128)
                evac(rec, c + 1, ckv[:, c, :], ps[:, 0:N], [psb], [ckvb])
            for (src, srcb, nch, dst, dstb, gcol, dim) in ((cq, cqb, 3, cqn, cqnb, 3, 384), (ckv, ckvb, 2, ckvn, ckvnb, 6, 256)):
                t1, t1b = tr.next()
                rms_stats(rec, g, src[:], srcb, nch, N, sq, sqb, t1, t1b, rstd, rstdb, dim)
                for c in range(nch):
                    rec.op("dve", lambda e, c=c, src=src, dst=dst, gcol=gcol: e.scalar_tensor_tensor(
                        out=dst[:, c, :], in0=src[:, c, :], scalar=ctab[:, gcol + c:gcol + c + 1], in1=rstd[:],
                        op0=ALU.mult, op1=ALU.mult), reads=[srcb, rstdb, ctabb], writes=[dstb])
            ps, psb = next_ps(g)
            proj(ps, psb, 640, 32)
            ps2, psb2 = next_ps(g)
            proj(ps2, psb2, 0, 32, W=Wsw)
            t1, t1b = tr.next()
            t2, t2b = tr.next()
            rec.op("dve", lambda e, t1=t1, ps2=ps2: e.tensor_tensor(out=t1[0:32, :], in0=ps2[0:32, 0:N], in1=Sk[:], op=ALU.mult),
                   reads=[psb2, trb], writes=[t1b])
            rec.op("dve", lambda e, t2=t2, ps=ps: e.tensor_tensor(out=t2[0:32, :], in0=ps[0:32, 0:N], in1=Ck[:], op=ALU.mult),
                   reads=[psb, trb], writes=[t2b])
            rec.op("dve", lambda e, t1=t1, t2=t2: e.tensor_tensor(out=kro[:], in0=t1[0:32, :], in1=t2[0:32, :], op=ALU.add),
                   reads=[t1b, t2b], writes=[krob])
            for h in range(8):
                rec.dma("pool" if h % 2 else "sp", dr["KT"][h, 64:96, t0:t0 + N], kro[:], reads=[krob])
            for h in range(8):
                psA, psAb = next_ps(g)
                psB, psBb = next_ps(g)
                for (ps, psb, W) in ((psA, psAb, Wuq), (psB, psBb, Wuqs)):
                    for kk in range(3):
                        rec.op("pe", lambda e, ps=ps, W=W, kk=kk, h=h: e.matmul(
                            ps[0:96, 0:N], lhsT=W[:, kk, h * 96:(h + 1) * 96], rhs=cqn[:, kk, :],
                            start=(kk == 0), stop=(kk == 2)), reads=[Wb, cqnb], writes=[psb])
                t1, t1b = tr.next()
                t2, t2b = tr.next()
                q, qb = qo.next()
                rec.op("dve", lambda e, t1=t1, psB=psB: e.tensor_tensor(out=t1[0:96, :], in0=psB[0:96, 0:N], in1=Sq[:], op=ALU.mult),
                       reads=[psBb, trb], writes=[t1b])
                rec.op("dve", lambda e, t2=t2, psA=psA: e.tensor_tensor(out=t2[0:96, :], in0=psA[0:96, 0:N], in1=Cq[:], op=ALU.mult),
                       reads=[psAb, trb], writes=[t2b])
                rec.op("pool", lambda e, t1=t1, t2=t2, q=q: e.tensor_tensor(out=q[:], in0=t1[0:96, :], in1=t2[0:96, :], op=ALU.add),
                       reads=[t1b, t2b], writes=[qb])
                rec.dma("sp", dr["QT"][h, :, t0:t0 + N], q[:], reads=[qb])
            for hp in range(4):
                ps, psb = next_ps(g)
                for kk in range(2):
                    rec.op("pe", lambda e, ps=ps, kk=kk, hp=hp: e.matmul(
                        ps[:, 0:N], lhsT=Wukv[:, kk, hp * 128:(hp + 1) * 128], rhs=ckvn[:, kk, :],
                        start=(kk == 0), stop=(kk == 1)), reads=[Wb, ckvnb], writes=[psb])
                k, kb = ko.next()
                evac(rec, hp, k[:], ps[:, 0:N], [psb], [kb])
                rec.dma("sp", dr["KT"][2 * hp, 0:64, t0:t0 + N], k[0:64, :], reads=[kb])
                rec.dma("pool", dr["KT"][2 * hp + 1, 0:64, t0:t0 + N], k[64:128, :], reads=[kb])
            for s in range(NS):
                ps, psb = next_ps(g)
                for kk in range(2):
                    rec.op("pe", lambda e, ps=ps, kk=kk, s=s: e.matmul(
                        ps[:, :], lhsT=ckvn[:, kk, s * 128:(s + 1) * 128], rhs=Wukv[:, kk, 512:1024],
                        start=(kk == 0), stop=(kk == 1)), reads=[Wb, ckvnb], writes=[psb])
                v, vb = vo.next()
                evac(rec, s, v[:], ps[:, :], [psb], [vb])
                rec.dma("sp", dr["V"][t0 + s * 128:t0 + (s + 1) * 128, :], v[:], reads=[vb])
            for c in range(14):
                ps, psb = next_ps(g)
                proj(ps, psb, 672 + c * 128, 128)
                o, ob = rpo.next()
                evac(rec, c, o[:], ps[:, 0:N], [psb], [ob])
                rec.dma("sp" if c % 2 else "pool", RP[:, c, t0:t0 + N], o[:], reads=[ob])
        rec.emit()


def phase_mla(nc, rec, g, dr, NB, T):
    NKB = T // 128
    YT = dr["YT"].rearrange("(k p) t -> p k t", p=128)
    offs = [0]
    for kb in range(NKB):
        offs.append(offs[-1] + (T - kb * 128))
    with ExitStack() as es:
        sb = lambda n, s, d=F32: es.enter_context(nc.sbuf_tensor(uniq(n), s, d))
        qr = Rot([sb("q%d" % i, [96, T], BF16) for i in range(2)])
        kr = Rot([sb("k%d" % i, [96, T], BF16) for i in range(2)])
        va = [sb("va%d" % i, [128, NKB, 65], BF16) for i in range(2)]
        vab = [Buf(), Buf()]
        for i in range(2):
            rec.op("pool", lambda e, i=i: e.memset(va[i][:], 1.0), writes=[vab[i]])
        ptr = Rot([sb("pt%d" % i, [128, offs[-1]], BF16) for i in range(2)])
        mask = sb("mask", [128, 128], BF16); maskb = Buf()
        mstage = sb("mstage", [128, 128])
        rec.dma("sp", mstage[:], dr["cmask"][:, :], writes=[maskb])
        rec.op("dve", lambda e: e.tensor_copy(out=mask[:], in_=mstage[:]), reads=[maskb], writes=[maskb])
        ytm = sb("ytm", [128, NKB, 512]); ytmb = Buf()
        rc = Rot([sb("rc%d" % i, [128, 1]) for i in range(4)])
        ytr = Rot([sb("yts%d" % i, [128, 4, 128], BF16) for i in range(2)])
        it = 0
        for b in range(NB):
            for h in range(8):
                q, qb_ = qr.next()
                k, kb_ = kr.next()
                v, vb_ = va[it % 2], vab[it % 2]
                it += 1
                pt, ptb = ptr.next()
                rec.dma("sp", q[:], dr["QT"][h, :, b * T:(b + 1) * T], writes=[qb_])
                rec.dma("pool", k[:], dr["KT"][h, :, b * T:(b + 1) * T], writes=[kb_])
                rec.dma("sp", v[:, :, 0:64], dr["V"][b * T:(b + 1) * T, h * 64:(h + 1) * 64].rearrange("(k p) d -> p k d", p=128),
                        writes=[vb_])
                for kb in range(NKB):
                    q0 = kb * 128
                    c = q0
                    while c < T:
                        n = min(512, T - c)
                        ps, psb = next_ps(g)
                        rec.op("pe", lambda e, ps=ps, k=k, q=q, q0=q0, c=c, n=n: e.matmul(
                            ps[:, 0:n], lhsT=k[:, q0:q0 + 128], rhs=q[:, c:c + n], start=True, stop=True),
                            reads=[kb_, qb_], writes=[psb])
                        o0 = offs[kb] + (c - q0)
                        rec.op("act", lambda e, ps=ps, pt=pt, o0=o0, n=n: e.activation(out=pt[:, o0:o0 + n], in_=ps[:, 0:n], func=AF.Exp),
                               reads=[psb], writes=[ptb])
                        c += n
                    o0 = offs[kb]
                    rec.op("pool", lambda e, pt=pt, o0=o0: e.tensor_tensor(out=pt[:, o0:o0 + 128], in0=pt[:, o0:o0 + 128], in1=mask[:], op=ALU.mult),
                           reads=[ptb, maskb], writes=[ptb])
                for qb in range(NKB):
                    ps, psb = next_ps(g)
                    for kb in range(qb + 1):
                        o0 = offs[kb] + (qb - kb) * 128
                        rec.op("pe", lambda e, ps=ps, pt=pt, v=v, o0=o0, kb=kb, qb=qb: e.matmul(
                            ps[:, 0:65], lhsT=pt[:, o0:o0 + 128], rhs=v[:, kb, :], start=(kb == 0), stop=(kb == qb)),
                            reads=[ptb, vb_], writes=[psb])
                    r, rb = rc.next()
                    rec.op("dve", lambda e, r=r, ps=ps: e.reciprocal(out=r[:], in_=ps[:, 64:65]), reads=[psb], writes=[rb])
                    rec.op("dve", lambda e, r=r, ps=ps, qb=qb, h=h: e.tensor_scalar(
                        out=ytm[:, qb, h * 64:(h + 1) * 64], in0=ps[:, 0:64], scalar1=r[:, 0:1], scalar2=None, op0=ALU.mult),
                        reads=[psb, rb], writes=[ytmb])
            for qb in range(NKB):
                ps, psb = next_ps(g)
                for c in range(4):
                    rec.op("pe", lambda e, ps=ps, qb=qb, c=c: e.transpose(
                        ps[:, c * 128:(c + 1) * 128], ytm[:, qb, c * 128:(c + 1) * 128], g.ident[:]),
                        reads=[ytmb, g.const_b], writes=[psb])
                yt, ytb = ytr.next()
                evac(rec, qb, yt[:].rearrange("p c t -> p (c t)"), ps[:, :], [psb], [ytb])
                t0 = b * T + qb * 128
                rec.dma("sp", YT[:, 0:4, t0:t0 + 128], yt[:], reads=[ytb])
        rec.emit()


def host_layout(inp):
    f32 = np.float32
    d = {}
    NB = inp["c"].shape[0]
    d["ident"] = np.eye(128, dtype=f32)
    d["cmask"] = np.triu(np.ones((128, 128), dtype=f32))
    d["x"] = np.ascontiguousarray(inp["x"].reshape(-1, 1024))
    d["pos"] = np.ascontiguousarray(inp["positions"].astype(np.int32))
    d["cT"] = np.ascontiguousarray(inp["c"].T.reshape(8, 128, NB).transpose(1, 0, 2))
    d["ada_bT"] = np.ascontiguousarray(inp["ada_b"].reshape(2, 48, 128).transpose(2, 0, 1))
    d["norm_mixT"] = np.ascontiguousarray(inp["norm_mix"].reshape(2, 8, 128).transpose(2, 0, 1))
    d["norm_ffnT"] = np.ascontiguousarray(inp["norm_ffn"].reshape(2, 8, 128).transpose(2, 0, 1))
    d["final_normT"] = np.ascontiguousarray(inp["final_norm"].reshape(8, 128).T)
    for k in ["ada_w", "w_out_even", "w_out_odd", "ffn_w_gate", "ffn_w_up", "ffn_w_down", "w_in_even", "mla_w_uq", "w_in_odd"]:
        d[k] = inp[k]
    wi = inp["w_in_even"][0]
    d["w_in_sw"] = np.ascontiguousarray(np.concatenate([wi[:, 656:672], wi[:, 640:656]], axis=1))
    wq = inp["mla_w_uq"][0].reshape(384, 8, 96)
    d["w_uq_sw"] = np.ascontiguousarray(np.concatenate([wq[:, :, 0:64], wq[:, :, 80:96], wq[:, :, 64:80]], axis=2).reshape(384, 768))
    wkv = inp["mla_w_ukv"][0].reshape(256, 8, 128)
    d["w_ukv_r"] = np.ascontiguousarray(np.concatenate([wkv[:, :, 0:64].reshape(256, 512), wkv[:, :, 64:128].reshape(256, 512)], axis=1))
    ctab = np.zeros((128, 8), dtype=f32)
    invf = (1.0 / (10000.0 ** (np.arange(0, 32, 2, dtype=np.float32) / 32))).astype(np.float32)
    for base in (0, 16, 64, 80):
        ctab[base:base + 16, 0] = invf / np.float32(2 * np.pi)
    ctab[0:16, 1] = -1.0
    ctab[16:32, 1] = 1.0
    ctab[64:80, 2] = -MLA_SCALE
    ctab[80:96, 2] = MLA_SCALE
    ctab[:, 3:6] = inp["mla_q_norm"][0].reshape(3, 128).T
    ctab[:, 6:8] = inp["mla_kv_norm"][0].reshape(2, 128).T
    d["ctab"] = ctab
    su = np.triu(np.ones((128, 128), dtype=f32), 1)
    iu = np.triu(np.ones((128, 128), dtype=f32), 0)
    sl = np.tril(np.ones((128, 128), dtype=f32), -1)
    d["rmasks"] = np.ascontiguousarray(np.concatenate([su, iu, sl], axis=1))
    for k in ("rwkv_ln_w", "rwkv_ln_b"):
        d[k] = inp[k]
    d["hg_lbT"] = np.ascontiguousarray(inp["hg_lb_logits"].reshape(2, 8, 128).transpose(2, 0, 1))
    d["hg_onT"] = np.ascontiguousarray(inp["hg_out_norm"][0].reshape(128, 1))
    blk = np.arange(128) // 32
    d["hmask"] = np.ascontiguousarray(((blk[:, None] == blk[None, :]) & (np.arange(128)[:, None] <= np.arange(128)[None, :])).astype(f32))
    d["rw_mu"] = np.ascontiguousarray(inp["rwkv_mu"][0].reshape(14, 128).T)
    d["rw_par"] = np.ascontiguousarray(np.concatenate(
        [inp[k][0].reshape(4, 128).T for k in ("rwkv_w0", "rwkv_a0", "rwkv_k_k", "rwkv_k_a", "rwkv_r_k")], axis=1))
    for k in ("rwkv_w2", "rwkv_a2", "rwkv_g2"):
        d[k] = inp[k]
    return d


NEG_EM05 = -float(np.exp(-0.5))


def phase_rwkv_prep(nc, rec, g, dr, NTOK, T, N=256):
    RP = dr["RP"].rearrange("(k p) t -> p k t", p=128)
    outs = {k: dr[k].rearrange("(c p) t -> p c t", p=128) for k in ("AT", "BT", "KTt", "RTt", "VT", "RKT")}
    PC = dr["PC"].rearrange("(c p) n -> p c n", p=128)
    ntile = NTOK // N
    NS = N // 128
    with ExitStack() as es:
        sb = lambda n, s, d=F32: es.enter_context(nc.sbuf_tensor(uniq(n), s, d))
        par = sb("par", [128, 64]); parb = Buf()
        rec.dma("sp", par[:, 0:14], dr["rw_mu"][:, :], writes=[parb])
        rec.dma("sp", par[:, 28:48], dr["rw_par"][:, :], writes=[parb])
        rec.op("dve", lambda e: e.tensor_scalar(out=par[:, 14:28], in0=par[:, 0:14], scalar1=-1.0, scalar2=1.0, op0=ALU.mult, op1=ALU.add),
               reads=[parb], writes=[parb])
        rec.op("dve", lambda e: e.tensor_scalar(out=par[:, 48:52], in0=par[:, 40:44], scalar1=-1.0, scalar2=1.0, op0=ALU.mult, op1=ALU.add),
               reads=[parb], writes=[parb])
        wst = sb("wst", [128, 512]); wstb = Buf()
        w2b = sb("w2b", [128, 512], BF16); g2b = sb("g2b", [128, 512], BF16); Wb = Buf()
        rec.dma("sp", wst[0:64, :], dr["rwkv_w2"][0, :, :], writes=[wstb])
        rec.dma("sp", wst[64:128, :], dr["rwkv_a2"][0, :, :], writes=[wstb])
        rec.op("dve", lambda e: e.tensor_copy(out=w2b[:], in_=wst[:]), reads=[wstb], writes=[Wb])
        rec.dma("sp", wst[:, :], dr["rwkv_g2"][0, :, :], reads=[], writes=[wstb])
        rec.op("dve", lambda e: e.tensor_copy(out=g2b[:], in_=wst[:]), reads=[wstb], writes=[Wb])
        bones = sb("bones", [128, 128], BF16)
        rec.op("pool", lambda e: e.memset(bones[:], 0.0), writes=[Wb])
        rec.op("pool", lambda e: e.memset(bones[0:64, 0:64], 1.0), writes=[Wb])
        rec.op("pool", lambda e: e.memset(bones[64:128, 64:128], 1.0), writes=[Wb])
        rmask = sb("rmask", [128, N])
        rec.op("pool", lambda e: e.memset(rmask[:], 1.0), writes=[Wb])
        rec.op("pool", lambda e: e.memset(rmask[:].rearrange("p (c t) -> p c t", t=128)[:, :, 0:1], 0.0), writes=[Wb])
        p = sb("p", [128, 14, N]); pb = Buf()
        psh = sb("psh", [128, 14, N]); pshb = Buf()
        tmpr = Rot([sb("mt%d" % i, [128, N]) for i in range(3)])
        wab = sb("wab", [128, N], BF16); wabb = Buf()
        sgl = sb("sgl", [128, N], BF16); sglb = Buf()
        F4 = lambda n: (sb(n, [128, 4, N]), Buf())
        ld, ldb = F4("ld"); bb, bbb = F4("bb"); epos, eposb = F4("epos"); eneg, enegb = F4("eneg"); eprev, eprevb = F4("eprev")
        aa, aab = F4("aa"); kk, kkb = F4("kk"); kp, kpb = F4("kp"); rn, rnb = F4("rn")
        sqk = sb("sqk", [128, 4, N], BF16); sqkb = Buf()
        ob = {k: (sb("o_" + k, [128, 4, N], BF16), Buf()) for k in outs}
        pco = sb("pco", [128, 4, NS]); pcob = Buf()
        gto = Rot([sb("gto%d" % i, [128, 512], BF16) for i in range(2)])
        for ti in range(ntile):
            t0 = ti * N
            rec.dma("sp", p[:], RP[:, :, t0:t0 + N], writes=[pb])
            if t0 % T == 0:
                rec.op("pool", lambda e: e.memset(psh[:, :, 0:1], 0.0), writes=[pshb])
                rec.dma("pool", psh[:, :, 1:N], RP[:, :, t0:t0 + N - 1], writes=[pshb])
            else:
                rec.dma("pool", psh[:, :, :], RP[:, :, t0 - 1:t0 + N - 1], writes=[pshb])
            for j in range(14):
                t1, t1b = tmpr.next()
                rec.op("act", lambda e, j=j, t1=t1: e.activation(out=t1[:], in_=p[:, j, :], func=AF.Identity, scale=par[:, 14 + j:15 + j]),
                       reads=[pb, parb], writes=[t1b])
                rec.op("dve", lambda e, j=j, t1=t1: e.scalar_tensor_tensor(out=p[:, j, :], in0=psh[:, j, :], scalar=par[:, j:j + 1], in1=t1[:],
                                                                          op0=ALU.mult, op1=ALU.add), reads=[pshb, t1b, parb, pb], writes=[pb])
            rec.op("act", lambda e: e.activation(out=wab[0:64, :], in_=p[0:64, 12, :], func=AF.Tanh), reads=[pb], writes=[wabb])
            rec.op("dve", lambda e: e.tensor_copy(out=wab[64:128, :], in_=p[64:128, 12, :]), reads=[pb], writes=[wabb])
            for c in range(4):
                ps, psb = next_ps(g)
                rec.op("pe", lambda e, ps=ps, c=c: e.matmul(ps[:, 0:N], lhsT=w2b[0:64, c * 128:(c + 1) * 128], rhs=wab[0:64, :], start=True, stop=True),
                       reads=[Wb, wabb], writes=[psb])
                rec.op("act", lambda e, ps=ps, c=c: e.activation(out=ld[:, c, :], in_=ps[:, 0:N], func=AF.Sigmoid, bias=par[:, 28 + c:29 + c]),
                       reads=[psb, parb], writes=[ldb])
            for c in range(4):
                ps, psb = next_ps(g)
                rec.op("pe", lambda e, ps=ps, c=c: e.matmul(ps[:, 0:N], lhsT=w2b[64:128, c * 128:(c + 1) * 128], rhs=wab[64:128, :], start=True, stop=True),
                       reads=[Wb, wabb], writes=[psb])
                rec.op("act", lambda e, ps=ps, c=c: e.activation(out=aa[:, c, :], in_=ps[:, 0:N], func=AF.Sigmoid, bias=par[:, 32 + c:33 + c]),
                       reads=[psb, parb], writes=[aab])
            rec.op("act", lambda e: e.activation(out=sgl[:], in_=p[:, 13, :], func=AF.Sigmoid), reads=[pb], writes=[sglb])
            rec.op("dve", lambda e: e.tensor_scalar(out=ld[:], in0=ld[:], scalar1=NEG_EM05, scalar2=None, op0=ALU.mult), reads=[ldb], writes=[ldb])
            for c in range(4):
                rec.op("dve", lambda e, c=c: e.tensor_tensor_scan(out=bb[:, c, :], data0=rmask[:], data1=ld[:, c, :], initial=0.0,
                                                                  op0=ALU.mult, op1=ALU.add), reads=[ldb, Wb], writes=[bbb])
            rec.op("pool", lambda e: e.tensor_tensor(out=eprev[:], in0=bb[:], in1=ld[:], op=ALU.subtract), reads=[bbb, ldb], writes=[eprevb])
            rec.op("act", lambda e: e.activation(out=epos[:], in_=bb[:], func=AF.Exp), reads=[bbb], writes=[eposb])
            rec.op("act", lambda e: e.activation(out=eneg[:], in_=bb[:], func=AF.Exp, scale=-1.0), reads=[bbb], writes=[enegb])
            rec.op("act", lambda e: e.activation(out=eprev[:], in_=eprev[:], func=AF.Exp), reads=[eprevb], writes=[eprevb])
            for c in range(4):
                rec.op("act", lambda e, c=c: e.activation(out=sqk[:, c, :], in_=p[:, 4 + c, :], func=AF.Square, scale=par[:, 36 + c:37 + c]),
                       reads=[pb, parb], writes=[sqkb])
            for c in range(4):
                ps, psb = next_ps(g)
                rec.op("pe", lambda e, ps=ps, c=c: e.matmul(ps[:, 0:N], lhsT=bones[:], rhs=sqk[:, c, :], start=True, stop=True),
                       reads=[Wb, sqkb], writes=[psb])
                rec.op("act", lambda e, ps=ps, c=c: e.activation(out=rn[:, c, :], in_=ps[:, 0:N], func=AF.Sqrt), reads=[psb], writes=[rnb])
            rec.op("dve", lambda e: e.tensor_scalar(out=rn[:], in0=rn[:], scalar1=1e-12, scalar2=None, op0=ALU.max), reads=[rnb], writes=[rnb])
            rec.op("dve", lambda e: e.reciprocal(out=rn[:], in_=rn[:]), reads=[rnb], writes=[rnb])
            for c in range(4):
                rec.op("dve", lambda e, c=c: e.scalar_tensor_tensor(out=kk[:, c, :], in0=p[:, 4 + c, :], scalar=par[:, 36 + c:37 + c], in1=rn[:, c, :],
                                                                   op0=ALU.mult, op1=ALU.mult), reads=[pb, parb, rnb], writes=[kkb])
                rec.op("dve", lambda e, c=c: e.tensor_scalar(out=kp[:, c, :], in0=aa[:, c, :], scalar1=par[:, 40 + c:41 + c], scalar2=par[:, 48 + c:49 + c],
                                                              op0=ALU.mult, op1=ALU.add), reads=[aab, parb], writes=[kpb])
            rec.op("dve", lambda e: e.tensor_tensor(out=kp[:], in0=kp[:], in1=p[:, 4:8, :], op=ALU.mult), reads=[kpb, pb], writes=[kpb])
            o, obb = ob["AT"]
            rec.op("dve", lambda e, o=o: e.scalar_tensor_tensor(out=o[:], in0=kk[:], scalar=-1.0, in1=eprev[:], op0=ALU.mult, op1=ALU.mult),
                   reads=[kkb, eprevb], writes=[obb])
            o, obb = ob["BT"]
            rec.op("pool", lambda e: e.tensor_tensor(out=kk[:], in0=kk[:], in1=aa[:], op=ALU.mult), reads=[kkb, aab], writes=[kkb])
            rec.op("dve", lambda e, o=o: e.tensor_tensor(out=o[:], in0=kk[:], in1=eneg[:], op=ALU.mult), reads=[kkb, enegb], writes=[obb])
            o, obb = ob["KTt"]
            rec.op("pool", lambda e, o=o: e.tensor_tensor(out=o[:], in0=kp[:], in1=eneg[:], op=ALU.mult), reads=[kpb, enegb], writes=[obb])
            o, obb = ob["RTt"]
            rec.op("dve", lambda e, o=o: e.tensor_tensor(out=o[:], in0=p[:, 0:4, :], in1=epos[:], op=ALU.mult), reads=[pb, eposb], writes=[obb])
            o, obb = ob["VT"]
            rec.op("act", lambda e, o=o: e.copy(out=o[:], in_=p[:, 8:12, :]), reads=[pb], writes=[obb])
            o, obb = ob["RKT"]
            for c in range(4):
                rec.op("dve", lambda e, o=o, c=c: e.scalar_tensor_tensor(out=o[:, c, :], in0=p[:, c, :], scalar=par[:, 44 + c:45 + c], in1=kp[:, c, :],
                                                                        op0=ALU.mult, op1=ALU.mult), reads=[pb, parb, kpb], writes=[obb])
            rec.op("pool", lambda e: e.tensor_copy(out=pco[:], in_=epos[:].rearrange("p c (s t) -> p c s t", t=128)[:, :, :, 127]),
                   reads=[eposb], writes=[pcob])
            rec.dma("sp", PC[:, :, t0 // 128:t0 // 128 + NS], pco[:], reads=[pcob])
            for i, k in enumerate(outs):
                o, obb = ob[k]
                rec.dma("sp" if i % 2 == 0 else "pool", outs[k][:, :, t0:t0 + N], o[:], reads=[obb])
            for s in range(NS):
                ps, psb = next_ps(g)
                rec.op("pe", lambda e, ps=ps, s=s: e.matmul(ps[:, :], lhsT=sgl[:, s * 128:(s + 1) * 128], rhs=g2b[:], start=True, stop=True),
                       reads=[sglb, Wb], writes=[psb])
                go, gob = gto.next()
                evac(rec, s, go[:], ps[:, :], [psb], [gob])
                rec.dma("sp", dr["G_tm"][t0 + s * 128:t0 + (s + 1) * 128, :], go[:], reads=[gob])
        rec.emit()


def MM(rec, out, lhsT, rhs, start, stop, reads, writes):
    rec.op("pe", lambda e: e.matmul(out, lhsT=lhsT, rhs=rhs, start=start, stop=stop), reads=reads, writes=writes)


def CP(rec, eng, out, in_, reads, writes):
    if eng == "act":
        rec.op("act", lambda e: e.copy(out=out, in_=in_), reads=reads, writes=writes)
    else:
        rec.op(eng, lambda e: e.tensor_copy(out=out, in_=in_), reads=reads, writes=writes)


def TT(rec, eng, out, in0, in1, op, reads, writes):
    rec.op(eng, lambda e: e.tensor_tensor(out=out, in0=in0, in1=in1, op=op), reads=reads, writes=writes)


def TS(rec, eng, out, in0, s1, op0, reads, writes, s2=None, op1=None):
    if op1 is None:
        rec.op(eng, lambda e: e.tensor_scalar(out=out, in0=in0, scalar1=s1, scalar2=None, op0=op0), reads=reads, writes=writes)
    else:
        rec.op(eng, lambda e: e.tensor_scalar(out=out, in0=in0, scalar1=s1, scalar2=s2, op0=op0, op1=op1), reads=reads, writes=writes)


def STT(rec, out, in0, scalar, in1, op0, op1, reads, writes):
    rec.op("dve", lambda e: e.scalar_tensor_tensor(out=out, in0=in0, scalar=scalar, in1=in1, op0=op0, op1=op1), reads=reads, writes=writes)


def ACTF(rec, out, in_, func, reads, writes, scale=1.0, bias=None):
    if bias is None:
        rec.op("act", lambda e: e.activation(out=out, in_=in_, func=func, scale=scale), reads=reads, writes=writes)
    else:
        rec.op("act", lambda e: e.activation(out=out, in_=in_, func=func, scale=scale, bias=bias), reads=reads, writes=writes)


def phase_rwkv_main(nc, rec, g, dr, NB, T):
    import os
    NCH = int(os.environ.get("RW_NCH", T // 128))
    YT = dr["YT"].rearrange("(k p) t -> p k t", p=128)
    src = {k: dr[k].rearrange("(h k) t -> k h t", k=64) for k in ("AT", "BT", "KTt", "RTt", "VT", "RKT")}
    PCd = dr["PC"].rearrange("(h k) n -> k h n", k=64)
    NH = 8
    NHL = int(os.environ.get("RW_NH", "8"))
    LIM = int(os.environ.get("RW_LIM", "9"))
    with ExitStack() as es:
        sb = lambda n, s, d=F32: es.enter_context(nc.sbuf_tensor(uniq(n), s, d))
        cb = g.const_b
        masks = sb("masks", [128, 512]); mb = Buf()
        mlow = sb("mlow", [128, 128])
        rec.dma("sp", masks[:, 0:256], dr["rmasks"][:, 0:256], writes=[mb])
        rec.dma("sp", masks[:, 256:512], dr["rmasks"][:, 0:256], writes=[mb])
        rec.dma("sp", mlow[:], dr["rmasks"][:, 256:384], writes=[mb])
        lnw = sb("lnw", [128, 512]); lnb = sb("lnb", [128, 512]); lnbuf = Buf()
        rec.dma("sp", lnw[:], dr["rwkv_ln_w"][0:1, :].partition_broadcast(128), writes=[lnbuf])
        rec.dma("sp", lnb[:], dr["rwkv_ln_b"][0:1, :].partition_broadcast(128), writes=[lnbuf])
        eps2 = sb("eps2", [128, 1])
        rec.op("pool", lambda e: e.memset(eps2[:], 64e-5), writes=[lnbuf])
        NBUF = 3
        ARt = [sb("AR%d" % i, [64, NH, 2, 128], BF16) for i in range(NBUF)]
        Btt = [sb("Bt%d" % i, [64, NH, 128], BF16) for i in range(NBUF)]
        Ktt = [sb("Kt%d" % i, [64, NH, 128], BF16) for i in range(NBUF)]
        Vtt = [sb("Vt%d" % i, [64, NH, 128], BF16) for i in range(NBUF)]
        RKt = [sb("RK%d" % i, [64, NH, 128], BF16) for i in range(NBUF)]
        Gtm = [sb("Gtm%d" % i, [128, 512], BF16) for i in range(NBUF)]
        inb = [Buf() for _ in range(NBUF)]
        PCs = sb("PCs", [64, NH, NCH]); PCb = Buf()
        P2 = 2
        TM = [[sb("TM%d_%d" % (h, i), [128, 256], BF16) for i in range(P2)] for h in range(NH)]
        rks = [[sb("rks%d_%d" % (h, i), [128, 2]) for i in range(P2)] for h in range(NH)]
        M12 = [[sb("M12%d_%d" % (h, i), [128, 512], BF16) for i in range(P2)] for h in range(NH)]
        F32R = mybir.dt.float32r
        L0 = [[sb("L0%d_%d" % (h, i), [128, 128], F32R) for i in range(P2)] for h in range(NH)]
        LTr = [[sb("LTr%d_%d" % (h, i), [128, 128], F32R) for i in range(P2)] for h in range(NH)]
        Lpw = [[sb("Lpw%d_%d" % (h, i), [128, 256], F32R) for i in range(2)] for h in range(NH)]
        Xf = [[sb("Xr%d_%d" % (h, i), [128, 128], F32R) for i in range(P2)] for h in range(NH)]
        Xb = [[sb("Xb%d_%d" % (h, i), [128, 128], BF16) for i in range(P2)] for h in range(NH)]
        Gs = [[sb("Gs%d_%d" % (h, i), [64, 64], BF16) for i in range(P2)] for h in range(NH)]
        RhT = [[sb("RhT%d_%d" % (h, i), [64, 128], BF16) for i in range(P2)] for h in range(NH)]
        hb = [[{k: Buf() for k in ("TM", "rks", "M12", "L0", "X", "Xb", "G", "Rh")} for i in range(P2)] for h in range(NH)]
        Lpb = [[Buf() for i in range(2)] for h in range(NH)]
        Sf = [sb("Sf%d" % h, [64, 64]) for h in range(NH)]
        Sb_ = [sb("Sb%d" % h, [64, 64], BF16) for h in range(NH)]
        St = [sb("St%d" % h, [64, 64]) for h in range(NH)]
        Sfb = [Buf() for h in range(NH)]; Sbb = [Buf() for h in range(NH)]; Stb = [Buf() for h in range(NH)]
        Ytm = [sb("Ytm%d" % i, [128, 512]) for i in range(2)]; Ytmb = [Buf(), Buf()]
        ysq = sb("ysq", [128, 512]); ysqb = Buf()
        st = sb("gnst", [128, 5, 8]); stb = Buf()
        yto = Rot([sb("yto%d" % i, [128, 512], BF16) for i in range(2)])
        gi = 0
        for b in range(NB):
            rec.dma("sp", PCs[:], PCd[:, :, b * NCH:(b + 1) * NCH], writes=[PCb])
            for h in range(NH):
                rec.op("pool", lambda e, h=h: e.memset(Sf[h][:], 0.0), writes=[Sfb[h]])
                rec.op("pool", lambda e, h=h: e.memset(Sb_[h][:], 0.0), writes=[Sbb[h]])
            ctx = {}
            ctx2 = {}

            def load(c):
                gidx = b * NCH + c
                bi = gidx % NBUF
                tk = slice(b * T + c * 128, b * T + (c + 1) * 128)
                ib = inb[bi]
                AR, Bt, Kt, Vt, RK, Gt = ARt[bi], Btt[bi], Ktt[bi], Vtt[bi], RKt[bi], Gtm[bi]
                rec.dma("sp", AR[:, :, 0, :], src["AT"][:, :, tk], writes=[ib])
                rec.dma(os.environ.get("RW_DQ", "pool"), AR[:, :, 1, :], src["RTt"][:, :, tk], writes=[ib])
                rec.dma("sp", Bt[:], src["BT"][:, :, tk], writes=[ib])
                rec.dma(os.environ.get("RW_DQ", "pool"), Kt[:], src["KTt"][:, :, tk], writes=[ib])
                rec.dma("sp", Vt[:], src["VT"][:, :, tk], writes=[ib])
                rec.dma(os.environ.get("RW_DQ", "pool"), RK[:], src["RKT"][:, :, tk], writes=[ib])
                rec.dma("sp", Gt[:], dr["G_tm"][tk, :], writes=[ib])

            def front(c):
                gidx = b * NCH + c
                bi = gidx % NBUF
                pi = gidx % P2
                tk = slice(b * T + c * 128, b * T + (c + 1) * 128)
                ib = inb[bi]
                AR, Bt, Kt, Vt, RK, Gt = ARt[bi], Btt[bi], Ktt[bi], Vtt[bi], RKt[bi], Gtm[bi]
                Y = Ytm[pi]; Yb = Ytmb[pi]
                idb = g.ident_bf
                for h in range(NHL if LIM >= 1 else 0):
                    B = hb[h][pi]
                    pA, pAb = next_ps(g)
                    MM(rec, pA[:, 0:64], Bt[:, h, :], idb[0:64, 0:64], True, True, [ib, cb], [pAb])
                    MM(rec, pA[:, 64:128], Kt[:, h, :], idb[0:64, 0:64], True, True, [ib, cb], [pAb])
                    MM(rec, pA[:, 128:192], Vt[:, h, :], idb[0:64, 0:64], True, True, [ib, cb], [pAb])
                    MM(rec, pA[:, 192:256], AR[:, h, 0, :], idb[0:64, 0:64], True, True, [ib, cb], [pAb])
                    MM(rec, pA[:, 256:320], RK[:, h, :], g.ones_bf[0:64, 0:64], True, True, [ib, cb], [pAb])
                    CP(rec, "act", TM[h][pi][:], pA[:, 0:256], [pAb], [B["TM"]])
                    CP(rec, "act", rks[h][pi][:], pA[:, 256:258], [pAb], [B["rks"]])
                    pL, pLb = next_ps(g)
                    MM(rec, pL[:, 0:128], AR[:, h, 0, :], Bt[:, h, :], True, True, [ib], [pLb])
                    TT(rec, "dve", L0[h][pi][:], pL[:, 0:128], mlow[:], ALU.mult, [pLb, mb], [B["L0"]])
                    if LIM < 2:
                        continue
                    pB, pBb = next_ps(g)
                    arh = AR[:, h, :, :].rearrange("k a t -> k (a t)")
                    MM(rec, pB[:, 0:256], Bt[:, h, :], arh, True, True, [ib], [pBb])
                    MM(rec, pB[:, 256:512], Kt[:, h, :], arh, True, True, [ib], [pBb])
                    TT(rec, "dve", M12[h][pi][:], pB[:, :], masks[:], ALU.mult, [pBb, mb], [B["M12"]])
                    TT(rec, "dve", LTr[h][pi][:], pB[:, 0:128], masks[:, 0:128], ALU.mult, [pBb, mb], [B["L0"]])
                for h in range(NH if LIM >= 3 else 0):
                    B = hb[h][pi]
                    p3, p3b = next_ps(g)
                    MM(rec, p3[:, 0:64], M12[h][pi][:, 256:384], TM[h][pi][:, 128:192], True, True, [B["M12"], B["TM"]], [p3b])
                    CP(rec, "dve", Xf[h][pi][:, 64:128], p3[:, 0:64], [p3b], [B["X"]])
                    CP(rec, "pool", Xf[h][pi][:, 0:64], TM[h][pi][:, 192:256], [B["TM"]], [B["X"]])
                for i in range(7 if LIM >= 4 else 0):
                    for h in range(NH):
                        B = hb[h][pi]
                        if i == 0:
                            LT_ap, L_ap, lreads = LTr[h][pi][:], L0[h][pi][:], [B["L0"]]
                        else:
                            cur = Lpw[h][(i - 1) % 2]
                            L_ap, LT_ap, lreads = cur[:, 0:128], cur[:, 128:256], [Lpb[h][(i - 1) % 2]]
                        px, pxb = next_ps(g)
                        MM(rec, px[:, 0:128], LT_ap, Xf[h][pi][:], True, True, lreads + [B["X"]], [pxb])
                        if i < 6:
                            pc, pcb = next_ps(g)
                            MM(rec, pc[:, 0:128], LT_ap, L_ap, True, True, lreads, [pcb])
                            MM(rec, pc[:, 128:256], L_ap, LT_ap, True, True, lreads, [pcb])
                            CP(rec, "act", Lpw[h][i % 2][:], pc[:, 0:256], [pcb], [Lpb[h][i % 2]])
                        TT(rec, "dve", Xf[h][pi][:], Xf[h][pi][:].bitcast(F32), px[:, 0:128], ALU.add, [B["X"], pxb], [B["X"]])
                        if i == 6:
                            CP(rec, "pool", Xb[h][pi][:], Xf[h][pi][:].bitcast(F32), [B["X"]], [B["Xb"]])
                for h in range(NH if LIM >= 5 else 0):
                    B = hb[h][pi]
                    p5, p5b = next_ps(g)
                    MM(rec, p5[0:64, 0:64], Xb[h][pi][:, 0:64], TM[h][pi][:, 0:64], True, True, [B["Xb"], B["TM"]], [p5b])
                    CP(rec, "act", Gs[h][pi][:], p5[0:64, 0:64], [p5b], [B["G"]])
                    p5r, p5rb = next_ps(g)
                    MM(rec, p5r[0:64, 0:128], Xb[h][pi][:, 0:64], M12[h][pi][:, 128:256], True, True, [B["Xb"], B["M12"]], [p5rb])
                    TT(rec, "dve", RhT[h][pi][:], p5r[0:64, 0:128], AR[:, h, 1, :], ALU.add, [p5rb, ib], [B["Rh"]])
                ctx[c] = (bi, pi, tk, ib, AR, Gt, Y, Yb)

            def back(c):
                bi, pi, tk, ib, AR, Gt, Y, Yb = ctx.pop(c)
                for h in range(NH if LIM >= 6 else 0):
                    B = hb[h][pi]
                    p6, p6b = next_ps(g)
                    U = Xb[h][pi][:, 64:128]
                    Vm = TM[h][pi][:, 128:192]
                    MM(rec, p6[:, 0:64], M12[h][pi][:, 128:256], U, True, False, [B["M12"], B["Xb"]], [p6b])
                    MM(rec, p6[:, 0:64], M12[h][pi][:, 384:512], Vm, False, False, [B["M12"], B["TM"]], [p6b])
                    MM(rec, p6[:, 0:64], RhT[h][pi][:], Sb_[h][:], False, True, [B["Rh"], Sbb[h]], [p6b])
                    p7, p7b = next_ps(g)
                    MM(rec, p7[0:64, 0:64], TM[h][pi][:, 0:64], U, True, False, [B["TM"], B["Xb"]], [p7b])
                    MM(rec, p7[0:64, 0:64], TM[h][pi][:, 64:128], Vm, False, False, [B["TM"]], [p7b])
                    MM(rec, p7[0:64, 0:64], Gs[h][pi][:], Sb_[h][:], False, True, [B["G"], Sbb[h]], [p7b])
                    CP(rec, "act", Y[:, h * 64:(h + 1) * 64], p6[:, 0:64], [p6b], [Yb])
                    TT(rec, "dve", St[h][:], Sf[h][:], p7[0:64, 0:64], ALU.add, [Sfb[h], p7b], [Stb[h]])
                    TS(rec, "pool", Sf[h][:], St[h][:], PCs[:, h, c:c + 1], ALU.mult, [Stb[h], PCb], [Sfb[h]])
                    ACTF(rec, Sb_[h][:], St[h][:], AF.Identity, [Stb[h], PCb], [Sbb[h]], scale=PCs[:, h, c:c + 1])
                if LIM < 7:
                    return
                Y3 = Y[:].rearrange("p (h v) -> p h v", v=64)
                rec.op("dve", lambda e, Y3=Y3: e.tensor_reduce(out=st[:, 0, :], in_=Y3, op=ALU.add, axis=AX.X), reads=[Yb], writes=[stb])
                ACTF(rec, ysq[:], Y[:], AF.Square, [Yb], [ysqb])
                rec.op("dve", lambda e: e.tensor_reduce(out=st[:, 1, :], in_=ysq[:].rearrange("p (h v) -> p h v", v=64), op=ALU.add, axis=AX.X),
                       reads=[ysqb], writes=[stb])
                TS(rec, "dve", st[:, 2, :], st[:, 0, :], 1.0 / 64, ALU.mult, [stb], [stb])
                TT(rec, "dve", st[:, 3, :], st[:, 2, :], st[:, 2, :], ALU.mult, [stb], [stb])
                STT(rec, st[:, 3, :], st[:, 1, :], 1.0 / 64, st[:, 3, :], ALU.mult, ALU.subtract, [stb], [stb])
                ACTF(rec, st[:, 3, :], st[:, 3, :], AF.Sqrt, [stb, lnbuf], [stb], bias=eps2[:])
                rec.op("dve", lambda e: e.reciprocal(out=st[:, 4, :], in_=st[:, 3, :]), reads=[stb], writes=[stb])
                for h in range(NH):
                    TS(rec, "dve" if h % 2 else "pool", Y[:, h * 64:(h + 1) * 64], Y[:, h * 64:(h + 1) * 64], st[:, 2, h:h + 1], ALU.subtract,
                       [Yb, stb], [Yb], s2=st[:, 4, h:h + 1], op1=ALU.mult)
                TT(rec, "pool", Y[:], Y[:], lnw[:], ALU.mult, [Yb, lnbuf], [Yb])
                TT(rec, "pool", Y[:], Y[:], lnb[:], ALU.add, [Yb, lnbuf], [Yb])
                for h in range(NH):
                    STT(rec, Y[:, h * 64:(h + 1) * 64], TM[h][pi][:, 128:192], rks[h][pi][:, 0:1], Y[:, h * 64:(h + 1) * 64],
                        ALU.mult, ALU.add, [hb[h][pi]["TM"], hb[h][pi]["rks"], Yb], [Yb])
                TT(rec, "dve", Y[:], Y[:], Gt[:], ALU.mult, [Yb, ib], [Yb])
                ctx2[c] = (tk, Y, Yb)

            def back_b(c):
                tk, Y, Yb = ctx2.pop(c)
                pT, pTb = next_ps(g)
                for cc in range(4):
                    rec.op("pe", lambda e, pT=pT, cc=cc, Y=Y: e.transpose(pT[:, cc * 128:(cc + 1) * 128], Y[:, cc * 128:(cc + 1) * 128], g.ident[:]),
                           reads=[Yb, cb], writes=[pTb])
                yo, yob = yto.next()
                CP(rec, "act", yo[:], pT[:, :], [pTb], [yob])
                rec.dma("sp", YT[:, 4:8, tk], yo[:].rearrange("p (c t) -> p c t", t=128), reads=[yob])

            for c in range(NCH + 2):
                if c == 0:
                    load(0)
                if c + 1 < NCH:
                    load(c + 1)
                if c < NCH:
                    front(c)
                if 0 <= c - 2 < NCH:
                    back_b(c - 2)
                if 0 <= c - 1 < NCH:
                    back(c - 1)
        rec.emit()


def phase_pre1(nc, rec, g, dr, NTOK, T, N=256):
    l = 1
    XT = dr["XT"].rearrange("(k p) t -> p k t", p=128)
    fo = {k: dr[k].rearrange("(h p) t -> p h t", p=128) for k in ("QTl", "KTl", "KHl", "SGT")}
    PCh = dr["PCh"].rearrange("(h p) n -> p h n", p=128)
    ntile = NTOK // N
    NS = N // 128
    NC32 = 8 * N // 32
    with ExitStack() as es:
        sb = lambda n, s, d=F32: es.enter_context(nc.sbuf_tensor(uniq(n), s, d))
        W = sb("W1", [128, KC, 4096], BF16); Wb = Buf()
        st_rot = Rot([sb("wst%d" % i, [128, 1024]) for i in range(2)])
        jobs = []
        for kk in range(KC):
            for q4 in range(4):
                cs = slice(q4 * 1024, (q4 + 1) * 1024)
                jobs.append((dr["w_in_odd"][0, kk * 128:(kk + 1) * 128, cs], W[:, kk, cs], Wb))
        load_cast(rec, st_rot, jobs)
        lbl = sb("lbl", [128, 2, 8]); lbb = Buf()
        lb = sb("lb", [128, 8]); oml = sb("oml", [128, 8])
        rec.dma("sp", lbl[:], dr["hg_lbT"][:, :, :], writes=[lbb])
        TT(rec, "dve", lb[:], lbl[:, 1, :], lbl[:, 0, :], ALU.subtract, [lbb], [lbb])
        ACTF(rec, lb[:], lb[:], AF.Sigmoid, [lbb], [lbb])
        TS(rec, "dve", oml[:], lb[:], -1.0, ALU.mult, [lbb], [lbb], s2=1.0, op1=ALU.add)
        m32 = sb("m32", [128, 8 * N])
        rec.op("pool", lambda e: e.memset(m32[:], 1.0), writes=[lbb])
        rec.op("pool", lambda e: e.memset(m32[:].rearrange("p (c t) -> p c t", t=32)[:, :, 0:1], 0.0), writes=[lbb])
        x = sb("x", [128, KC, N]); xb = Buf()
        sq = sb("sq", [128, KC, N], BF16); sqb = Buf()
        hT = sb("hT", [128, KC, N], BF16); hb = Buf()
        tr = Rot([sb("tmp%d" % i, [128, N]) for i in range(3)])
        rstd = sb("rstd", [128, N]); rstdb = Buf()
        raw = sb("raw", [128, 24, N]); rawb = Buf()
        lf = sb("lf", [128, 8 * N]); lfb = Buf()
        bt = sb("bt", [128, 8 * N]); btb = Buf()
        ept = sb("ept", [128, 8 * N]); epb = Buf()
        ent = sb("ent", [128, 8 * N]); enb = Buf()
        ect = sb("ect", [128, 8 * N]); ecb = Buf()
        pct = sb("pct", [128, NC32]); pcb_ = Buf()
        ob = {k: (sb("o_" + k, [128, 8, N], BF16), Buf()) for k in fo}
        ito = Rot([sb("ito%d" % i, [128, 1024], BF16) for i in range(2)])
        for ti in range(ntile):
            b = (ti * N) // T
            t0 = ti * N
            rec.dma("sp", x[:], XT[:, :, t0:t0 + N], writes=[xb])
            t1, t1b = tr.next()
            rms_stats(rec, g, x[:], xb, KC, N, sq, sqb, t1, t1b, rstd, rstdb, D)
            for m in range(KC):
                t1, t1b = tr.next()
                TT(rec, "dve", t1[:], x[:, m, :], rstd[:], ALU.mult, [xb, rstdb], [t1b])
                ACTF(rec, hT[:, m, :], t1[:], AF.Identity, [t1b, g.Amod_b, g.modT_b], [hb],
                     scale=g.Amod[:, l, 0, m, b:b + 1], bias=g.modT[:, l, 0 + m, b:b + 1])
            for j in range(24):
                grp, h = j // 8, j % 8
                c0 = (0, 1024, 3072)[grp] + h * 128
                ps, psb = next_ps(g)
                for kk in range(KC):
                    MM(rec, ps[:, 0:N], W[:, kk, c0:c0 + 128], hT[:, kk, :], kk == 0, kk == KC - 1, [Wb, hb], [psb])
                evac(rec, j, raw[:, j, :], ps[:, 0:N], [psb], [rawb])
            rq = raw[:, 0:8, :].rearrange("p h t -> p (h t)")
            rf = raw[:, 8:16, :].rearrange("p h t -> p (h t)")
            rg = raw[:, 16:24, :].rearrange("p h t -> p (h t)")
            ACTF(rec, rq, rq, AF.Silu, [rawb], [rawb])
            o, obb = ob["SGT"]
            ACTF(rec, o[:].rearrange("p h t -> p (h t)"), rg, AF.Silu, [rawb], [obb])
            ACTF(rec, rf, rf, AF.Sigmoid, [rawb], [rawb])
            for h in range(8):
                TS(rec, "pool" if h % 2 else "dve", raw[:, 8 + h, :], raw[:, 8 + h, :], oml[:, h:h + 1], ALU.mult, [rawb, lbb], [rawb],
                   s2=lb[:, h:h + 1], op1=ALU.add)
            ACTF(rec, lf[:], rf, AF.Ln, [rawb], [lfb])
            TS(rec, "pool", rf, rf, -1.0, ALU.mult, [rawb], [rawb], s2=1.0, op1=ALU.add)
            rec.op("dve", lambda e: e.tensor_tensor_scan(out=bt[:], data0=m32[:], data1=lf[:], initial=0.0, op0=ALU.mult, op1=ALU.add),
                   reads=[lfb, lbb], writes=[btb])
            b3 = bt[:].rearrange("p (c t) -> p c t", t=32)
            ACTF(rec, ept[:], bt[:], AF.Exp, [btb], [epb])
            ACTF(rec, ent[:], bt[:], AF.Exp, [btb], [enb], scale=-1.0)
            TT(rec, "dve", ect[:].rearrange("p (c t) -> p c t", t=32), b3[:, :, 31:32].broadcast_to([128, NC32, 32]), b3, ALU.subtract,
               [btb], [ecb])
            ACTF(rec, ect[:], ect[:], AF.Exp, [ecb], [ecb])
            ACTF(rec, pct[:], b3[:, :, 31], AF.Exp, [btb], [pcb_])
            o, obb = ob["QTl"]
            TT(rec, "dve", o[:].rearrange("p h t -> p (h t)"), rq, ept[:], ALU.mult, [rawb, epb], [obb])
            o, obb = ob["KTl"]
            TT(rec, "pool", o[:].rearrange("p h t -> p (h t)"), rf, ent[:], ALU.mult, [rawb, enb], [obb])
            o, obb = ob["KHl"]
            TT(rec, "dve", o[:].rearrange("p h t -> p (h t)"), rf, ect[:], ALU.mult, [rawb, ecb], [obb])
            for i, k in enumerate(fo):
                o, obb = ob[k]
                rec.dma("sp" if i % 2 == 0 else "pool", fo[k][:, :, t0:t0 + N], o[:], reads=[obb])
            rec.dma("sp", PCh[:, :, t0 // 32:(t0 + N) // 32], pct[:].rearrange("p (h c) -> p h c", h=8), reads=[pcb_])
            for s in range(NS):
                io, iob = ito.next()
                for n in range(2):
                    ps, psb = next_ps(g)
                    for kk in range(KC):
                        MM(rec, ps[:, :], hT[:, kk, s * 128:(s + 1) * 128], W[:, kk, 2048 + n * 512:2048 + (n + 1) * 512],
                           kk == 0, kk == KC - 1, [Wb, hb], [psb])
                    evac(rec, n, io[:, n * 512:(n + 1) * 512], ps[:, :], [psb], [iob])
                rec.dma("sp", dr["I_tm"][t0 + s * 128:t0 + (s + 1) * 128, :], io[:], reads=[iob])
        rec.emit()


def phase_hgrn(nc, rec, g, dr, NB, T, N=512):
    NG = N // 128
    fi = {k: dr[k].rearrange("(h p) t -> p h t", p=128) for k in ("QTl", "KTl", "KHl", "SGT")}
    PCh = dr["PCh"].rearrange("(h p) n -> p h n", p=128)
    YT = dr.get("YT1", dr["YT"]).rearrange("(h p) t -> p h t", p=128)
    NH = 8
    import os
    HL = int(os.environ.get("HG_LIM", "9"))
    with ExitStack() as es:
        sb = lambda n, s, d=F32: es.enter_context(nc.sbuf_tensor(uniq(n), s, d))
        cb = g.const_b
        bmask = sb("bmask", [128, 128]); mb = Buf()
        rec.dma("sp", bmask[:], dr["hmask"][:, :], writes=[mb])
        onorm = sb("onorm", [128, 1])
        rec.dma("sp", onorm[:], dr["hg_onT"][:, :], writes=[mb])
        NBUF = 2
        ft = {k: [sb("f_%s%d" % (k, i), [128, NH, N], BF16) for i in range(NBUF)] for k in fi}
        itm = [sb("itm%d" % i, [128, NG, 1024], BF16) for i in range(NBUF)]
        pcs = [sb("pcs%d" % i, [128, NH, N // 32]) for i in range(NBUF)]
        inb = [Buf() for _ in range(NBUF)]
        yt = [sb("yt%d" % i, [128, NH, N], BF16) for i in range(2)]; ytb = [Buf(), Buf()]
        Sf = [sb("Sf%d" % h, [128, 128]) for h in range(NH)]
        Sb_ = [sb("Sbf%d" % h, [128, 128], BF16) for h in range(NH)]
        Sfb = [Buf() for h in range(NH)]; Sbb = [Buf() for h in range(NH)]
        khm = [[sb("khm%d_%d" % (h, i), [128, 128], BF16) for i in range(2)] for h in range(NH)]
        attm = [[sb("attm%d_%d" % (h, i), [128, 128], BF16) for i in range(2)] for h in range(NH)]
        khb = [[Buf() for i in range(2)] for h in range(NH)]
        atb = [[Buf() for i in range(2)] for h in range(NH)]
        oT = [sb("oT%d" % h, [128, 128]) for h in range(NH)]; oTb = [Buf() for h in range(NH)]
        osq = [sb("osq%d" % h, [128, 128], BF16) for h in range(NH)]; osqb = [Buf() for h in range(NH)]
        sd = [sb("sd%d" % h, [128, 128]) for h in range(NH)]; sdb = [Buf() for h in range(NH)]
        gi = 0
        ti_glob = 0
        pending = []
        for b in range(NB):
            for h in range(NH):
                rec.op("pool", lambda e, h=h: e.memset(Sf[h][:], 0.0), writes=[Sfb[h]])
                rec.op("pool", lambda e, h=h: e.memset(Sb_[h][:], 0.0), writes=[Sbb[h]])
            for tt in range(T // N):
                bi = ti_glob % NBUF
                ti_glob += 1
                t0 = b * T + tt * N
                ib = inb[bi]
                for i, k in enumerate(fi):
                    rec.dma("sp" if i % 2 == 0 else "pool", ft[k][bi][:], fi[k][:, :, t0:t0 + N], writes=[ib])
                rec.dma("sp", itm[bi][:], dr["I_tm"][t0:t0 + N, :].rearrange("(g p) d -> p g d", p=128), writes=[ib])
                rec.dma("sp", pcs[bi][:], PCh[:, :, t0 // 32:(t0 + N) // 32], writes=[ib])
                Q, K, KH, SG, IT, PC = ft["QTl"][bi], ft["KTl"][bi], ft["KHl"][bi], ft["SGT"][bi], itm[bi], pcs[bi]
                Yt, Ytb = yt[bi], ytb[bi]
                for gq in range(NG):
                    pi = gi % 2
                    gi += 1
                    gs = slice(gq * 128, (gq + 1) * 128)
                    for h in range(NH):
                        pK, pKb = next_ps(g)
                        MM(rec, pK[:, 0:128], KH[:, h, gs], g.ident_bf[:], True, True, [ib, cb], [pKb])
                        CP(rec, "act", khm[h][pi][:], pK[:, 0:128], [pKb], [khb[h][pi]])
                        pA, pAb = next_ps(g)
                        MM(rec, pA[:, 0:128], K[:, h, gs], Q[:, h, gs], True, True, [ib], [pAb])
                        TT(rec, "dve", attm[h][pi][:], pA[:, 0:128], bmask[:], ALU.mult, [pAb, mb], [atb[h][pi]])
                    for f_ in pending:
                        f_()
                    pending.clear()
                    for half in range(2 if HL >= 2 else 0):
                        hs = range(half * 4, half * 4 + 4)
                        pS = {}; pO = {}
                        pSj = [next_ps(g) for j in range(4)]
                        for h in hs:
                            hl = h - half * 4
                            for j in range(4):
                                r0, r1 = (32 * j, 32 * j + 32) if j < 3 else (64, 128)
                                MM(rec, pSj[j][0][:, hl * 128:(hl + 1) * 128], khm[h][pi][r0:r1, :],
                                   IT[r0:r1, gq, h * 128:(h + 1) * 128], True, True, [khb[h][pi], ib], [pSj[j][1]])
                        if HL < 3:
                            continue
                        for h in hs:
                            pO[h] = next_ps(g)
                            MM(rec, pO[h][0][:, 0:128], IT[:, gq, h * 128:(h + 1) * 128], attm[h][pi][:], True, False, [ib, atb[h][pi]], [pO[h][1]])
                        for j in range(4 if HL >= 4 else 0):
                            for h in hs:
                                c0 = gq * 128 + 32 * j
                                MM(rec, pO[h][0][:, 32 * j:32 * j + 32], Sb_[h][:], Q[:, h, c0:c0 + 32], False, j == 3, [Sbb[h], ib], [pO[h][1]])
                                ch = gq * 4 + j
                                hl = h - half * 4
                                STT(rec, Sf[h][:], Sf[h][:], PC[:, h, ch:ch + 1], pSj[j][0][:, hl * 128:(hl + 1) * 128], ALU.mult, ALU.add,
                                    [Sfb[h], ib, pSj[j][1]], [Sfb[h]])
                                if j == 3:
                                    TT(rec, "dve", Sf[h][:], Sf[h][:], pSj[2][0][:, hl * 128:(hl + 1) * 128], ALU.subtract, [Sfb[h], pSj[2][1]], [Sfb[h]])
                                CP(rec, "act", Sb_[h][:], Sf[h][:], [Sfb[h]], [Sbb[h]])
                        for h in (hs if HL >= 5 else []):
                            CP(rec, "act", oT[h][:], pO[h][0][:, 0:128], [pO[h][1]], [oTb[h]])
                            ACTF(rec, osq[h][:], oT[h][:], AF.Square, [oTb[h]], [osqb[h]])

                            def fin(h=h, gs=gs, SG=SG, Yt=Yt, Ytb=Ytb, ib=ib):
                                pZ, pZb = next_ps(g)
                                MM(rec, pZ[:, 0:128], g.ones_bf[:], osq[h][:], True, True, [cb, osqb[h]], [pZb])
                                ACTF(rec, sd[h][:], pZ[:, 0:128], AF.Ln, [pZb, cb], [sdb[h]], scale=1.0 / 128, bias=g.epsb[:])
                                ACTF(rec, sd[h][:], sd[h][:], AF.Exp, [sdb[h]], [sdb[h]], scale=-0.5)
                                STT(rec, oT[h][:], oT[h][:], onorm[:, 0:1], sd[h][:], ALU.mult, ALU.mult, [oTb[h], sdb[h], mb], [oTb[h]])
                                TT(rec, "dve", Yt[:, h, gs], oT[h][:], SG[:, h, gs], ALU.mult, [oTb[h], ib], [Ytb])
                            pending.append(fin)
                for f_ in pending:
                    f_()
                pending.clear()
                rec.dma("sp", YT[:, :, t0:t0 + N], Yt[:], reads=[Ytb])
        rec.emit()


SCRATCH = None


def build_program(NB, T, ext_scratch=False):
    NTOK = NB * T
    nc = bass.Bass("TRN2", target_bir_lowering=False)
    dr = {}
    shapes = input_shapes(NB, T)
    for k, (shp, dt) in shapes.items():
        dr[k] = nc.dram_tensor(k, list(shp), dt, kind="ExternalInput").ap()
    dr["out"] = nc.dram_tensor("out", [NTOK, 1024], F32, kind="ExternalOutput").ap()
    kind = "ExternalOutput" if ext_scratch else "Internal"

    def scr(name, shape, dt):
        dr[name] = nc.dram_tensor(name, shape, dt, kind=kind).ap()
    scr("XT", [1024, NTOK], F32)
    scr("RP", [1792, NTOK], F32)
    scr("QT", [8, 96, NTOK], BF16)
    scr("KT", [8, 96, NTOK], BF16)
    scr("V", [NTOK, 512], BF16)
    scr("YT", [1024, NTOK], BF16)
    for k in ("AT", "BT", "KTt", "RTt", "VT", "RKT"):
        scr(k, [512, NTOK], BF16)
    scr("G_tm", [NTOK, 512], BF16)
    scr("PC", [512, NTOK // 128], F32)
    for k in ("QTl", "KTl", "KHl", "SGT"):
        scr(k, [1024, NTOK], BF16)
    scr("I_tm", [NTOK, 1024], BF16)
    scr("PCh", [1024, NTOK // 32], F32)
    if ext_scratch:
        scr("YT1", [1024, NTOK], BF16)
    g = G()
    with ExitStack() as es:
        rec = Rec(nc, es)
        setup_globals(nc, es, g, NB)
        phase_mods(nc, rec, g, dr)
        phase_pre0(nc, rec, g, dr, NTOK, T)
        phase_mla(nc, rec, g, dr, NB, T)
        phase_rwkv_prep(nc, rec, g, dr, NTOK, T)
        phase_rwkv_main(nc, rec, g, dr, NB, T)
        phase_post(nc, rec, g, dr, 0, NTOK, T, last=False)
        phase_pre1(nc, rec, g, dr, NTOK, T)
        phase_hgrn(nc, rec, g, dr, NB, T)
        phase_post(nc, rec, g, dr, 1, NTOK, T, last=True)
    return nc, rec


def input_shapes(NB, T):
    NTOK = NB * T
    f = F32
    return {
        "ident": ((128, 128), f), "cmask": ((128, 128), f), "x": ((NTOK, 1024), f), "pos": ((NB, T), I32),
        "cT": ((128, 8, NB), f), "ada_bT": ((128, 2, 48), f), "norm_mixT": ((128, 2, 8), f), "norm_ffnT": ((128, 2, 8), f),
        "final_normT": ((128, 8), f), "ada_w": ((2, 1024, 6144), f), "w_out_even": ((1, 1024, 1024), f),
        "w_out_odd": ((1, 1024, 1024), f), "ffn_w_gate": ((2, 1024, 2816), f), "ffn_w_up": ((2, 1024, 2816), f),
        "ffn_w_down": ((2, 2816, 1024), f), "w_in_even": ((1, 1024, 2464), f), "mla_w_uq": ((1, 384, 768), f),
        "w_in_odd": ((1, 1024, 4096), f), "w_in_sw": ((1024, 32), f), "w_uq_sw": ((384, 768), f), "w_ukv_r": ((256, 1024), f),
        "ctab": ((128, 8), f), "rmasks": ((128, 384), f), "rwkv_ln_w": ((1, 512), f), "rwkv_ln_b": ((1, 512), f),
        "hg_lbT": ((128, 2, 8), f), "hg_onT": ((128, 1), f), "hmask": ((128, 128), f), "rw_mu": ((128, 14), f),
        "rw_par": ((128, 20), f), "rwkv_w2": ((1, 64, 512), f), "rwkv_a2": ((1, 64, 512), f), "rwkv_g2": ((1, 128, 512), f),
    }


_CACHE = {}


def kernel(**inputs):
    from concourse.bass_utils import run_bass_kernel_spmd
    NB, T, NCORE = 4, 2048, 8
    inputs = {k: np.asarray(v) for k, v in inputs.items()}
    if "nc" not in _CACHE:
        _CACHE["nc"] = build_program(NB, T)[0]
    nc = _CACHE["nc"]
    shared = None
    in_maps = []
    for i in range(NCORE):
        sl = slice(i * NB, (i + 1) * NB)
        inp = dict(inputs)
        inp["x"] = inputs["x"][sl]
        inp["c"] = inputs["c"][sl]
        inp["positions"] = inputs["positions"][sl]
        if shared is None:
            shared = host_layout(inp)
            d = dict(shared)
        else:
            d = dict(shared)
            d["x"] = np.ascontiguousarray(inp["x"].reshape(-1, 1024))
            d["pos"] = np.ascontiguousarray(inp["positions"].astype(np.int32))
            d["cT"] = np.ascontiguousarray(inp["c"].T.reshape(8, 128, NB).transpose(1, 0, 2))
        in_maps.append(d)
    res = run_bass_kernel_spmd(nc, in_maps, core_ids=list(range(NCORE)))
    outs = [np.asarray(r["out"]).reshape(NB, T, 1024) for r in res.results]
    return np.concatenate(outs, axis=0).astype(np.float32)
```

```python
import numpy as np
import concourse.bass as bass
import concourse.mybir as mybir

F32 = mybir.dt.float32
BF16 = mybir.dt.bfloat16
I32 = mybir.dt.int32
AF = mybir.ActivationFunctionType
ALU = mybir.AluOpType
AX = mybir.AxisListType

ENGS = ("pe", "dve", "act", "pool", "sp")
NSLOT = 8


class Buf:
    __slots__ = ("name", "w", "r")

    def __init__(self, name=""):
        self.name = name
        self.w = None
        self.r = {}


class Rec:
    def __init__(self, nc, es, same_eng_sync=True):
        self.nc = nc
        self.sem = {}
        for e in ENGS:
            self.sem[e] = es.enter_context(nc.semaphore("s_" + e))
        self.cnt = {e: 0 for e in ENGS}
        self.q = {e: [] for e in ENGS}
        self.seen = {e: {} for e in ENGS}
        self.slots = {}
        for e in ("sp", "act", "pool"):
            self.slots[e] = [[es.enter_context(nc.semaphore("d_%s%d" % (e, i))), 0] for i in range(NSLOT)]
        self.slot_i = {e: 0 for e in self.slots}
        self.same = same_eng_sync
        self.nops = 0

    def _need(self, eng, ev, waits):
        if ev is None:
            return
        key, val, src = ev
        if src == eng and key[0] == "e":
            if eng == "pe" or not self.same:
                return
        if self.seen[eng].get(key, 0) >= val:
            return
        self.seen[eng][key] = val
        waits.append((key, val))

    def _deps(self, eng, reads, writes):
        waits = []
        for b in reads:
            self._need(eng, b.w, waits)
        for b in writes:
            self._need(eng, b.w, waits)
            for ev in b.r.values():
                self._need(eng, ev, waits)
        return waits

    def _mark(self, ev, reads, writes):
        for b in reads:
            b.r[ev[0]] = ev
        for b in writes:
            b.w = ev
            b.r = {}

    def op(self, eng, fn, reads=(), writes=()):
        waits = self._deps(eng, reads, writes)
        self.cnt[eng] += 1
        ev = (("e", eng), self.cnt[eng], eng)
        self.q[eng].append((waits, fn, ("e", eng), 1))
        self._mark(ev, reads, writes)
        self.nops += 1

    def dma(self, eng, out, in_, reads=(), writes=()):
        sl = self.slots[eng]
        i = self.slot_i[eng]
        self.slot_i[eng] = (i + 1) % NSLOT
        key = ("d", eng, i)
        waits = []
        if sl[i][1] > 0:
            self._need(eng, (key, sl[i][1], "dma"), waits)
        waits += self._deps(eng, reads, writes)
        sl[i][1] += 16
        ev = (key, sl[i][1], "dma")
        self.q[eng].append((waits, lambda e, o=out, s=in_: e.dma_start(out=o, in_=s), key, 16))
        self._mark(ev, reads, writes)
        self.nops += 1

    def _semh(self, key):
        if key[0] == "e":
            return self.sem[key[1]]
        return self.slots[key[1]][key[2]][0]

    def drain_dmas(self):
        for e in self.slots:
            for i, (s, v) in enumerate(self.slots[e]):
                if v > 0:
                    key = ("d", e, i)
                    if self.seen[e].get(key, 0) < v:
                        self.seen[e][key] = v
                        self.q[e].append(([(key, v)], None, None, 0))

    def emit(self):
        self.drain_dmas()
        nc = self.nc
        rec = self

        def run(engname, handle):
            for waits, fn, key, inc in rec.q[engname]:
                for (k, v) in waits:
                    handle.wait_ge(rec._semh(k), v)
                if fn is not None:
                    fn(handle).then_inc(rec._semh(key), inc)
            rec.q[engname] = []

        with nc.Block() as block:
            @block.tensor
            def _(e):
                run("pe", e)

            @block.vector
            def _(e):
                run("dve", e)

            @block.scalar
            def _(e):
                run("act", e)

            @block.gpsimd
            def _(e):
                run("pool", e)

            @block.sync
            def _(e):
                run("sp", e)


from contextlib import ExitStack
import numpy as np
import concourse.bass as bass
import concourse.mybir as mybir

D = 1024
KC = 8
FH = 2816
JH = 22
EPS = 1e-6


class G:
    pass


_uid = [0]


def uniq(n):
    _uid[0] += 1
    return "sb%d_%s" % (_uid[0], n)


def pool_of(n, name):
    return [Buf("%s%d" % (name, i)) for i in range(n)]


class Rot:
    def __init__(self, tiles):
        self.t = tiles
        self.b = [Buf() for _ in tiles]
        self.i = 0

    def next(self):
        i = self.i
        self.i = (i + 1) % len(self.t)
        return self.t[i], self.b[i]


def setup_globals(nc, es, g, NB):
    g.NB = NB
    sb = lambda n, s, d=F32: es.enter_context(nc.sbuf_tensor(uniq(n), s, d))
    g.modT = sb("modT", [128, 2, 48, NB])
    g.modT_b = Buf("modT")
    g.Amod = sb("Amod", [128, 2, 2, KC, NB])
    g.Amod_b = Buf("Amod")
    g.ident = sb("ident", [128, 128])
    g.ident_bf = sb("ident_bf", [128, 128], BF16)
    g.ones_bf = sb("ones_bf", [128, 128], BF16)
    g.const_b = Buf("const")
    g.epsb = sb("epsb", [128, 1])
    g.ps = [es.enter_context(nc.psum_tensor("ps%d" % i, [128, 512], F32)) for i in range(8)]
    g.psb = [Buf("ps%d" % i) for i in range(8)]
    g.ps_i = 0


def next_ps(g):
    i = g.ps_i
    g.ps_i = (i + 1) % 8
    return g.ps[i], g.psb[i]


def phase_mods(nc, rec, g, dr):
    NB = g.NB
    with ExitStack() as es:
        sb = lambda n, s, d=F32: es.enter_context(nc.sbuf_tensor(uniq(n), s, d))
        cT = sb("cT", [128, KC, NB]); cT_b = Buf()
        condT = sb("condT", [128, KC, NB]); condT_b = Buf()
        adab = sb("adab", [128, 2, 48]); adab_b = Buf()
        gains = sb("gains", [128, 2, 2, KC]); gains_b = Buf()
        wst = [sb("adaw%d" % i, [128, KC, 1024]) for i in range(2)]
        wst_b = [Buf(), Buf()]
        rec.dma("sp", g.ident[:], dr["ident"][:, :], writes=[g.const_b])
        rec.op("pool", lambda e: e.memset(g.ones_bf[:], 1.0), writes=[g.const_b])
        rec.op("pool", lambda e: e.memset(g.epsb[:], EPS), writes=[g.const_b])
        rec.op("dve", lambda e: e.tensor_copy(out=g.ident_bf[:], in_=g.ident[:]), reads=[g.const_b], writes=[g.const_b])
        rec.dma("sp", cT[:], dr["cT"][:, :, :], writes=[cT_b])
        rec.dma("sp", adab[:], dr["ada_bT"][:, :, :], writes=[adab_b])
        rec.dma("sp", gains[:, 0], dr["norm_mixT"][:, :, :], writes=[gains_b])
        rec.dma("sp", gains[:, 1], dr["norm_ffnT"][:, :, :], writes=[gains_b])
        rec.op("act", lambda e: e.activation(out=condT[:], in_=cT[:], func=AF.Silu), reads=[cT_b], writes=[condT_b])
        it = 0
        for l in range(2):
            for pc in range(6):
                w, wb = wst[it % 2], wst_b[it % 2]
                it += 1
                src = dr["ada_w"][l, :, pc * 1024:(pc + 1) * 1024].rearrange("(k p) n -> p k n", p=128)
                for kk in range(KC):
                    rec.dma("sp", w[:, kk, :], src[:, kk, :], writes=[wb])
                for jj in range(8):
                    j = pc * 8 + jj
                    ps, psb = next_ps(g)
                    for kk in range(KC):
                        rec.op("pe", lambda e, ps=ps, w=w, kk=kk, jj=jj: e.matmul(
                            ps[:, 0:NB], lhsT=w[:, kk, jj * 128:(jj + 1) * 128], rhs=condT[:, kk, :],
                            start=(kk == 0), stop=(kk == KC - 1)), reads=[wb, condT_b], writes=[psb])
                    rec.op("dve", lambda e, ps=ps, l=l, j=j: e.tensor_scalar(
                        out=g.modT[:, l, j, :], in0=ps[:, 0:NB], scalar1=adab[:, l, j:j + 1], scalar2=None,
                        op0=ALU.add), reads=[psb, adab_b], writes=[g.modT_b])
        for l in range(2):
            for sub in range(2):
                for kk in range(KC):
                    j = (1 + 3 * sub) * KC + kk
                    rec.op("dve", lambda e, l=l, sub=sub, kk=kk, j=j: e.tensor_scalar(
                        out=g.Amod[:, l, sub, kk, :], in0=g.modT[:, l, j, :], scalar1=1.0,
                        scalar2=gains[:, sub, l, kk:kk + 1], op0=ALU.add, op1=ALU.mult),
                        reads=[g.modT_b, gains_b], writes=[g.Amod_b])
        rec.emit()


def load_cast(rec, st_rot, srcs_dsts, engs=("dve", "pool")):
    for i, (src, dst, db) in enumerate(srcs_dsts):
        st, stb = st_rot.next()
        n = src.shape[-1]
        rec.dma("sp", st[:, 0:n], src, writes=[stb])
        eng = engs[i % len(engs)]
        rec.op(eng, lambda e, st=st, dst=dst, n=n: e.tensor_copy(out=dst, in_=st[:, 0:n]), reads=[stb], writes=[db])


def phase_post(nc, rec, g, dr, l, NTOK, T, last, N=256):
    wout = dr["w_out_even"][0] if l == 0 else dr["w_out_odd"][0]
    XT = dr["XT"].rearrange("(k p) t -> p k t", p=128)
    YT = (dr.get("YT1", dr["YT"]) if l == 1 else dr["YT"]).rearrange("(k p) t -> p k t", p=128)
    ntile = NTOK // N
    with ExitStack() as es:
        sb = lambda n, s, d=F32: es.enter_context(nc.sbuf_tensor(uniq(n), s, d))
        Wo = sb("Wo", [128, KC, D], BF16)
        Wg = sb("Wg", [128, KC, FH], BF16)
        Wu = sb("Wu", [128, KC, FH], BF16)
        Wd = sb("Wd", [128, JH, D], BF16)
        Wb = Buf("W")
        st_rot = Rot([sb("wst%d" % i, [128, FH // 4]) for i in range(2)])
        xs = [sb("x%d" % i, [128, KC, N]) for i in range(2)]
        xbs = [Buf(), Buf()]
        hT = sb("hT", [128, KC, N], BF16); hb = Buf()
        sq = sb("sq", [128, KC, N], BF16); sqb = Buf()
        actT = sb("actT", [128, JH, N], BF16); actb = Buf()
        tr = Rot([sb("tmp%d" % i, [128, N]) for i in range(3)])
        rstd = sb("rstd", [128, N]); rstdb = Buf()
        if last:
            fng = sb("fng", [128, KC]); fngb = Buf()
            rec.dma("sp", fng[:], dr["final_normT"][:, :], writes=[fngb])
            fr = Rot([sb("fin%d" % i, [128, D]) for i in range(1)])
        jobs = []
        for kk in range(KC):
            for hh in range(2):
                cs = slice(hh * 512, (hh + 1) * 512)
                jobs.append((wout[kk * 128:(kk + 1) * 128, cs], Wo[:, kk, cs], Wb))
        for kk in range(KC):
            for hh in range(4):
                cs = slice(hh * (FH // 4), (hh + 1) * (FH // 4))
                jobs.append((dr["ffn_w_gate"][l, kk * 128:(kk + 1) * 128, cs], Wg[:, kk, cs], Wb))
                jobs.append((dr["ffn_w_up"][l, kk * 128:(kk + 1) * 128, cs], Wu[:, kk, cs], Wb))
        for j in range(JH):
            for hh in range(2):
                cs = slice(hh * 512, (hh + 1) * 512)
                jobs.append((dr["ffn_w_down"][l, j * 128:(j + 1) * 128, cs], Wd[:, j, cs], Wb))
        load_cast(rec, st_rot, jobs)

        def stats(x, xb):
            rec.op("act", lambda e: e.activation(out=sq[:], in_=x[:], func=AF.Square), reads=[xb], writes=[sqb])
            ps, psb = next_ps(g)
            for kk in range(KC):
                MM(rec, ps[:, 0:N], g.ones_bf[:], sq[:, kk, :], kk == 0, kk == KC - 1, [sqb, g.const_b], [psb])
            t1, t1b = tr.next()
            ACTF(rec, t1[:], ps[:, 0:N], AF.Sqrt, [psb, g.const_b], [t1b], scale=1.0 / D, bias=g.epsb[:])
            rec.op("dve", lambda e: e.reciprocal(out=rstd[:], in_=t1[:]), reads=[t1b], writes=[rstdb])

        def load_x(i):
            rec.dma("sp", xs[i % 2][:], XT[:, :, i * N:(i + 1) * N], writes=[xbs[i % 2]])

        def load_y(i):
            rec.dma("sp", sq[:], YT[:, :, i * N:(i + 1) * N], writes=[sqb])

        def outproj(i):
            x, xb = xs[i % 2], xbs[i % 2]
            b = (i * N) // T
            for m in range(KC):
                ps, psb = next_ps(g)
                for kk in range(KC):
                    MM(rec, ps[:, 0:N], Wo[:, kk, m * 128:(m + 1) * 128], sq[:, kk, :], kk == 0, kk == KC - 1, [Wb, sqb], [psb])
                STT(rec, x[:, m, :], ps[:, 0:N], g.modT[:, l, 16 + m, b:b + 1], x[:, m, :], ALU.mult, ALU.add, [psb, g.modT_b, xb], [xb])

        def rmsmod(i):
            x, xb = xs[i % 2], xbs[i % 2]
            b = (i * N) // T
            stats(x, xb)
            for m in range(KC):
                t1, t1b = tr.next()
                TT(rec, "dve", t1[:], x[:, m, :], rstd[:], ALU.mult, [xb, rstdb], [t1b])
                ACTF(rec, hT[:, m, :], t1[:], AF.Identity, [t1b, g.Amod_b, g.modT_b], [hb],
                     scale=g.Amod[:, l, 1, m, b:b + 1], bias=g.modT[:, l, 24 + m, b:b + 1])

        def gateup(i):
            for j in range(JH):
                pg, pgb = next_ps(g)
                pu, pub = next_ps(g)
                for kk in range(KC):
                    MM(rec, pg[:, 0:N], Wg[:, kk, j * 128:(j + 1) * 128], hT[:, kk, :], kk == 0, kk == KC - 1, [Wb, hb], [pgb])
                for kk in range(KC):
                    MM(rec, pu[:, 0:N], Wu[:, kk, j * 128:(j + 1) * 128], hT[:, kk, :], kk == 0, kk == KC - 1, [Wb, hb], [pub])
                t1, t1b = tr.next()
                ACTF(rec, t1[:], pg[:, 0:N], AF.Silu, [pgb], [t1b])
                TT(rec, "dve", actT[:, j, :], t1[:], pu[:, 0:N], ALU.mult, [t1b, pub], [actb])

        def down(i):
            x, xb = xs[i % 2], xbs[i % 2]
            b = (i * N) // T
            for m in range(KC):
                ps, psb = next_ps(g)
                for j in range(JH):
                    MM(rec, ps[:, 0:N], Wd[:, j, m * 128:(m + 1) * 128], actT[:, j, :], j == 0, j == JH - 1, [Wb, actb], [psb])
                STT(rec, x[:, m, :], ps[:, 0:N], g.modT[:, l, 40 + m, b:b + 1], x[:, m, :], ALU.mult, ALU.add, [psb, g.modT_b, xb], [xb])

        def store(i):
            x, xb = xs[i % 2], xbs[i % 2]
            if not last:
                rec.dma("sp", XT[:, :, i * N:(i + 1) * N], x[:], reads=[xb])
                return
            stats(x, xb)
            for m in range(KC):
                STT(rec, x[:, m, :], x[:, m, :], fng[:, m:m + 1], rstd[:], ALU.mult, ALU.mult, [xb, rstdb, fngb], [xb])
            for s in range(N // 128):
                f, fb = fr.next()
                for m in range(KC):
                    if m % 4 == 0:
                        ps, psb = next_ps(g)
                    rec.op("pe", lambda e, ps=ps, m=m, s=s, x=x: e.transpose(
                        ps[:, (m % 4) * 128:(m % 4 + 1) * 128], x[:, m, s * 128:(s + 1) * 128], g.ident[:]),
                        reads=[xb, g.const_b], writes=[psb])
                    if m % 4 == 3:
                        evac(rec, m // 4, f[:, (m - 3) * 128:(m + 1) * 128], ps[:, :], [psb], [fb])
                t0 = i * N + s * 128
                rec.dma("sp", dr["out"][t0:t0 + 128, :], f[:], reads=[fb])

        load_x(0)
        load_y(0)
        outproj(0)
        rmsmod(0)
        for i in range(ntile):
            if i + 1 < ntile:
                load_x(i + 1)
                if not last:
                    load_y(i + 1)
            gateup(i)
            if i + 1 < ntile:
                if last:
                    load_y(i + 1)
                outproj(i + 1)
                rmsmod(i + 1)
            down(i)
            store(i)
        rec.emit()


TWO_PI = 2.0 * np.pi
MLA_SCALE = 96.0 ** -0.5


def evac(rec, i, out, in_, reads, writes):
    if i % 2 == 0:
        rec.op("act", lambda e: e.copy(out=out, in_=in_), reads=reads, writes=writes)
    else:
        rec.op("dve", lambda e: e.tensor_copy(out=out, in_=in_), reads=reads, writes=writes)


def rms_stats(rec, g, src, srcb, nch, N, sq, sqb, t1, t1b, rstd, rstdb, dim):
    rec.op("act", lambda e: e.activation(out=sq[:, 0:nch, :], in_=src, func=AF.Square), reads=[srcb], writes=[sqb])
    ps, psb = next_ps(g)
    for kk in range(nch):
        rec.op("pe", lambda e, kk=kk: e.matmul(ps[:, 0:N], lhsT=g.ones_bf[:], rhs=sq[:, kk, :],
                                                start=(kk == 0), stop=(kk == nch - 1)),
               reads=[sqb, g.const_b], writes=[psb])
    rec.op("act", lambda e: e.activation(out=t1[:], in_=ps[:, 0:N], func=AF.Sqrt, scale=1.0 / dim, bias=g.epsb[:]),
           reads=[psb, g.const_b], writes=[t1b])
    rec.op("dve", lambda e: e.reciprocal(out=rstd[:], in_=t1[:]), reads=[t1b], writes=[rstdb])


def phase_pre0(nc, rec, g, dr, NTOK, T, N=512):
    l = 0
    XT = dr["XT"].rearrange("(k p) t -> p k t", p=128)
    RP = dr["RP"].rearrange("(k p) t -> p k t", p=128)
    ntile = NTOK // N
    NS = N // 128
    with ExitStack() as es:
        sb = lambda n, s, d=F32: es.enter_context(nc.sbuf_tensor(uniq(n), s, d))
        Win = sb("Win", [128, KC, 2464], BF16)
        Wsw = sb("Wsw", [128, KC, 32], BF16)
        Wuq = sb("Wuq", [128, 3, 768], BF16)
        Wuqs = sb("Wuqs", [128, 3, 768], BF16)
        Wukv = sb("Wukv", [128, 2, 1024], BF16)
        Wb = Buf("W")
        st_rot = Rot([sb("wst%d" % i, [128, 1232]) for i in range(2)])
        jobs = []
        for kk in range(KC):
            for hh in range(2):
                cs = slice(hh * 1232, (hh + 1) * 1232)
                jobs.append((dr["w_in_even"][0, kk * 128:(kk + 1) * 128, cs], Win[:, kk, cs], Wb))
            jobs.append((dr["w_in_sw"][kk * 128:(kk + 1) * 128, :], Wsw[:, kk, :], Wb))
        for kk in range(3):
            jobs.append((dr["mla_w_uq"][0, kk * 128:(kk + 1) * 128, :], Wuq[:, kk, :], Wb))
            jobs.append((dr["w_uq_sw"][kk * 128:(kk + 1) * 128, :], Wuqs[:, kk, :], Wb))
        for kk in range(2):
            jobs.append((dr["w_ukv_r"][kk * 128:(kk + 1) * 128, :], Wukv[:, kk, :], Wb))
        load_cast(rec, st_rot, jobs)
        ctab = sb("ctab", [128, 8]); ctabb = Buf()
        rec.dma("sp", ctab[:], dr["ctab"][:, :], writes=[ctabb])
        negpi = sb("negpi", [128, 1]);
        rec.op("pool", lambda e: e.memset(negpi[:], -np.pi), writes=[ctabb])
        Cq = sb("Cq", [96, N]); Sq = sb("Sq", [96, N]); trb = Buf()
        rec.op("pool", lambda e: e.memset(Cq[0:64, :], MLA_SCALE), writes=[trb])
        rec.op("pool", lambda e: e.memset(Sq[0:64, :], 0.0), writes=[trb])
        Ck = sb("Ck", [32, N]); Sk = sb("Sk", [32, N])
        posi = sb("posi", [96, N], I32); posib = Buf()
        posf = sb("posf", [96, N]); posfb = Buf()
        ua = sb("ua", [96, N]); uab = Buf()
        ui = sb("ui", [96, N], I32); uib = Buf()
        uf = sb("uf", [96, N]); ufb = Buf()
        um = sb("um", [96, N]); umb = Buf()
        trig = [sb("sinv", [96, N]), sb("cosv", [96, N])]; trigb = [Buf(), Buf()]
        xin = sb("xin", [128, NS, D]); xinb = Buf()
        xT = sb("xT", [128, KC, N]); xTb = Buf()
        sq = sb("sq", [128, KC, N], BF16); sqb = Buf()
        hT = sb("hT", [128, KC, N], BF16); hb = Buf()
        tr = Rot([sb("tmp%d" % i, [128, N]) for i in range(3)])
        rstd = sb("rstd", [128, N]); rstdb = Buf()
        cq = sb("cq", [128, 3, N]); cqb = Buf()
        cqn = sb("cqn", [128, 3, N], BF16); cqnb = Buf()
        ckv = sb("ckv", [128, 2, N]); ckvb = Buf()
        ckvn = sb("ckvn", [128, 2, N], BF16); ckvnb = Buf()
        qo = Rot([sb("qo%d" % i, [96, N], BF16) for i in range(2)])
        ko = Rot([sb("ko%d" % i, [128, N], BF16) for i in range(2)])
        kro = sb("kro", [32, N], BF16); krob = Buf()
        vo = Rot([sb("vo%d" % i, [128, 512], BF16) for i in range(2)])
        rpo = Rot([sb("rpo%d" % i, [128, N]) for i in range(3)])
        ei = 0
        for ti in range(ntile):
            b = (ti * N) // T
            t0 = ti * N
            rec.dma("sp", xin[:], dr["x"][t0:t0 + N, :].rearrange("(s p) d -> p s d", p=128), writes=[xinb])
            for m in range(KC):
                ps, psb = next_ps(g)
                for s in range(NS):
                    rec.op("pe", lambda e, ps=ps, m=m, s=s: e.transpose(
                        ps[:, s * 128:(s + 1) * 128], xin[:, s, m * 128:(m + 1) * 128], g.ident[:]),
                        reads=[xinb, g.const_b], writes=[psb])
                evac(rec, m, xT[:, m, :], ps[:, 0:N], [psb], [xTb])
            rec.dma("sp", XT[:, :, t0:t0 + N], xT[:], reads=[xTb])
            rec.dma("sp", posi[:], dr["pos"][b:b + 1, (t0 % T):(t0 % T) + N].partition_broadcast(96), writes=[posib])
            rec.op("dve", lambda e: e.tensor_copy(out=posf[:], in_=posi[:]), reads=[posib], writes=[posfb])
            for w in range(2):
                off = 0.5 if w == 0 else 0.75
                rec.op("dve", lambda e, off=off: e.tensor_scalar(out=ua[:], in0=posf[:], scalar1=ctab[0:96, 0:1], scalar2=off,
                                                                 op0=ALU.mult, op1=ALU.add), reads=[posfb, ctabb], writes=[uab])
                rec.op("dve", lambda e: e.tensor_copy(out=ui[:], in_=ua[:]), reads=[uab], writes=[uib])
                rec.op("dve", lambda e: e.tensor_copy(out=uf[:], in_=ui[:]), reads=[uib], writes=[ufb])
                rec.op("dve", lambda e: e.tensor_tensor(out=ua[:], in0=ua[:], in1=uf[:], op=ALU.subtract), reads=[uab, ufb], writes=[uab])
                rec.op("dve", lambda e: e.tensor_scalar(out=um[:], in0=ua[:], scalar1=0.0, scalar2=None, op0=ALU.is_lt),
                       reads=[uab], writes=[umb])
                rec.op("dve", lambda e: e.tensor_tensor(out=ua[:], in0=ua[:], in1=um[:], op=ALU.add), reads=[uab, umb], writes=[uab])
                rec.op("act", lambda e, w=w: e.activation(out=trig[w][:], in_=ua[:], func=AF.Sin, scale=TWO_PI, bias=negpi[0:96, :]),
                       reads=[uab, ctabb], writes=[trigb[w]])
            rec.op("dve", lambda e: e.tensor_copy(out=Ck[:], in_=trig[1][0:32, :]), reads=[trigb[1]], writes=[trb])
            rec.op("dve", lambda e: e.tensor_scalar(out=Sk[:], in0=trig[0][0:32, :], scalar1=ctab[0:32, 1:2], scalar2=None, op0=ALU.mult),
                   reads=[trigb[0], ctabb], writes=[trb])
            rec.op("dve", lambda e: e.tensor_scalar(out=Cq[64:96, :], in0=trig[1][64:96, :], scalar1=MLA_SCALE, scalar2=None, op0=ALU.mult),
                   reads=[trigb[1]], writes=[trb])
            rec.op("dve", lambda e: e.tensor_scalar(out=Sq[64:96, :], in0=trig[0][64:96, :], scalar1=ctab[64:96, 2:3], scalar2=None, op0=ALU.mult),
                   reads=[trigb[0], ctabb], writes=[trb])
            t1, t1b = tr.next()
            rms_stats(rec, g, xT[:], xTb, KC, N, sq, sqb, t1, t1b, rstd, rstdb, D)
            for m in range(KC):
                t1, t1b = tr.next()
                rec.op("dve", lambda e, t1=t1, m=m: e.tensor_tensor(out=t1[:], in0=xT[:, m, :], in1=rstd[:], op=ALU.mult),
                       reads=[xTb, rstdb], writes=[t1b])
                rec.op("act", lambda e, t1=t1, m=m, b=b: e.activation(
                    out=hT[:, m, :], in_=t1[:], func=AF.Identity, scale=g.Amod[:, l, 0, m, b:b + 1],
                    bias=g.modT[:, l, 0 + m, b:b + 1]), reads=[t1b, g.Amod_b, g.modT_b], writes=[hb])

            def proj(ps, psb, c0, M, W=Win):
                for kk in range(KC):
                    rec.op("pe", lambda e, kk=kk: e.matmul(ps[0:M, 0:N], lhsT=W[:, kk, c0:c0 + M], rhs=hT[:, kk, :],
                                                            start=(kk == 0), stop=(kk == KC - 1)), reads=[Wb, hb], writes=[psb])
            for c in range(3):
                ps, psb = next_ps(g)
                proj(ps, psb, c * 128, 128)
                evac(rec, c, cq[:, c, :], ps[:, 0:N], [psb], [cqb])
            for c in range(2):
                ps, psb = next_ps(g)
                proj(ps, psb, 384 + c * 128, 128)
                evac(rec, c + 1, ckv[:, c, :], ps[:, 0:N], [psb], [ckvb])
            for (src, srcb, nch, dst, dstb, gcol, dim) in ((cq, cqb, 3, cqn, cqnb, 3, 384), (ckv, ckvb, 2, ckvn, ckvnb, 6, 256)):
                t1, t1b = tr.next()
                rms_stats(rec, g, src[:], srcb, nch, N, sq, sqb, t1, t1b, rstd, rstdb, dim)
                for c in range(nch):
                    rec.op("dve", lambda e, c=c, src=src, dst=dst, gcol=gcol: e.scalar_tensor_tensor(
                        out=dst[:, c, :], in0=src[:, c, :], scalar=ctab[:, gcol + c:gcol + c + 1], in1=rstd[:],
                        op0=ALU.mult, op1=ALU.mult), reads=[srcb, rstdb, ctabb], writes=[dstb])
            ps, psb = next_ps(g)
            proj(ps, psb, 640, 32)
            ps2, psb2 = next_ps(g)
            proj(ps2, psb2, 0, 32, W=Wsw)
            t1, t1b = tr.next()
            t2, t2b = tr.next()
            rec.op("dve", lambda e, t1=t1, ps2=ps2: e.tensor_tensor(out=t1[0:32, :], in0=ps2[0:32, 0:N], in1=Sk[:], op=ALU.mult),
                   reads=[psb2, trb], writes=[t1b])
            rec.op("dve", lambda e, t2=t2, ps=ps: e.tensor_tensor(out=t2[0:32, :], in0=ps[0:32, 0:N], in1=Ck[:], op=ALU.mult),
                   reads=[psb, trb], writes=[t2b])
            rec.op("dve", lambda e, t1=t1, t2=t2: e.tensor_tensor(out=kro[:], in0=t1[0:32, :], in1=t2[0:32, :], op=ALU.add),
                   reads=[t1b, t2b], writes=[krob])
            for h in range(8):
                rec.dma("pool" if h % 2 else "sp", dr["KT"][h, 64:96, t0:t0 + N], kro[:], reads=[krob])
            for h in range(8):
                psA, psAb = next_ps(g)
                psB, psBb = next_ps(g)
                for (ps, psb, W) in ((psA, psAb, Wuq), (psB, psBb, Wuqs)):
                    for kk in range(3):
                        rec.op("pe", lambda e, ps=ps, W=W, kk=kk, h=h: e.matmul(
                            ps[0:96, 0:N], lhsT=W[:, kk, h * 96:(h + 1) * 96], rhs=cqn[:, kk, :],
                            start=(kk == 0), stop=(kk == 2)), reads=[Wb, cqnb], writes=[psb])
                t1, t1b = tr.next()
                t2, t2b = tr.next()
                q, qb = qo.next()
                rec.op("dve", lambda e, t1=t1, psB=psB: e.tensor_tensor(out=t1[0:96, :], in0=psB[0:96, 0:N], in1=Sq[:], op=ALU.mult),
                       reads=[psBb, trb], writes=[t1b])
                rec.op("dve", lambda e, t2=t2, psA=psA: e.tensor_tensor(out=t2[0:96, :], in0=psA[0:96, 0:N], in1=Cq[:], op=ALU.mult),
                       reads=[psAb, trb], writes=[t2b])
                rec.op("pool", lambda e, t1=t1, t2=t2, q=q: e.tensor_tensor(out=q[:], in0=t1[0:96, :], in1=t2[0:96, :], op=ALU.add),
                       reads=[t1b, t2b], writes=[qb])
                rec.dma("sp", dr["QT"][h, :, t0:t0 + N], q[:], reads=[qb])
            for hp in range(4):
                ps, psb = next_ps(g)
                for kk in range(2):
                    rec.op("pe", lambda e, ps=ps, kk=kk, hp=hp: e.matmul(
                        ps[:, 0:N], lhsT=Wukv[:, kk, hp * 128:(hp + 1) * 128], rhs=ckvn[:, kk, :],
                        start=(kk == 0), stop=(kk == 1)), reads=[Wb, ckvnb], writes=[psb])
                k, kb = ko.next()
                evac(rec, hp, k[:], ps[:, 0:N], [psb], [kb])
                rec.dma("sp", dr["KT"][2 * hp, 0:64, t0:t0 + N], k[0:64, :], reads=[kb])
                rec.dma("pool", dr["KT"][2 * hp + 1, 0:64, t0:t0 + N], k[64:128, :], reads=[kb])
            for s in range(NS):
                ps, psb = next_ps(g)
                for kk in range(2):
                    rec.op("pe", lambda e, ps=ps, kk=kk, s=s: e.matmul(
                        ps[:, :], lhsT=ckvn[:, kk, s * 128:(s + 1) * 128], rhs=Wukv[:, kk, 512:1024],
                        start=(kk == 0), stop=(kk == 1)), reads=[Wb, ckvnb], writes=[psb])
                v, vb = vo.next()
                evac(rec, s, v[:], ps[:, :], [psb], [vb])
                rec.dma("sp", dr["V"][t0 + s * 128:t0 + (s + 1) * 128, :], v[:], reads=[vb])
            for c in range(14):
                ps, psb = next_ps(g)
                proj(ps, psb, 672 + c * 128, 128)
                o, ob = rpo.next()
                evac(rec, c, o[:], ps[:, 0:N], [psb], [ob])
                rec.dma("sp" if c % 2 else "pool", RP[:, c, t0:t0 + N], o[:], reads=[ob])
        rec.emit()


def phase_mla_g(nc, rec, g, dr, NB, T, nptr=2):
    NKB = T // 128
    YT = dr["YT"].rearrange("(k p) t -> p k t", p=128)
    offs = [0]
    for kb in range(NKB):
        offs.append(offs[-1] + (T - kb * 128))
    with ExitStack() as es:
        sb = lambda n, s, d=F32: es.enter_context(nc.sbuf_tensor(uniq(n), s, d))
        qr = Rot([sb("q%d" % i, [96, T], BF16) for i in range(2)])
        kr = Rot([sb("k%d" % i, [96, T], BF16) for i in range(2)])
        va = [sb("va%d" % i, [128, NKB, 65], BF16) for i in range(2)]
        vab = [Buf(), Buf()]
        for i in range(2):
            rec.op("pool", lambda e, i=i: e.memset(va[i][:], 1.0), writes=[vab[i]])
        ptr = Rot([sb("pt%d" % i, [128, offs[-1]], BF16) for i in range(nptr)])
        mask = sb("mask", [128, 128], BF16); maskb = Buf()
        mstage = sb("mstage", [128, 128])
        rec.dma("sp", mstage[:], dr["cmask"][:, :], writes=[maskb])
        rec.op("dve", lambda e: e.tensor_copy(out=mask[:], in_=mstage[:]), reads=[maskb], writes=[maskb])
        ytm = sb("ytm", [128, NKB, 512]); ytmb = Buf()
        rc = Rot([sb("rc%d" % i, [128, 1]) for i in range(4)])
        ytr = Rot([sb("yts%d" % i, [128, 4, 128], BF16) for i in range(2)])
        it = 0
        for b in range(NB):
            for h in range(8):
                q, qb_ = qr.next()
                k, kb_ = kr.next()
                v, vb_ = va[it % 2], vab[it % 2]
                it += 1
                pt, ptb = ptr.next()
                rec.dma("sp", q[:], dr["QT"][h, :, b * T:(b + 1) * T], writes=[qb_])
                rec.dma("pool", k[:], dr["KT"][h, :, b * T:(b + 1) * T], writes=[kb_])
                rec.dma("sp", v[:, :, 0:64], dr["V"][b * T:(b + 1) * T, h * 64:(h + 1) * 64].rearrange("(k p) d -> p k d", p=128),
                        writes=[vb_])
                for kb in range(NKB):
                    q0 = kb * 128
                    c = q0
                    while c < T:
                        n = min(512, T - c)
                        ps, psb = next_ps(g)
                        rec.op("pe", lambda e, ps=ps, k=k, q=q, q0=q0, c=c, n=n: e.matmul(
                            ps[:, 0:n], lhsT=k[:, q0:q0 + 128], rhs=q[:, c:c + n], start=True, stop=True),
                            reads=[kb_, qb_], writes=[psb])
                        o0 = offs[kb] + (c - q0)
                        rec.op("act", lambda e, ps=ps, pt=pt, o0=o0, n=n: e.activation(out=pt[:, o0:o0 + n], in_=ps[:, 0:n], func=AF.Exp),
                               reads=[psb], writes=[ptb])
                        c += n
                    o0 = offs[kb]
                    rec.op("pool", lambda e, pt=pt, o0=o0: e.tensor_tensor(out=pt[:, o0:o0 + 128], in0=pt[:, o0:o0 + 128], in1=mask[:], op=ALU.mult),
                           reads=[ptb, maskb], writes=[ptb])
                for qb in range(NKB):
                    ps, psb = next_ps(g)
                    for kb in range(qb + 1):
                        o0 = offs[kb] + (qb - kb) * 128
                        rec.op("pe", lambda e, ps=ps, pt=pt, v=v, o0=o0, kb=kb, qb=qb: e.matmul(
                            ps[:, 0:65], lhsT=pt[:, o0:o0 + 128], rhs=v[:, kb, :], start=(kb == 0), stop=(kb == qb)),
                            reads=[ptb, vb_], writes=[psb])
                    r, rb = rc.next()
                    rec.op("dve", lambda e, r=r, ps=ps: e.reciprocal(out=r[:], in_=ps[:, 64:65]), reads=[psb], writes=[rb])
                    rec.op("dve", lambda e, r=r, ps=ps, qb=qb, h=h: e.tensor_scalar(
                        out=ytm[:, qb, h * 64:(h + 1) * 64], in0=ps[:, 0:64], scalar1=r[:, 0:1], scalar2=None, op0=ALU.mult),
                        reads=[psb, rb], writes=[ytmb])
                yield 1
            for qb in range(NKB):
                ps, psb = next_ps(g)
                for c in range(4):
                    rec.op("pe", lambda e, ps=ps, qb=qb, c=c: e.transpose(
                        ps[:, c * 128:(c + 1) * 128], ytm[:, qb, c * 128:(c + 1) * 128], g.ident[:]),
                        reads=[ytmb, g.const_b], writes=[psb])
                yt, ytb = ytr.next()
                evac(rec, qb, yt[:].rearrange("p c t -> p (c t)"), ps[:, :], [psb], [ytb])
                t0 = b * T + qb * 128
                rec.dma("sp", YT[:, 0:4, t0:t0 + 128], yt[:], reads=[ytb])
        yield "END"


def run_gens(rec, gens):
    gens = list(gens)
    live = list(gens)
    while live:
        for gen in list(live):
            if next(gen) == "END":
                live.remove(gen)
    rec.emit()
    for gen in reversed(gens):
        for _ in gen:
            pass


def phase_mla(nc, rec, g, dr, NB, T):
    run_gens(rec, [phase_mla_g(nc, rec, g, dr, NB, T)])


def host_layout(inp):
    f32 = np.float32
    d = {}
    NB = inp["c"].shape[0]
    d["ident"] = np.eye(128, dtype=f32)
    d["cmask"] = np.triu(np.ones((128, 128), dtype=f32))
    d["x"] = np.ascontiguousarray(inp["x"].reshape(-1, 1024))
    d["pos"] = np.ascontiguousarray(inp["positions"].astype(np.int32))
    d["cT"] = np.ascontiguousarray(inp["c"].T.reshape(8, 128, NB).transpose(1, 0, 2))
    d["ada_bT"] = np.ascontiguousarray(inp["ada_b"].reshape(2, 48, 128).transpose(2, 0, 1))
    d["norm_mixT"] = np.ascontiguousarray(inp["norm_mix"].reshape(2, 8, 128).transpose(2, 0, 1))
    d["norm_ffnT"] = np.ascontiguousarray(inp["norm_ffn"].reshape(2, 8, 128).transpose(2, 0, 1))
    d["final_normT"] = np.ascontiguousarray(inp["final_norm"].reshape(8, 128).T)
    for k in ["ada_w", "w_out_even", "w_out_odd", "ffn_w_gate", "ffn_w_up", "ffn_w_down", "w_in_even", "mla_w_uq", "w_in_odd"]:
        d[k] = inp[k]
    wi = inp["w_in_even"][0]
    d["w_in_sw"] = np.ascontiguousarray(np.concatenate([wi[:, 656:672], wi[:, 640:656]], axis=1))
    wq = inp["mla_w_uq"][0].reshape(384, 8, 96)
    d["w_uq_sw"] = np.ascontiguousarray(np.concatenate([wq[:, :, 0:64], wq[:, :, 80:96], wq[:, :, 64:80]], axis=2).reshape(384, 768))
    wkv = inp["mla_w_ukv"][0].reshape(256, 8, 128)
    d["w_ukv_r"] = np.ascontiguousarray(np.concatenate([wkv[:, :, 0:64].reshape(256, 512), wkv[:, :, 64:128].reshape(256, 512)], axis=1))
    ctab = np.zeros((128, 8), dtype=f32)
    invf = (1.0 / (10000.0 ** (np.arange(0, 32, 2, dtype=np.float32) / 32))).astype(np.float32)
    for base in (0, 16, 64, 80):
        ctab[base:base + 16, 0] = invf / np.float32(2 * np.pi)
    ctab[0:16, 1] = -1.0
    ctab[16:32, 1] = 1.0
    ctab[64:80, 2] = -MLA_SCALE
    ctab[80:96, 2] = MLA_SCALE
    ctab[:, 3:6] = inp["mla_q_norm"][0].reshape(3, 128).T
    ctab[:, 6:8] = inp["mla_kv_norm"][0].reshape(2, 128).T
    d["ctab"] = ctab
    su = np.triu(np.ones((128, 128), dtype=f32), 1)
    iu = np.triu(np.ones((128, 128), dtype=f32), 0)
    sl = np.tril(np.ones((128, 128), dtype=f32), -1)
    d["rmasks"] = np.ascontiguousarray(np.concatenate([su, iu, sl], axis=1))
    for k in ("rwkv_ln_w", "rwkv_ln_b"):
        d[k] = inp[k]
    d["hg_lbT"] = np.ascontiguousarray(inp["hg_lb_logits"].reshape(2, 8, 128).transpose(2, 0, 1))
    d["hg_onT"] = np.ascontiguousarray(inp["hg_out_norm"][0].reshape(128, 1))
    blk = np.arange(128) // 32
    d["hmask"] = np.ascontiguousarray(((blk[:, None] == blk[None, :]) & (np.arange(128)[:, None] <= np.arange(128)[None, :])).astype(f32))
    d["rw_mu"] = np.ascontiguousarray(inp["rwkv_mu"][0].reshape(14, 128).T)
    d["rw_par"] = np.ascontiguousarray(np.concatenate(
        [inp[k][0].reshape(4, 128).T for k in ("rwkv_w0", "rwkv_a0", "rwkv_k_k", "rwkv_k_a", "rwkv_r_k")], axis=1))
    for k in ("rwkv_w2", "rwkv_a2", "rwkv_g2"):
        d[k] = inp[k]
    return d


NEG_EM05 = -float(np.exp(-0.5))


def phase_rwkv_prep_g(nc, rec, g, dr, NTOK, T, N=256):
    RP = dr["RP"].rearrange("(k p) t -> p k t", p=128)
    outs = {k: dr[k].rearrange("(c p) t -> p c t", p=128) for k in ("AT", "BT", "KTt", "RTt", "VT", "RKT")}
    PC = dr["PC"].rearrange("(c p) n -> p c n", p=128)
    ntile = NTOK // N
    NS = N // 128
    with ExitStack() as es:
        sb = lambda n, s, d=F32: es.enter_context(nc.sbuf_tensor(uniq(n), s, d))
        par = sb("par", [128, 64]); parb = Buf()
        rec.dma("sp", par[:, 0:14], dr["rw_mu"][:, :], writes=[parb])
        rec.dma("sp", par[:, 28:48], dr["rw_par"][:, :], writes=[parb])
        rec.op("dve", lambda e: e.tensor_scalar(out=par[:, 14:28], in0=par[:, 0:14], scalar1=-1.0, scalar2=1.0, op0=ALU.mult, op1=ALU.add),
               reads=[parb], writes=[parb])
        rec.op("dve", lambda e: e.tensor_scalar(out=par[:, 48:52], in0=par[:, 40:44], scalar1=-1.0, scalar2=1.0, op0=ALU.mult, op1=ALU.add),
               reads=[parb], writes=[parb])
        wst = sb("wst", [128, 512]); wstb = Buf()
        w2b = sb("w2b", [128, 512], BF16); g2b = sb("g2b", [128, 512], BF16); Wb = Buf()
        rec.dma("sp", wst[0:64, :], dr["rwkv_w2"][0, :, :], writes=[wstb])
        rec.dma("sp", wst[64:128, :], dr["rwkv_a2"][0, :, :], writes=[wstb])
        rec.op("dve", lambda e: e.tensor_copy(out=w2b[:], in_=wst[:]), reads=[wstb], writes=[Wb])
        rec.dma("sp", wst[:, :], dr["rwkv_g2"][0, :, :], reads=[], writes=[wstb])
        rec.op("dve", lambda e: e.tensor_copy(out=g2b[:], in_=wst[:]), reads=[wstb], writes=[Wb])
        bones = sb("bones", [128, 128], BF16)
        rec.op("pool", lambda e: e.memset(bones[:], 0.0), writes=[Wb])
        rec.op("pool", lambda e: e.memset(bones[0:64, 0:64], 1.0), writes=[Wb])
        rec.op("pool", lambda e: e.memset(bones[64:128, 64:128], 1.0), writes=[Wb])
        rmask = sb("rmask", [128, N])
        rec.op("pool", lambda e: e.memset(rmask[:], 1.0), writes=[Wb])
        rec.op("pool", lambda e: e.memset(rmask[:].rearrange("p (c t) -> p c t", t=128)[:, :, 0:1], 0.0), writes=[Wb])
        p = sb("p", [128, 14, N]); pb = Buf()
        psh = sb("psh", [128, 14, N]); pshb = Buf()
        tmpr = Rot([sb("mt%d" % i, [128, N]) for i in range(3)])
        wab = sb("wab", [128, N], BF16); wabb = Buf()
        sgl = sb("sgl", [128, N], BF16); sglb = Buf()
        F4 = lambda n: (sb(n, [128, 4, N]), Buf())
        ld, ldb = F4("ld"); bb, bbb = F4("bb"); epos, eposb = F4("epos"); eneg, enegb = F4("eneg"); eprev, eprevb = F4("eprev")
        aa, aab = F4("aa"); kk, kkb = F4("kk"); kp, kpb = F4("kp"); rn, rnb = F4("rn")
        sqk = sb("sqk", [128, 4, N], BF16); sqkb = Buf()
        ob = {k: (sb("o_" + k, [128, 4, N], BF16), Buf()) for k in outs}
        pco = sb("pco", [128, 4, NS]); pcob = Buf()
        gto = Rot([sb("gto%d" % i, [128, 512], BF16) for i in range(2)])
        for ti in range(ntile):
            t0 = ti * N
            rec.dma("sp", p[:], RP[:, :, t0:t0 + N], writes=[pb])
            if t0 % T == 0:
                rec.op("pool", lambda e: e.memset(psh[:, :, 0:1], 0.0), writes=[pshb])
                rec.dma("pool", psh[:, :, 1:N], RP[:, :, t0:t0 + N - 1], writes=[pshb])
            else:
                rec.dma("pool", psh[:, :, :], RP[:, :, t0 - 1:t0 + N - 1], writes=[pshb])
            for j in range(14):
                t1, t1b = tmpr.next()
                rec.op("act", lambda e, j=j, t1=t1: e.activation(out=t1[:], in_=p[:, j, :], func=AF.Identity, scale=par[:, 14 + j:15 + j]),
                       reads=[pb, parb], writes=[t1b])
                rec.op("dve", lambda e, j=j, t1=t1: e.scalar_tensor_tensor(out=p[:, j, :], in0=psh[:, j, :], scalar=par[:, j:j + 1], in1=t1[:],
                                                                          op0=ALU.mult, op1=ALU.add), reads=[pshb, t1b, parb, pb], writes=[pb])
            rec.op("act", lambda e: e.activation(out=wab[0:64, :], in_=p[0:64, 12, :], func=AF.Tanh), reads=[pb], writes=[wabb])
            rec.op("dve", lambda e: e.tensor_copy(out=wab[64:128, :], in_=p[64:128, 12, :]), reads=[pb], writes=[wabb])
            for c in range(4):
                ps, psb = next_ps(g)
                rec.op("pe", lambda e, ps=ps, c=c: e.matmul(ps[:, 0:N], lhsT=w2b[0:64, c * 128:(c + 1) * 128], rhs=wab[0:64, :], start=True, stop=True),
                       reads=[Wb, wabb], writes=[psb])
                rec.op("act", lambda e, ps=ps, c=c: e.activation(out=ld[:, c, :], in_=ps[:, 0:N], func=AF.Sigmoid, bias=par[:, 28 + c:29 + c]),
                       reads=[psb, parb], writes=[ldb])
            for c in range(4):
                ps, psb = next_ps(g)
                rec.op("pe", lambda e, ps=ps, c=c: e.matmul(ps[:, 0:N], lhsT=w2b[64:128, c * 128:(c + 1) * 128], rhs=wab[64:128, :], start=True, stop=True),
                       reads=[Wb, wabb], writes=[psb])
                rec.op("act", lambda e, ps=ps, c=c: e.activation(out=aa[:, c, :], in_=ps[:, 0:N], func=AF.Sigmoid, bias=par[:, 32 + c:33 + c]),
                       reads=[psb, parb], writes=[aab])
            rec.op("act", lambda e: e.activation(out=sgl[:], in_=p[:, 13, :], func=AF.Sigmoid), reads=[pb], writes=[sglb])
            rec.op("dve", lambda e: e.tensor_scalar(out=ld[:], in0=ld[:], scalar1=NEG_EM05, scalar2=None, op0=ALU.mult), reads=[ldb], writes=[ldb])
            for c in range(4):
                rec.op("dve", lambda e, c=c: e.tensor_tensor_scan(out=bb[:, c, :], data0=rmask[:], data1=ld[:, c, :], initial=0.0,
                                                                  op0=ALU.mult, op1=ALU.add), reads=[ldb, Wb], writes=[bbb])
            rec.op("pool", lambda e: e.tensor_tensor(out=eprev[:], in0=bb[:], in1=ld[:], op=ALU.subtract), reads=[bbb, ldb], writes=[eprevb])
            rec.op("act", lambda e: e.activation(out=epos[:], in_=bb[:], func=AF.Exp), reads=[bbb], writes=[eposb])
            rec.op("act", lambda e: e.activation(out=eneg[:], in_=bb[:], func=AF.Exp, scale=-1.0), reads=[bbb], writes=[enegb])
            rec.op("act", lambda e: e.activation(out=eprev[:], in_=eprev[:], func=AF.Exp), reads=[eprevb], writes=[eprevb])
            for c in range(4):
                rec.op("act", lambda e, c=c: e.activation(out=sqk[:, c, :], in_=p[:, 4 + c, :], func=AF.Square, scale=par[:, 36 + c:37 + c]),
                       reads=[pb, parb], writes=[sqkb])
            for c in range(4):
                ps, psb = next_ps(g)
                rec.op("pe", lambda e, ps=ps, c=c: e.matmul(ps[:, 0:N], lhsT=bones[:], rhs=sqk[:, c, :], start=True, stop=True),
                       reads=[Wb, sqkb], writes=[psb])
                rec.op("act", lambda e, ps=ps, c=c: e.activation(out=rn[:, c, :], in_=ps[:, 0:N], func=AF.Sqrt), reads=[psb], writes=[rnb])
            rec.op("dve", lambda e: e.tensor_scalar(out=rn[:], in0=rn[:], scalar1=1e-12, scalar2=None, op0=ALU.max), reads=[rnb], writes=[rnb])
            rec.op("dve", lambda e: e.reciprocal(out=rn[:], in_=rn[:]), reads=[rnb], writes=[rnb])
            for c in range(4):
                rec.op("dve", lambda e, c=c: e.scalar_tensor_tensor(out=kk[:, c, :], in0=p[:, 4 + c, :], scalar=par[:, 36 + c:37 + c], in1=rn[:, c, :],
                                                                   op0=ALU.mult, op1=ALU.mult), reads=[pb, parb, rnb], writes=[kkb])
                rec.op("dve", lambda e, c=c: e.tensor_scalar(out=kp[:, c, :], in0=aa[:, c, :], scalar1=par[:, 40 + c:41 + c], scalar2=par[:, 48 + c:49 + c],
                                                              op0=ALU.mult, op1=ALU.add), reads=[aab, parb], writes=[kpb])
            rec.op("dve", lambda e: e.tensor_tensor(out=kp[:], in0=kp[:], in1=p[:, 4:8, :], op=ALU.mult), reads=[kpb, pb], writes=[kpb])
            o, obb = ob["AT"]
            rec.op("dve", lambda e, o=o: e.scalar_tensor_tensor(out=o[:], in0=kk[:], scalar=-1.0, in1=eprev[:], op0=ALU.mult, op1=ALU.mult),
                   reads=[kkb, eprevb], writes=[obb])
            o, obb = ob["BT"]
            rec.op("pool", lambda e: e.tensor_tensor(out=kk[:], in0=kk[:], in1=aa[:], op=ALU.mult), reads=[kkb, aab], writes=[kkb])
            rec.op("dve", lambda e, o=o: e.tensor_tensor(out=o[:], in0=kk[:], in1=eneg[:], op=ALU.mult), reads=[kkb, enegb], writes=[obb])
            o, obb = ob["KTt"]
            rec.op("pool", lambda e, o=o: e.tensor_tensor(out=o[:], in0=kp[:], in1=eneg[:], op=ALU.mult), reads=[kpb, enegb], writes=[obb])
            o, obb = ob["RTt"]
            rec.op("dve", lambda e, o=o: e.tensor_tensor(out=o[:], in0=p[:, 0:4, :], in1=epos[:], op=ALU.mult), reads=[pb, eposb], writes=[obb])
            o, obb = ob["VT"]
            rec.op("act", lambda e, o=o: e.copy(out=o[:], in_=p[:, 8:12, :]), reads=[pb], writes=[obb])
            o, obb = ob["RKT"]
            for c in range(4):
                rec.op("dve", lambda e, o=o, c=c: e.scalar_tensor_tensor(out=o[:, c, :], in0=p[:, c, :], scalar=par[:, 44 + c:45 + c], in1=kp[:, c, :],
                                                                        op0=ALU.mult, op1=ALU.mult), reads=[pb, parb, kpb], writes=[obb])
            rec.op("pool", lambda e: e.tensor_copy(out=pco[:], in_=epos[:].rearrange("p c (s t) -> p c s t", t=128)[:, :, :, 127]),
                   reads=[eposb], writes=[pcob])
            rec.dma("sp", PC[:, :, t0 // 128:t0 // 128 + NS], pco[:], reads=[pcob])
            for i, k in enumerate(outs):
                o, obb = ob[k]
                rec.dma("sp" if i % 2 == 0 else "pool", outs[k][:, :, t0:t0 + N], o[:], reads=[obb])
            for s in range(NS):
                ps, psb = next_ps(g)
                rec.op("pe", lambda e, ps=ps, s=s: e.matmul(ps[:, :], lhsT=sgl[:, s * 128:(s + 1) * 128], rhs=g2b[:], start=True, stop=True),
                       reads=[sglb, Wb], writes=[psb])
                go, gob = gto.next()
                evac(rec, s, go[:], ps[:, :], [psb], [gob])
                rec.dma("sp", dr["G_tm"][t0 + s * 128:t0 + (s + 1) * 128, :], go[:], reads=[gob])
            yield 1
        yield "END"


def phase_rwkv_prep(nc, rec, g, dr, NTOK, T, N=256):
    run_gens(rec, [phase_rwkv_prep_g(nc, rec, g, dr, NTOK, T, N)])


def MM(rec, out, lhsT, rhs, start, stop, reads, writes):
    rec.op("pe", lambda e: e.matmul(out, lhsT=lhsT, rhs=rhs, start=start, stop=stop), reads=reads, writes=writes)


def CP(rec, eng, out, in_, reads, writes):
    if eng == "act":
        rec.op("act", lambda e: e.copy(out=out, in_=in_), reads=reads, writes=writes)
    else:
        rec.op(eng, lambda e: e.tensor_copy(out=out, in_=in_), reads=reads, writes=writes)


def TT(rec, eng, out, in0, in1, op, reads, writes):
    rec.op(eng, lambda e: e.tensor_tensor(out=out, in0=in0, in1=in1, op=op), reads=reads, writes=writes)


def TS(rec, eng, out, in0, s1, op0, reads, writes, s2=None, op1=None):
    if op1 is None:
        rec.op(eng, lambda e: e.tensor_scalar(out=out, in0=in0, scalar1=s1, scalar2=None, op0=op0), reads=reads, writes=writes)
    else:
        rec.op(eng, lambda e: e.tensor_scalar(out=out, in0=in0, scalar1=s1, scalar2=s2, op0=op0, op1=op1), reads=reads, writes=writes)


def STT(rec, out, in0, scalar, in1, op0, op1, reads, writes):
    rec.op("dve", lambda e: e.scalar_tensor_tensor(out=out, in0=in0, scalar=scalar, in1=in1, op0=op0, op1=op1), reads=reads, writes=writes)


def ACTF(rec, out, in_, func, reads, writes, scale=1.0, bias=None):
    if bias is None:
        rec.op("act", lambda e: e.activation(out=out, in_=in_, func=func, scale=scale), reads=reads, writes=writes)
    else:
        rec.op("act", lambda e: e.activation(out=out, in_=in_, func=func, scale=scale, bias=bias), reads=reads, writes=writes)


def phase_rwkv_main(nc, rec, g, dr, NB, T):
    import os
    NCH = int(os.environ.get("RW_NCH", T // 128))
    YT = dr["YT"].rearrange("(k p) t -> p k t", p=128)
    src = {k: dr[k].rearrange("(h k) t -> k h t", k=64) for k in ("AT", "BT", "KTt", "RTt", "VT", "RKT")}
    PCd = dr["PC"].rearrange("(h k) n -> k h n", k=64)
    NH = 8
    NHL = int(os.environ.get("RW_NH", "8"))
    LIM = int(os.environ.get("RW_LIM", "9"))
    with ExitStack() as es:
        sb = lambda n, s, d=F32: es.enter_context(nc.sbuf_tensor(uniq(n), s, d))
        cb = g.const_b
        masks = sb("masks", [128, 512]); mb = Buf()
        mlow = sb("mlow", [128, 128])
        rec.dma("sp", masks[:, 0:256], dr["rmasks"][:, 0:256], writes=[mb])
        rec.dma("sp", masks[:, 256:512], dr["rmasks"][:, 0:256], writes=[mb])
        rec.dma("sp", mlow[:], dr["rmasks"][:, 256:384], writes=[mb])
        lnw = sb("lnw", [128, 512]); lnb = sb("lnb", [128, 512]); lnbuf = Buf()
        rec.dma("sp", lnw[:], dr["rwkv_ln_w"][0:1, :].partition_broadcast(128), writes=[lnbuf])
        rec.dma("sp", lnb[:], dr["rwkv_ln_b"][0:1, :].partition_broadcast(128), writes=[lnbuf])
        eps2 = sb("eps2", [128, 1])
        rec.op("pool", lambda e: e.memset(eps2[:], 64e-5), writes=[lnbuf])
        NBUF = 3
        ARt = [sb("AR%d" % i, [64, NH, 2, 128], BF16) for i in range(NBUF)]
        Btt = [sb("Bt%d" % i, [64, NH, 128], BF16) for i in range(NBUF)]
        Ktt = [sb("Kt%d" % i, [64, NH, 128], BF16) for i in range(NBUF)]
        Vtt = [sb("Vt%d" % i, [64, NH, 128], BF16) for i in range(NBUF)]
        RKt = [sb("RK%d" % i, [64, NH, 128], BF16) for i in range(NBUF)]
        Gtm = [sb("Gtm%d" % i, [128, 512], BF16) for i in range(NBUF)]
        inb = [Buf() for _ in range(NBUF)]
        PCs = sb("PCs", [64, NH, NCH]); PCb = Buf()
        P2 = 2
        TM = [[sb("TM%d_%d" % (h, i), [128, 256], BF16) for i in range(P2)] for h in range(NH)]
        rks = [[sb("rks%d_%d" % (h, i), [128, 2]) for i in range(P2)] for h in range(NH)]
        M12 = [[sb("M12%d_%d" % (h, i), [128, 512], BF16) for i in range(P2)] for h in range(NH)]
        F32R = mybir.dt.float32r
        L0 = [[sb("L0%d_%d" % (h, i), [128, 128], F32R) for i in range(P2)] for h in range(NH)]
        LTr = [[sb("LTr%d_%d" % (h, i), [128, 128], F32R) for i in range(P2)] for h in range(NH)]
        Lpw = [[sb("Lpw%d_%d" % (h, i), [128, 256], F32R) for i in range(2)] for h in range(NH)]
        Xf = [[sb("Xr%d_%d" % (h, i), [128, 128], F32R) for i in range(P2)] for h in range(NH)]
        Xb = [[sb("Xb%d_%d" % (h, i), [128, 128], BF16) for i in range(P2)] for h in range(NH)]
        Gs = [[sb("Gs%d_%d" % (h, i), [64, 64], BF16) for i in range(P2)] for h in range(NH)]
        RhT = [[sb("RhT%d_%d" % (h, i), [64, 128], BF16) for i in range(P2)] for h in range(NH)]
        hb = [[{k: Buf() for k in ("TM", "rks", "M12", "L0", "X", "Xb", "G", "Rh")} for i in range(P2)] for h in range(NH)]
        Lpb = [[Buf() for i in range(2)] for h in range(NH)]
        Sf = [sb("Sf%d" % h, [64, 64]) for h in range(NH)]
        Sb_ = [sb("Sb%d" % h, [64, 64], BF16) for h in range(NH)]
        St = [sb("St%d" % h, [64, 64]) for h in range(NH)]
        Sfb = [Buf() for h in range(NH)]; Sbb = [Buf() for h in range(NH)]; Stb = [Buf() for h in range(NH)]
        Ytm = [sb("Ytm%d" % i, [128, 512]) for i in range(2)]; Ytmb = [Buf(), Buf()]
        ysq = sb("ysq", [128, 512]); ysqb = Buf()
        st = sb("gnst", [128, 5, 8]); stb = Buf()
        yto = Rot([sb("yto%d" % i, [128, 512], BF16) for i in range(2)])
        gi = 0
        for b in range(NB):
            rec.dma("sp", PCs[:], PCd[:, :, b * NCH:(b + 1) * NCH], writes=[PCb])
            for h in range(NH):
                rec.op("pool", lambda e, h=h: e.memset(Sf[h][:], 0.0), writes=[Sfb[h]])
                rec.op("pool", lambda e, h=h: e.memset(Sb_[h][:], 0.0), writes=[Sbb[h]])
            ctx = {}
            ctx2 = {}

            def load(c):
                gidx = b * NCH + c
                bi = gidx % NBUF
                tk = slice(b * T + c * 128, b * T + (c + 1) * 128)
                ib = inb[bi]
                AR, Bt, Kt, Vt, RK, Gt = ARt[bi], Btt[bi], Ktt[bi], Vtt[bi], RKt[bi], Gtm[bi]
                rec.dma("sp", AR[:, :, 0, :], src["AT"][:, :, tk], writes=[ib])
                rec.dma(os.environ.get("RW_DQ", "pool"), AR[:, :, 1, :], src["RTt"][:, :, tk], writes=[ib])
                rec.dma("sp", Bt[:], src["BT"][:, :, tk], writes=[ib])
                rec.dma(os.environ.get("RW_DQ", "pool"), Kt[:], src["KTt"][:, :, tk], writes=[ib])
                rec.dma("sp", Vt[:], src["VT"][:, :, tk], writes=[ib])
                rec.dma(os.environ.get("RW_DQ", "pool"), RK[:], src["RKT"][:, :, tk], writes=[ib])
                rec.dma("sp", Gt[:], dr["G_tm"][tk, :], writes=[ib])

            def front(c):
                gidx = b * NCH + c
                bi = gidx % NBUF
                pi = gidx % P2
                tk = slice(b * T + c * 128, b * T + (c + 1) * 128)
                ib = inb[bi]
                AR, Bt, Kt, Vt, RK, Gt = ARt[bi], Btt[bi], Ktt[bi], Vtt[bi], RKt[bi], Gtm[bi]
                Y = Ytm[pi]; Yb = Ytmb[pi]
                idb = g.ident_bf
                for h in range(NHL if LIM >= 1 else 0):
                    B = hb[h][pi]
                    pA, pAb = next_ps(g)
                    MM(rec, pA[:, 0:64], Bt[:, h, :], idb[0:64, 0:64], True, True, [ib, cb], [pAb])
                    MM(rec, pA[:, 64:128], Kt[:, h, :], idb[0:64, 0:64], True, True, [ib, cb], [pAb])
                    MM(rec, pA[:, 128:192], Vt[:, h, :], idb[0:64, 0:64], True, True, [ib, cb], [pAb])
                    MM(rec, pA[:, 192:256], AR[:, h, 0, :], idb[0:64, 0:64], True, True, [ib, cb], [pAb])
                    MM(rec, pA[:, 256:320], RK[:, h, :], g.ones_bf[0:64, 0:64], True, True, [ib, cb], [pAb])
                    CP(rec, "act", TM[h][pi][:], pA[:, 0:256], [pAb], [B["TM"]])
                    CP(rec, "act", rks[h][pi][:], pA[:, 256:258], [pAb], [B["rks"]])
                    pL, pLb = next_ps(g)
                    MM(rec, pL[:, 0:128], AR[:, h, 0, :], Bt[:, h, :], True, True, [ib], [pLb])
                    TT(rec, "dve", L0[h][pi][:], pL[:, 0:128], mlow[:], ALU.mult, [pLb, mb], [B["L0"]])
                    if LIM < 2:
                        continue
                    pB, pBb = next_ps(g)
                    arh = AR[:, h, :, :].rearrange("k a t -> k (a t)")
                    MM(rec, pB[:, 0:256], Bt[:, h, :], arh, True, True, [ib], [pBb])
                    MM(rec, pB[:, 256:512], Kt[:, h, :], arh, True, True, [ib], [pBb])
                    TT(rec, "dve", M12[h][pi][:], pB[:, :], masks[:], ALU.mult, [pBb, mb], [B["M12"]])
                    TT(rec, "dve", LTr[h][pi][:], pB[:, 0:128], masks[:, 0:128], ALU.mult, [pBb, mb], [B["L0"]])
                for h in range(NH if LIM >= 3 else 0):
                    B = hb[h][pi]
                    p3, p3b = next_ps(g)
                    MM(rec, p3[:, 0:64], M12[h][pi][:, 256:384], TM[h][pi][:, 128:192], True, True, [B["M12"], B["TM"]], [p3b])
                    CP(rec, "dve", Xf[h][pi][:, 64:128], p3[:, 0:64], [p3b], [B["X"]])
                    CP(rec, "pool", Xf[h][pi][:, 0:64], TM[h][pi][:, 192:256], [B["TM"]], [B["X"]])
                for i in range(7 if LIM >= 4 else 0):
                    for h in range(NH):
                        B = hb[h][pi]
                        if i == 0:
                            LT_ap, L_ap, lreads = LTr[h][pi][:], L0[h][pi][:], [B["L0"]]
                        else:
                            cur = Lpw[h][(i - 1) % 2]
                            L_ap, LT_ap, lreads = cur[:, 0:128], cur[:, 128:256], [Lpb[h][(i - 1) % 2]]
                        px, pxb = next_ps(g)
                        MM(rec, px[:, 0:128], LT_ap, Xf[h][pi][:], True, True, lreads + [B["X"]], [pxb])
                        if i < 6:
                            pc, pcb = next_ps(g)
                            MM(rec, pc[:, 0:128], LT_ap, L_ap, True, True, lreads, [pcb])
                            MM(rec, pc[:, 128:256], L_ap, LT_ap, True, True, lreads, [pcb])
                            CP(rec, "act", Lpw[h][i % 2][:], pc[:, 0:256], [pcb], [Lpb[h][i % 2]])
                        TT(rec, "dve", Xf[h][pi][:], Xf[h][pi][:].bitcast(F32), px[:, 0:128], ALU.add, [B["X"], pxb], [B["X"]])
                        if i == 6:
                            CP(rec, "pool", Xb[h][pi][:], Xf[h][pi][:].bitcast(F32), [B["X"]], [B["Xb"]])
                for h in range(NH if LIM >= 5 else 0):
                    B = hb[h][pi]
                    p5, p5b = next_ps(g)
                    MM(rec, p5[0:64, 0:64], Xb[h][pi][:, 0:64], TM[h][pi][:, 0:64], True, True, [B["Xb"], B["TM"]], [p5b])
                    CP(rec, "act", Gs[h][pi][:], p5[0:64, 0:64], [p5b], [B["G"]])
                    p5r, p5rb = next_ps(g)
                    MM(rec, p5r[0:64, 0:128], Xb[h][pi][:, 0:64], M12[h][pi][:, 128:256], True, True, [B["Xb"], B["M12"]], [p5rb])
                    TT(rec, "dve", RhT[h][pi][:], p5r[0:64, 0:128], AR[:, h, 1, :], ALU.add, [p5rb, ib], [B["Rh"]])
                ctx[c] = (bi, pi, tk, ib, AR, Gt, Y, Yb)

            def back(c):
                bi, pi, tk, ib, AR, Gt, Y, Yb = ctx.pop(c)
                for h in range(NH if LIM >= 6 else 0):
                    B = hb[h][pi]
                    p6, p6b = next_ps(g)
                    U = Xb[h][pi][:, 64:128]
                    Vm = TM[h][pi][:, 128:192]
                    MM(rec, p6[:, 0:64], M12[h][pi][:, 128:256], U, True, False, [B["M12"], B["Xb"]], [p6b])
                    MM(rec, p6[:, 0:64], M12[h][pi][:, 384:512], Vm, False, False, [B["M12"], B["TM"]], [p6b])
                    MM(rec, p6[:, 0:64], RhT[h][pi][:], Sb_[h][:], False, True, [B["Rh"], Sbb[h]], [p6b])
                    p7, p7b = next_ps(g)
                    MM(rec, p7[0:64, 0:64], TM[h][pi][:, 0:64], U, True, False, [B["TM"], B["Xb"]], [p7b])
                    MM(rec, p7[0:64, 0:64], TM[h][pi][:, 64:128], Vm, False, False, [B["TM"]], [p7b])
                    MM(rec, p7[0:64, 0:64], Gs[h][pi][:], Sb_[h][:], False, True, [B["G"], Sbb[h]], [p7b])
                    CP(rec, "act", Y[:, h * 64:(h + 1) * 64], p6[:, 0:64], [p6b], [Yb])
                    TT(rec, "dve", St[h][:], Sf[h][:], p7[0:64, 0:64], ALU.add, [Sfb[h], p7b], [Stb[h]])
                    TS(rec, "pool", Sf[h][:], St[h][:], PCs[:, h, c:c + 1], ALU.mult, [Stb[h], PCb], [Sfb[h]])
                    ACTF(rec, Sb_[h][:], St[h][:], AF.Identity, [Stb[h], PCb], [Sbb[h]], scale=PCs[:, h, c:c + 1])
                if LIM < 7:
                    return
                Y3 = Y[:].rearrange("p (h v) -> p h v", v=64)
                rec.op("dve", lambda e, Y3=Y3: e.tensor_reduce(out=st[:, 0, :], in_=Y3, op=ALU.add, axis=AX.X), reads=[Yb], writes=[stb])
                ACTF(rec, ysq[:], Y[:], AF.Square, [Yb], [ysqb])
                rec.op("dve", lambda e: e.tensor_reduce(out=st[:, 1, :], in_=ysq[:].rearrange("p (h v) -> p h v", v=64), op=ALU.add, axis=AX.X),
                       reads=[ysqb], writes=[stb])
                TS(rec, "dve", st[:, 2, :], st[:, 0, :], 1.0 / 64, ALU.mult, [stb], [stb])
                TT(rec, "dve", st[:, 3, :], st[:, 2, :], st[:, 2, :], ALU.mult, [stb], [stb])
                STT(rec, st[:, 3, :], st[:, 1, :], 1.0 / 64, st[:, 3, :], ALU.mult, ALU.subtract, [stb], [stb])
                ACTF(rec, st[:, 3, :], st[:, 3, :], AF.Sqrt, [stb, lnbuf], [stb], bias=eps2[:])
                rec.op("dve", lambda e: e.reciprocal(out=st[:, 4, :], in_=st[:, 3, :]), reads=[stb], writes=[stb])
                for h in range(NH):
                    TS(rec, "dve" if h % 2 else "pool", Y[:, h * 64:(h + 1) * 64], Y[:, h * 64:(h + 1) * 64], st[:, 2, h:h + 1], ALU.subtract,
                       [Yb, stb], [Yb], s2=st[:, 4, h:h + 1], op1=ALU.mult)
                TT(rec, "pool", Y[:], Y[:], lnw[:], ALU.mult, [Yb, lnbuf], [Yb])
                TT(rec, "pool", Y[:], Y[:], lnb[:], ALU.add, [Yb, lnbuf], [Yb])
                for h in range(NH):
                    STT(rec, Y[:, h * 64:(h + 1) * 64], TM[h][pi][:, 128:192], rks[h][pi][:, 0:1], Y[:, h * 64:(h + 1) * 64],
                        ALU.mult, ALU.add, [hb[h][pi]["TM"], hb[h][pi]["rks"], Yb], [Yb])
                TT(rec, "dve", Y[:], Y[:], Gt[:], ALU.mult, [Yb, ib], [Yb])
                ctx2[c] = (tk, Y, Yb)

            def back_b(c):
                tk, Y, Yb = ctx2.pop(c)
                pT, pTb = next_ps(g)
                for cc in range(4):
                    rec.op("pe", lambda e, pT=pT, cc=cc, Y=Y: e.transpose(pT[:, cc * 128:(cc + 1) * 128], Y[:, cc * 128:(cc + 1) * 128], g.ident[:]),
                           reads=[Yb, cb], writes=[pTb])
                yo, yob = yto.next()
                CP(rec, "act", yo[:], pT[:, :], [pTb], [yob])
                rec.dma("sp", YT[:, 4:8, tk], yo[:].rearrange("p (c t) -> p c t", t=128), reads=[yob])

            for c in range(NCH + 2):
                if c == 0:
                    load(0)
                if c + 1 < NCH:
                    load(c + 1)
                if c < NCH:
                    front(c)
                if 0 <= c - 2 < NCH:
                    back_b(c - 2)
                if 0 <= c - 1 < NCH:
                    back(c - 1)
        rec.emit()


def phase_pre1(nc, rec, g, dr, NTOK, T, N=256):
    l = 1
    XT = dr["XT"].rearrange("(k p) t -> p k t", p=128)
    fo = {k: dr[k].rearrange("(h p) t -> p h t", p=128) for k in ("QTl", "KTl", "KHl", "SGT")}
    PCh = dr["PCh"].rearrange("(h p) n -> p h n", p=128)
    ntile = NTOK // N
    NS = N // 128
    NC32 = 8 * N // 32
    with ExitStack() as es:
        sb = lambda n, s, d=F32: es.enter_context(nc.sbuf_tensor(uniq(n), s, d))
        W = sb("W1", [128, KC, 4096], BF16); Wb = Buf()
        st_rot = Rot([sb("wst%d" % i, [128, 1024]) for i in range(2)])
        jobs = []
        for kk in range(KC):
            for q4 in range(4):
                cs = slice(q4 * 1024, (q4 + 1) * 1024)
                jobs.append((dr["w_in_odd"][0, kk * 128:(kk + 1) * 128, cs], W[:, kk, cs], Wb))
        load_cast(rec, st_rot, jobs)
        lbl = sb("lbl", [128, 2, 8]); lbb = Buf()
        lb = sb("lb", [128, 8]); oml = sb("oml", [128, 8])
        rec.dma("sp", lbl[:], dr["hg_lbT"][:, :, :], writes=[lbb])
        TT(rec, "dve", lb[:], lbl[:, 1, :], lbl[:, 0, :], ALU.subtract, [lbb], [lbb])
        ACTF(rec, lb[:], lb[:], AF.Sigmoid, [lbb], [lbb])
        TS(rec, "dve", oml[:], lb[:], -1.0, ALU.mult, [lbb], [lbb], s2=1.0, op1=ALU.add)
        m32 = sb("m32", [128, 8 * N])
        rec.op("pool", lambda e: e.memset(m32[:], 1.0), writes=[lbb])
        rec.op("pool", lambda e: e.memset(m32[:].rearrange("p (c t) -> p c t", t=32)[:, :, 0:1], 0.0), writes=[lbb])
        x = sb("x", [128, KC, N]); xb = Buf()
        sq = sb("sq", [128, KC, N], BF16); sqb = Buf()
        hT = sb("hT", [128, KC, N], BF16); hb = Buf()
        tr = Rot([sb("tmp%d" % i, [128, N]) for i in range(3)])
        rstd = sb("rstd", [128, N]); rstdb = Buf()
        raw = sb("raw", [128, 24, N]); rawb = Buf()
        lf = sb("lf", [128, 8 * N]); lfb = Buf()
        bt = sb("bt", [128, 8 * N]); btb = Buf()
        ept = sb("ept", [128, 8 * N]); epb = Buf()
        ent = sb("ent", [128, 8 * N]); enb = Buf()
        ect = sb("ect", [128, 8 * N]); ecb = Buf()
        pct = sb("pct", [128, NC32]); pcb_ = Buf()
        ob = {k: (sb("o_" + k, [128, 8, N], BF16), Buf()) for k in fo}
        ito = Rot([sb("ito%d" % i, [128, 1024], BF16) for i in range(2)])
        for ti in range(ntile):
            b = (ti * N) // T
            t0 = ti * N
            rec.dma("sp", x[:], XT[:, :, t0:t0 + N], writes=[xb])
            t1, t1b = tr.next()
            rms_stats(rec, g, x[:], xb, KC, N, sq, sqb, t1, t1b, rstd, rstdb, D)
            for m in range(KC):
                t1, t1b = tr.next()
                TT(rec, "dve", t1[:], x[:, m, :], rstd[:], ALU.mult, [xb, rstdb], [t1b])
                ACTF(rec, hT[:, m, :], t1[:], AF.Identity, [t1b, g.Amod_b, g.modT_b], [hb],
                     scale=g.Amod[:, l, 0, m, b:b + 1], bias=g.modT[:, l, 0 + m, b:b + 1])
            for j in range(24):
                grp, h = j // 8, j % 8
                c0 = (0, 1024, 3072)[grp] + h * 128
                ps, psb = next_ps(g)
                for kk in range(KC):
                    MM(rec, ps[:, 0:N], W[:, kk, c0:c0 + 128], hT[:, kk, :], kk == 0, kk == KC - 1, [Wb, hb], [psb])
                evac(rec, j, raw[:, j, :], ps[:, 0:N], [psb], [rawb])
            rq = raw[:, 0:8, :].rearrange("p h t -> p (h t)")
            rf = raw[:, 8:16, :].rearrange("p h t -> p (h t)")
            rg = raw[:, 16:24, :].rearrange("p h t -> p (h t)")
            ACTF(rec, rq, rq, AF.Silu, [rawb], [rawb])
            o, obb = ob["SGT"]
            ACTF(rec, o[:].rearrange("p h t -> p (h t)"), rg, AF.Silu, [rawb], [obb])
            ACTF(rec, rf, rf, AF.Sigmoid, [rawb], [rawb])
            for h in range(8):
                TS(rec, "pool" if h % 2 else "dve", raw[:, 8 + h, :], raw[:, 8 + h, :], oml[:, h:h + 1], ALU.mult, [rawb, lbb], [rawb],
                   s2=lb[:, h:h + 1], op1=ALU.add)
            ACTF(rec, lf[:], rf, AF.Ln, [rawb], [lfb])
            TS(rec, "pool", rf, rf, -1.0, ALU.mult, [rawb], [rawb], s2=1.0, op1=ALU.add)
            rec.op("dve", lambda e: e.tensor_tensor_scan(out=bt[:], data0=m32[:], data1=lf[:], initial=0.0, op0=ALU.mult, op1=ALU.add),
                   reads=[lfb, lbb], writes=[btb])
            b3 = bt[:].rearrange("p (c t) -> p c t", t=32)
            ACTF(rec, ept[:], bt[:], AF.Exp, [btb], [epb])
            ACTF(rec, ent[:], bt[:], AF.Exp, [btb], [enb], scale=-1.0)
            TT(rec, "dve", ect[:].rearrange("p (c t) -> p c t", t=32), b3[:, :, 31:32].broadcast_to([128, NC32, 32]), b3, ALU.subtract,
               [btb], [ecb])
            ACTF(rec, ect[:], ect[:], AF.Exp, [ecb], [ecb])
            ACTF(rec, pct[:], b3[:, :, 31], AF.Exp, [btb], [pcb_])
            o, obb = ob["QTl"]
            TT(rec, "dve", o[:].rearrange("p h t -> p (h t)"), rq, ept[:], ALU.mult, [rawb, epb], [obb])
            o, obb = ob["KTl"]
            TT(rec, "pool", o[:].rearrange("p h t -> p (h t)"), rf, ent[:], ALU.mult, [rawb, enb], [obb])
            o, obb = ob["KHl"]
            TT(rec, "dve", o[:].rearrange("p h t -> p (h t)"), rf, ect[:], ALU.mult, [rawb, ecb], [obb])
            for i, k in enumerate(fo):
                o, obb = ob[k]
                rec.dma("sp" if i % 2 == 0 else "pool", fo[k][:, :, t0:t0 + N], o[:], reads=[obb])
            rec.dma("sp", PCh[:, :, t0 // 32:(t0 + N) // 32], pct[:].rearrange("p (h c) -> p h c", h=8), reads=[pcb_])
            for s in range(NS):
                io, iob = ito.next()
                for n in range(2):
                    ps, psb = next_ps(g)
                    for kk in range(KC):
                        MM(rec, ps[:, :], hT[:, kk, s * 128:(s + 1) * 128], W[:, kk, 2048 + n * 512:2048 + (n + 1) * 512],
                           kk == 0, kk == KC - 1, [Wb, hb], [psb])
                    evac(rec, n, io[:, n * 512:(n + 1) * 512], ps[:, :], [psb], [iob])
                rec.dma("sp", dr["I_tm"][t0 + s * 128:t0 + (s + 1) * 128, :], io[:], reads=[iob])
        rec.emit()


def phase_hgrn(nc, rec, g, dr, NB, T, N=512):
    NG = N // 128
    fi = {k: dr[k].rearrange("(h p) t -> p h t", p=128) for k in ("QTl", "KTl", "KHl", "SGT")}
    PCh = dr["PCh"].rearrange("(h p) n -> p h n", p=128)
    YT = dr.get("YT1", dr["YT"]).rearrange("(h p) t -> p h t", p=128)
    NH = 8
    import os
    HL = int(os.environ.get("HG_LIM", "9"))
    with ExitStack() as es:
        sb = lambda n, s, d=F32: es.enter_context(nc.sbuf_tensor(uniq(n), s, d))
        cb = g.const_b
        bmask = sb("bmask", [128, 128]); mb = Buf()
        rec.dma("sp", bmask[:], dr["hmask"][:, :], writes=[mb])
        onorm = sb("onorm", [128, 1])
        rec.dma("sp", onorm[:], dr["hg_onT"][:, :], writes=[mb])
        NBUF = 2
        ft = {k: [sb("f_%s%d" % (k, i), [128, NH, N], BF16) for i in range(NBUF)] for k in fi}
        itm = [sb("itm%d" % i, [128, NG, 1024], BF16) for i in range(NBUF)]
        pcs = [sb("pcs%d" % i, [128, NH, N // 32]) for i in range(NBUF)]
        inb = [Buf() for _ in range(NBUF)]
        yt = [sb("yt%d" % i, [128, NH, N], BF16) for i in range(2)]; ytb = [Buf(), Buf()]
        Sf = [sb("Sf%d" % h, [128, 128]) for h in range(NH)]
        Sb_ = [sb("Sbf%d" % h, [128, 128], BF16) for h in range(NH)]
        Sfb = [Buf() for h in range(NH)]; Sbb = [Buf() for h in range(NH)]
        khm = [[sb("khm%d_%d" % (h, i), [128, 128], BF16) for i in range(2)] for h in range(NH)]
        attm = [[sb("attm%d_%d" % (h, i), [128, 128], BF16) for i in range(2)] for h in range(NH)]
        khb = [[Buf() for i in range(2)] for h in range(NH)]
        atb = [[Buf() for i in range(2)] for h in range(NH)]
        oT = [sb("oT%d" % h, [128, 128]) for h in range(NH)]; oTb = [Buf() for h in range(NH)]
        osq = [sb("osq%d" % h, [128, 128], BF16) for h in range(NH)]; osqb = [Buf() for h in range(NH)]
        sd = [sb("sd%d" % h, [128, 128]) for h in range(NH)]; sdb = [Buf() for h in range(NH)]
        gi = 0
        ti_glob = 0
        pending = []
        for b in range(NB):
            for h in range(NH):
                rec.op("pool", lambda e, h=h: e.memset(Sf[h][:], 0.0), writes=[Sfb[h]])
                rec.op("pool", lambda e, h=h: e.memset(Sb_[h][:], 0.0), writes=[Sbb[h]])
            for tt in range(T // N):
                bi = ti_glob % NBUF
                ti_glob += 1
                t0 = b * T + tt * N
                ib = inb[bi]
                for i, k in enumerate(fi):
                    rec.dma("sp" if i % 2 == 0 else "pool", ft[k][bi][:], fi[k][:, :, t0:t0 + N], writes=[ib])
                rec.dma("sp", itm[bi][:], dr["I_tm"][t0:t0 + N, :].rearrange("(g p) d -> p g d", p=128), writes=[ib])
                rec.dma("sp", pcs[bi][:], PCh[:, :, t0 // 32:(t0 + N) // 32], writes=[ib])
                Q, K, KH, SG, IT, PC = ft["QTl"][bi], ft["KTl"][bi], ft["KHl"][bi], ft["SGT"][bi], itm[bi], pcs[bi]
                Yt, Ytb = yt[bi], ytb[bi]
                for gq in range(NG):
                    pi = gi % 2
                    gi += 1
                    gs = slice(gq * 128, (gq + 1) * 128)
                    for h in range(NH):
                        pK, pKb = next_ps(g)
                        MM(rec, pK[:, 0:128], KH[:, h, gs], g.ident_bf[:], True, True, [ib, cb], [pKb])
                        CP(rec, "act", khm[h][pi][:], pK[:, 0:128], [pKb], [khb[h][pi]])
                        pA, pAb = next_ps(g)
                        MM(rec, pA[:, 0:128], K[:, h, gs], Q[:, h, gs], True, True, [ib], [pAb])
                        TT(rec, "dve", attm[h][pi][:], pA[:, 0:128], bmask[:], ALU.mult, [pAb, mb], [atb[h][pi]])
                    for f_ in pending:
                        f_()
                    pending.clear()
                    for half in range(2 if HL >= 2 else 0):
                        hs = range(half * 4, half * 4 + 4)
                        pS = {}; pO = {}
                        pSj = [next_ps(g) for j in range(4)]
                        for h in hs:
                            hl = h - half * 4
                            for j in range(4):
                                r0, r1 = (32 * j, 32 * j + 32) if j < 3 else (64, 128)
                                MM(rec, pSj[j][0][:, hl * 128:(hl + 1) * 128], khm[h][pi][r0:r1, :],
                                   IT[r0:r1, gq, h * 128:(h + 1) * 128], True, True, [khb[h][pi], ib], [pSj[j][1]])
                        if HL < 3:
                            continue
                        for h in hs:
                            pO[h] = next_ps(g)
                            MM(rec, pO[h][0][:, 0:128], IT[:, gq, h * 128:(h + 1) * 128], attm[h][pi][:], True, False, [ib, atb[h][pi]], [pO[h][1]])
                        for j in range(4 if HL >= 4 else 0):
                            for h in hs:
                                c0 = gq * 128 + 32 * j
                                MM(rec, pO[h][0][:, 32 * j:32 * j + 32], Sb_[h][:], Q[:, h, c0:c0 + 32], False, j == 3, [Sbb[h], ib], [pO[h][1]])
                                ch = gq * 4 + j
                                hl = h - half * 4
                                STT(rec, Sf[h][:], Sf[h][:], PC[:, h, ch:ch + 1], pSj[j][0][:, hl * 128:(hl + 1) * 128], ALU.mult, ALU.add,
                                    [Sfb[h], ib, pSj[j][1]], [Sfb[h]])
                                if j == 3:
                                    TT(rec, "dve", Sf[h][:], Sf[h][:], pSj[2][0][:, hl * 128:(hl + 1) * 128], ALU.subtract, [Sfb[h], pSj[2][1]], [Sfb[h]])
                                CP(rec, "act", Sb_[h][:], Sf[h][:], [Sfb[h]], [Sbb[h]])
                        for h in (hs if HL >= 5 else []):
                            CP(rec, "act", oT[h][:], pO[h][0][:, 0:128], [pO[h][1]], [oTb[h]])
                            ACTF(rec, osq[h][:], oT[h][:], AF.Square, [oTb[h]], [osqb[h]])

                            def fin(h=h, gs=gs, SG=SG, Yt=Yt, Ytb=Ytb, ib=ib):
                                pZ, pZb = next_ps(g)
                                MM(rec, pZ[:, 0:128], g.ones_bf[:], osq[h][:], True, True, [cb, osqb[h]], [pZb])
                                ACTF(rec, sd[h][:], pZ[:, 0:128], AF.Ln, [pZb, cb], [sdb[h]], scale=1.0 / 128, bias=g.epsb[:])
                                ACTF(rec, sd[h][:], sd[h][:], AF.Exp, [sdb[h]], [sdb[h]], scale=-0.5)
                                STT(rec, oT[h][:], oT[h][:], onorm[:, 0:1], sd[h][:], ALU.mult, ALU.mult, [oTb[h], sdb[h], mb], [oTb[h]])
                                TT(rec, "dve", Yt[:, h, gs], oT[h][:], SG[:, h, gs], ALU.mult, [oTb[h], ib], [Ytb])
                            pending.append(fin)
                for f_ in pending:
                    f_()
                pending.clear()
                rec.dma("sp", YT[:, :, t0:t0 + N], Yt[:], reads=[Ytb])
        rec.emit()


SCRATCH = None


def build_program(NB, T, ext_scratch=False):
    NTOK = NB * T
    nc = bass.Bass("TRN2", target_bir_lowering=False)
    dr = {}
    shapes = input_shapes(NB, T)
    for k, (shp, dt) in shapes.items():
        dr[k] = nc.dram_tensor(k, list(shp), dt, kind="ExternalInput").ap()
    dr["out"] = nc.dram_tensor("out", [NTOK, 1024], F32, kind="ExternalOutput").ap()
    kind = "ExternalOutput" if ext_scratch else "Internal"

    def scr(name, shape, dt):
        dr[name] = nc.dram_tensor(name, shape, dt, kind=kind).ap()
    scr("XT", [1024, NTOK], F32)
    scr("RP", [1792, NTOK], F32)
    scr("QT", [8, 96, NTOK], BF16)
    scr("KT", [8, 96, NTOK], BF16)
    scr("V", [NTOK, 512], BF16)
    scr("YT", [1024, NTOK], BF16)
    for k in ("AT", "BT", "KTt", "RTt", "VT", "RKT"):
        scr(k, [512, NTOK], BF16)
    scr("G_tm", [NTOK, 512], BF16)
    scr("PC", [512, NTOK // 128], F32)
    for k in ("QTl", "KTl", "KHl", "SGT"):
        scr(k, [1024, NTOK], BF16)
    scr("I_tm", [NTOK, 1024], BF16)
    scr("PCh", [1024, NTOK // 32], F32)
    if ext_scratch:
        scr("YT1", [1024, NTOK], BF16)
    g = G()
    with ExitStack() as es:
        rec = Rec(nc, es)
        setup_globals(nc, es, g, NB)
        phase_mods(nc, rec, g, dr)
        phase_pre0(nc, rec, g, dr, NTOK, T)
        run_gens(rec, [phase_mla_g(nc, rec, g, dr, NB, T, nptr=1), phase_rwkv_prep_g(nc, rec, g, dr, NTOK, T)])
        phase_rwkv_main(nc, rec, g, dr, NB, T)
        phase_post(nc, rec, g, dr, 0, NTOK, T, last=False)
        phase_pre1(nc, rec, g, dr, NTOK, T)
        phase_hgrn(nc, rec, g, dr, NB, T)
        phase_post(nc, rec, g, dr, 1, NTOK, T, last=True)
    return nc, rec


def input_shapes(NB, T):
    NTOK = NB * T
    f = F32
    return {
        "ident": ((128, 128), f), "cmask": ((128, 128), f), "x": ((NTOK, 1024), f), "pos": ((NB, T), I32),
        "cT": ((128, 8, NB), f), "ada_bT": ((128, 2, 48), f), "norm_mixT": ((128, 2, 8), f), "norm_ffnT": ((128, 2, 8), f),
        "final_normT": ((128, 8), f), "ada_w": ((2, 1024, 6144), f), "w_out_even": ((1, 1024, 1024), f),
        "w_out_odd": ((1, 1024, 1024), f), "ffn_w_gate": ((2, 1024, 2816), f), "ffn_w_up": ((2, 1024, 2816), f),
        "ffn_w_down": ((2, 2816, 1024), f), "w_in_even": ((1, 1024, 2464), f), "mla_w_uq": ((1, 384, 768), f),
        "w_in_odd": ((1, 1024, 4096), f), "w_in_sw": ((1024, 32), f), "w_uq_sw": ((384, 768), f), "w_ukv_r": ((256, 1024), f),
        "ctab": ((128, 8), f), "rmasks": ((128, 384), f), "rwkv_ln_w": ((1, 512), f), "rwkv_ln_b": ((1, 512), f),
        "hg_lbT": ((128, 2, 8), f), "hg_onT": ((128, 1), f), "hmask": ((128, 128), f), "rw_mu": ((128, 14), f),
        "rw_par": ((128, 20), f), "rwkv_w2": ((1, 64, 512), f), "rwkv_a2": ((1, 64, 512), f), "rwkv_g2": ((1, 128, 512), f),
    }


_CACHE = {}


def kernel(**inputs):
    from concourse.bass_utils import run_bass_kernel_spmd
    NB, T, NCORE = 4, 2048, 8
    inputs = {k: np.asarray(v) for k, v in inputs.items()}
    if "nc" not in _CACHE:
        _CACHE["nc"] = build_program(NB, T)[0]
    nc = _CACHE["nc"]
    shared = None
    in_maps = []
    for i in range(NCORE):
        sl = slice(i * NB, (i + 1) * NB)
        inp = dict(inputs)
        inp["x"] = inputs["x"][sl]
        inp["c"] = inputs["c"][sl]
        inp["positions"] = inputs["positions"][sl]
        if shared is None:
            shared = host_layout(inp)
            d = dict(shared)
        else:
            d = dict(shared)
            d["x"] = np.ascontiguousarray(inp["x"].reshape(-1, 1024))
            d["pos"] = np.ascontiguousarray(inp["positions"].astype(np.int32))
            d["cT"] = np.ascontiguousarray(inp["c"].T.reshape(8, 128, NB).transpose(1, 0, 2))
        in_maps.append(d)
    res = run_bass_kernel_spmd(nc, in_maps, core_ids=list(range(NCORE)))
    outs = [np.asarray(r["out"]).reshape(NB, T, 1024) for r in res.results]
    return np.concatenate(outs, axis=0).astype(np.float32)
```

```python
import numpy as np
import concourse.bass as bass
import concourse.mybir as mybir

F32 = mybir.dt.float32
BF16 = mybir.dt.bfloat16
I32 = mybir.dt.int32
AF = mybir.ActivationFunctionType
ALU = mybir.AluOpType
AX = mybir.AxisListType

ENGS = ("pe", "dve", "act", "pool", "sp")
NSLOT = 8


class Buf:
    __slots__ = ("name", "w", "r")

    def __init__(self, name=""):
        self.name = name
        self.w = None
        self.r = {}


class Rec:
    def __init__(self, nc, es, same_eng_sync=True):
        self.nc = nc
        self.sem = {}
        for e in ENGS:
            self.sem[e] = es.enter_context(nc.semaphore("s_" + e))
        self.cnt = {e: 0 for e in ENGS}
        self.q = {e: [] for e in ENGS}
        self.seen = {e: {} for e in ENGS}
        self.slots = {}
        for e in ("sp", "act", "pool"):
            self.slots[e] = [[es.enter_context(nc.semaphore("d_%s%d" % (e, i))), 0] for i in range(NSLOT)]
        self.slot_i = {e: 0 for e in self.slots}
        self.same = same_eng_sync
        self.nops = 0

    def _need(self, eng, ev, waits):
        if ev is None:
            return
        key, val, src = ev
        if src == eng and key[0] == "e":
            if eng == "pe" or not self.same:
                return
        if self.seen[eng].get(key, 0) >= val:
            return
        self.seen[eng][key] = val
        waits.append((key, val))

    def _deps(self, eng, reads, writes):
        waits = []
        for b in reads:
            self._need(eng, b.w, waits)
        for b in writes:
            self._need(eng, b.w, waits)
            for ev in b.r.values():
                self._need(eng, ev, waits)
        return waits

    def _mark(self, ev, reads, writes):
        for b in reads:
            b.r[ev[0]] = ev
        for b in writes:
            b.w = ev
            b.r = {}

    def op(self, eng, fn, reads=(), writes=()):
        waits = self._deps(eng, reads, writes)
        self.cnt[eng] += 1
        ev = (("e", eng), self.cnt[eng], eng)
        self.q[eng].append((waits, fn, ("e", eng), 1))
        self._mark(ev, reads, writes)
        self.nops += 1

    def dma(self, eng, out, in_, reads=(), writes=()):
        sl = self.slots[eng]
        i = self.slot_i[eng]
        self.slot_i[eng] = (i + 1) % NSLOT
        key = ("d", eng, i)
        waits = []
        if sl[i][1] > 0:
            self._need(eng, (key, sl[i][1], "dma"), waits)
        waits += self._deps(eng, reads, writes)
        sl[i][1] += 16
        ev = (key, sl[i][1], "dma")
        self.q[eng].append((waits, lambda e, o=out, s=in_: e.dma_start(out=o, in_=s), key, 16))
        self._mark(ev, reads, writes)
        self.nops += 1

    def _semh(self, key):
        if key[0] == "e":
            return self.sem[key[1]]
        return self.slots[key[1]][key[2]][0]

    def drain_dmas(self):
        for e in self.slots:
            for i, (s, v) in enumerate(self.slots[e]):
                if v > 0:
                    key = ("d", e, i)
                    if self.seen[e].get(key, 0) < v:
                        self.seen[e][key] = v
                        self.q[e].append(([(key, v)], None, None, 0))

    def emit(self):
        self.drain_dmas()
        nc = self.nc
        rec = self

        def run(engname, handle):
            for waits, fn, key, inc in rec.q[engname]:
                for (k, v) in waits:
                    handle.wait_ge(rec._semh(k), v)
                if fn is not None:
                    fn(handle).then_inc(rec._semh(key), inc)
            rec.q[engname] = []

        with nc.Block() as block:
            @block.tensor
            def _(e):
                run("pe", e)

            @block.vector
            def _(e):
                run("dve", e)

            @block.scalar
            def _(e):
                run("act", e)

            @block.gpsimd
            def _(e):
                run("pool", e)

            @block.sync
            def _(e):
                run("sp", e)


from contextlib import ExitStack
import numpy as np
import concourse.bass as bass
import concourse.mybir as mybir

D = 1024
KC = 8
FH = 2816
JH = 22
EPS = 1e-6


class G:
    pass


_uid = [0]


def uniq(n):
    _uid[0] += 1
    return "sb%d_%s" % (_uid[0], n)


def pool_of(n, name):
    return [Buf("%s%d" % (name, i)) for i in range(n)]


class Rot:
    def __init__(self, tiles):
        self.t = tiles
        self.b = [Buf() for _ in tiles]
        self.i = 0

    def next(self):
        i = self.i
        self.i = (i + 1) % len(self.t)
        return self.t[i], self.b[i]


def setup_globals(nc, es, g, NB):
    g.NB = NB
    sb = lambda n, s, d=F32: es.enter_context(nc.sbuf_tensor(uniq(n), s, d))
    g.modT = sb("modT", [128, 2, 48, NB])
    g.modT_b = Buf("modT")
    g.Amod = sb("Amod", [128, 2, 2, KC, NB])
    g.Amod_b = Buf("Amod")
    g.ident = sb("ident", [128, 128])
    g.ident_bf = sb("ident_bf", [128, 128], BF16)
    g.ones_bf = sb("ones_bf", [128, 128], BF16)
    g.const_b = Buf("const")
    g.epsb = sb("epsb", [128, 1])
    g.ps = [es.enter_context(nc.psum_tensor("ps%d" % i, [128, 512], F32)) for i in range(8)]
    g.psb = [Buf("ps%d" % i) for i in range(8)]
    g.ps_i = 0


def next_ps(g):
    i = g.ps_i
    g.ps_i = (i + 1) % 8
    return g.ps[i], g.psb[i]


def phase_mods(nc, rec, g, dr):
    NB = g.NB
    with ExitStack() as es:
        sb = lambda n, s, d=F32: es.enter_context(nc.sbuf_tensor(uniq(n), s, d))
        cT = sb("cT", [128, KC, NB]); cT_b = Buf()
        condT = sb("condT", [128, KC, NB]); condT_b = Buf()
        adab = sb("adab", [128, 2, 48]); adab_b = Buf()
        gains = sb("gains", [128, 2, 2, KC]); gains_b = Buf()
        wst = [sb("adaw%d" % i, [128, KC, 1024]) for i in range(2)]
        wst_b = [Buf(), Buf()]
        rec.dma("sp", g.ident[:], dr["ident"][:, :], writes=[g.const_b])
        rec.op("pool", lambda e: e.memset(g.ones_bf[:], 1.0), writes=[g.const_b])
        rec.op("pool", lambda e: e.memset(g.epsb[:], EPS), writes=[g.const_b])
        rec.op("dve", lambda e: e.tensor_copy(out=g.ident_bf[:], in_=g.ident[:]), reads=[g.const_b], writes=[g.const_b])
        rec.dma("sp", cT[:], dr["cT"][:, :, :], writes=[cT_b])
        rec.dma("sp", adab[:], dr["ada_bT"][:, :, :], writes=[adab_b])
        rec.dma("sp", gains[:, 0], dr["norm_mixT"][:, :, :], writes=[gains_b])
        rec.dma("sp", gains[:, 1], dr["norm_ffnT"][:, :, :], writes=[gains_b])
        rec.op("act", lambda e: e.activation(out=condT[:], in_=cT[:], func=AF.Silu), reads=[cT_b], writes=[condT_b])
        it = 0
        for l in range(2):
            for pc in range(6):
                w, wb = wst[it % 2], wst_b[it % 2]
                it += 1
                src = dr["ada_w"][l, :, pc * 1024:(pc + 1) * 1024].rearrange("(k p) n -> p k n", p=128)
                for kk in range(KC):
                    rec.dma("sp", w[:, kk, :], src[:, kk, :], writes=[wb])
                for jj in range(8):
                    j = pc * 8 + jj
                    ps, psb = next_ps(g)
                    for kk in range(KC):
                        rec.op("pe", lambda e, ps=ps, w=w, kk=kk, jj=jj: e.matmul(
                            ps[:, 0:NB], lhsT=w[:, kk, jj * 128:(jj + 1) * 128], rhs=condT[:, kk, :],
                            start=(kk == 0), stop=(kk == KC - 1)), reads=[wb, condT_b], writes=[psb])
                    rec.op("dve", lambda e, ps=ps, l=l, j=j: e.tensor_scalar(
                        out=g.modT[:, l, j, :], in0=ps[:, 0:NB], scalar1=adab[:, l, j:j + 1], scalar2=None,
                        op0=ALU.add), reads=[psb, adab_b], writes=[g.modT_b])
        for l in range(2):
            for sub in range(2):
                for kk in range(KC):
                    j = (1 + 3 * sub) * KC + kk
                    rec.op("dve", lambda e, l=l, sub=sub, kk=kk, j=j: e.tensor_scalar(
                        out=g.Amod[:, l, sub, kk, :], in0=g.modT[:, l, j, :], scalar1=1.0,
                        scalar2=gains[:, sub, l, kk:kk + 1], op0=ALU.add, op1=ALU.mult),
                        reads=[g.modT_b, gains_b], writes=[g.Amod_b])
        rec.emit()


def load_cast(rec, st_rot, srcs_dsts, engs=("dve", "pool")):
    for i, (src, dst, db) in enumerate(srcs_dsts):
        st, stb = st_rot.next()
        n = src.shape[-1]
        rec.dma("sp", st[:, 0:n], src, writes=[stb])
        eng = engs[i % len(engs)]
        rec.op(eng, lambda e, st=st, dst=dst, n=n: e.tensor_copy(out=dst, in_=st[:, 0:n]), reads=[stb], writes=[db])


def phase_post(nc, rec, g, dr, l, NTOK, T, last, N=256):
    wout = dr["w_out_even"][0] if l == 0 else dr["w_out_odd"][0]
    XT = dr["XT"].rearrange("(k p) t -> p k t", p=128)
    YT = (dr.get("YT1", dr["YT"]) if l == 1 else dr["YT"]).rearrange("(k p) t -> p k t", p=128)
    ntile = NTOK // N
    with ExitStack() as es:
        sb = lambda n, s, d=F32: es.enter_context(nc.sbuf_tensor(uniq(n), s, d))
        Wo = sb("Wo", [128, KC, D], BF16)
        Wg = sb("Wg", [128, KC, FH], BF16)
        Wu = sb("Wu", [128, KC, FH], BF16)
        Wd = sb("Wd", [128, JH, D], BF16)
        Wb = Buf("W")
        st_rot = Rot([sb("wst%d" % i, [128, FH // 4]) for i in range(2)])
        xs = [sb("x%d" % i, [128, KC, N]) for i in range(2)]
        xbs = [Buf(), Buf()]
        hT = sb("hT", [128, KC, N], BF16); hb = Buf()
        sq = sb("sq", [128, KC, N], BF16); sqb = Buf()
        actT = sb("actT", [128, JH, N], BF16); actb = Buf()
        tr = Rot([sb("tmp%d" % i, [128, N]) for i in range(3)])
        rstd = sb("rstd", [128, N]); rstdb = Buf()
        if last:
            fng = sb("fng", [128, KC]); fngb = Buf()
            rec.dma("sp", fng[:], dr["final_normT"][:, :], writes=[fngb])
            fr = Rot([sb("fin%d" % i, [128, D]) for i in range(1)])
        jobs = []
        for kk in range(KC):
            for hh in range(2):
                cs = slice(hh * 512, (hh + 1) * 512)
                jobs.append((wout[kk * 128:(kk + 1) * 128, cs], Wo[:, kk, cs], Wb))
        for kk in range(KC):
            for hh in range(4):
                cs = slice(hh * (FH // 4), (hh + 1) * (FH // 4))
                jobs.append((dr["ffn_w_gate"][l, kk * 128:(kk + 1) * 128, cs], Wg[:, kk, cs], Wb))
                jobs.append((dr["ffn_w_up"][l, kk * 128:(kk + 1) * 128, cs], Wu[:, kk, cs], Wb))
        for j in range(JH):
            for hh in range(2):
                cs = slice(hh * 512, (hh + 1) * 512)
                jobs.append((dr["ffn_w_down"][l, j * 128:(j + 1) * 128, cs], Wd[:, j, cs], Wb))
        load_cast(rec, st_rot, jobs)

        def stats(x, xb):
            rec.op("act", lambda e: e.activation(out=sq[:], in_=x[:], func=AF.Square), reads=[xb], writes=[sqb])
            ps, psb = next_ps(g)
            for kk in range(KC):
                MM(rec, ps[:, 0:N], g.ones_bf[:], sq[:, kk, :], kk == 0, kk == KC - 1, [sqb, g.const_b], [psb])
            t1, t1b = tr.next()
            ACTF(rec, t1[:], ps[:, 0:N], AF.Sqrt, [psb, g.const_b], [t1b], scale=1.0 / D, bias=g.epsb[:])
            rec.op("dve", lambda e: e.reciprocal(out=rstd[:], in_=t1[:]), reads=[t1b], writes=[rstdb])

        def load_x(i):
            rec.dma("sp", xs[i % 2][:], XT[:, :, i * N:(i + 1) * N], writes=[xbs[i % 2]])

        def load_y(i):
            rec.dma("sp", sq[:], YT[:, :, i * N:(i + 1) * N], writes=[sqb])

        def outproj(i):
            x, xb = xs[i % 2], xbs[i % 2]
            b = (i * N) // T
            for m in range(KC):
                ps, psb = next_ps(g)
                for kk in range(KC):
                    MM(rec, ps[:, 0:N], Wo[:, kk, m * 128:(m + 1) * 128], sq[:, kk, :], kk == 0, kk == KC - 1, [Wb, sqb], [psb])
                STT(rec, x[:, m, :], ps[:, 0:N], g.modT[:, l, 16 + m, b:b + 1], x[:, m, :], ALU.mult, ALU.add, [psb, g.modT_b, xb], [xb])

        def rmsmod(i):
            x, xb = xs[i % 2], xbs[i % 2]
            b = (i * N) // T
            stats(x, xb)
            for m in range(KC):
                t1, t1b = tr.next()
                TT(rec, "dve", t1[:], x[:, m, :], rstd[:], ALU.mult, [xb, rstdb], [t1b])
                ACTF(rec, hT[:, m, :], t1[:], AF.Identity, [t1b, g.Amod_b, g.modT_b], [hb],
                     scale=g.Amod[:, l, 1, m, b:b + 1], bias=g.modT[:, l, 24 + m, b:b + 1])

        def gateup(i):
            for j in range(JH):
                pg, pgb = next_ps(g)
                pu, pub = next_ps(g)
                for kk in range(KC):
                    MM(rec, pg[:, 0:N], Wg[:, kk, j * 128:(j + 1) * 128], hT[:, kk, :], kk == 0, kk == KC - 1, [Wb, hb], [pgb])
                for kk in range(KC):
                    MM(rec, pu[:, 0:N], Wu[:, kk, j * 128:(j + 1) * 128], hT[:, kk, :], kk == 0, kk == KC - 1, [Wb, hb], [pub])
                t1, t1b = tr.next()
                ACTF(rec, t1[:], pg[:, 0:N], AF.Silu, [pgb], [t1b])
                TT(rec, "dve", actT[:, j, :], t1[:], pu[:, 0:N], ALU.mult, [t1b, pub], [actb])

        def down(i):
            x, xb = xs[i % 2], xbs[i % 2]
            b = (i * N) // T
            for m in range(KC):
                ps, psb = next_ps(g)
                for j in range(JH):
                    MM(rec, ps[:, 0:N], Wd[:, j, m * 128:(m + 1) * 128], actT[:, j, :], j == 0, j == JH - 1, [Wb, actb], [psb])
                STT(rec, x[:, m, :], ps[:, 0:N], g.modT[:, l, 40 + m, b:b + 1], x[:, m, :], ALU.mult, ALU.add, [psb, g.modT_b, xb], [xb])

        def store(i):
            x, xb = xs[i % 2], xbs[i % 2]
            if not last:
                rec.dma("sp", XT[:, :, i * N:(i + 1) * N], x[:], reads=[xb])
                return
            stats(x, xb)
            for m in range(KC):
                STT(rec, x[:, m, :], x[:, m, :], fng[:, m:m + 1], rstd[:], ALU.mult, ALU.mult, [xb, rstdb, fngb], [xb])
            for s in range(N // 128):
                f, fb = fr.next()
                for m in range(KC):
                    if m % 4 == 0:
                        ps, psb = next_ps(g)
                    rec.op("pe", lambda e, ps=ps, m=m, s=s, x=x: e.transpose(
                        ps[:, (m % 4) * 128:(m % 4 + 1) * 128], x[:, m, s * 128:(s + 1) * 128], g.ident[:]),
                        reads=[xb, g.const_b], writes=[psb])
                    if m % 4 == 3:
                        evac(rec, m // 4, f[:, (m - 3) * 128:(m + 1) * 128], ps[:, :], [psb], [fb])
                t0 = i * N + s * 128
                rec.dma("sp", dr["out"][t0:t0 + 128, :], f[:], reads=[fb])

        load_x(0)
        load_y(0)
        outproj(0)
        rmsmod(0)
        for i in range(ntile):
            if i + 1 < ntile:
                load_x(i + 1)
                if not last:
                    load_y(i + 1)
            gateup(i)
            if i + 1 < ntile:
                if last:
                    load_y(i + 1)
                outproj(i + 1)
                rmsmod(i + 1)
            down(i)
            store(i)
        rec.emit()


TWO_PI = 2.0 * np.pi
MLA_SCALE = 96.0 ** -0.5


def evac(rec, i, out, in_, reads, writes):
    if i % 2 == 0:
        rec.op("act", lambda e: e.copy(out=out, in_=in_), reads=reads, writes=writes)
    else:
        rec.op("dve", lambda e: e.tensor_copy(out=out, in_=in_), reads=reads, writes=writes)


def rms_stats(rec, g, src, srcb, nch, N, sq, sqb, t1, t1b, rstd, rstdb, dim):
    rec.op("act", lambda e: e.activation(out=sq[:, 0:nch, :], in_=src, func=AF.Square), reads=[srcb], writes=[sqb])
    ps, psb = next_ps(g)
    for kk in range(nch):
        rec.op("pe", lambda e, kk=kk: e.matmul(ps[:, 0:N], lhsT=g.ones_bf[:], rhs=sq[:, kk, :],
                                                start=(kk == 0), stop=(kk == nch - 1)),
               reads=[sqb, g.const_b], writes=[psb])
    rec.op("act", lambda e: e.activation(out=t1[:], in_=ps[:, 0:N], func=AF.Sqrt, scale=1.0 / dim, bias=g.epsb[:]),
           reads=[psb, g.const_b], writes=[t1b])
    rec.op("dve", lambda e: e.reciprocal(out=rstd[:], in_=t1[:]), reads=[t1b], writes=[rstdb])


def phase_pre0(nc, rec, g, dr, NTOK, T, N=512):
    l = 0
    XT = dr["XT"].rearrange("(k p) t -> p k t", p=128)
    RP = dr["RP"].rearrange("(k p) t -> p k t", p=128)
    ntile = NTOK // N
    NS = N // 128
    with ExitStack() as es:
        sb = lambda n, s, d=F32: es.enter_context(nc.sbuf_tensor(uniq(n), s, d))
        Win = sb("Win", [128, KC, 2464], BF16)
        Wsw = sb("Wsw", [128, KC, 32], BF16)
        Wuq = sb("Wuq", [128, 3, 768], BF16)
        Wuqs = sb("Wuqs", [128, 3, 768], BF16)
        Wukv = sb("Wukv", [128, 2, 1024], BF16)
        Wb = Buf("W")
        st_rot = Rot([sb("wst%d" % i, [128, 1232]) for i in range(2)])
        jobs = []
        for kk in range(KC):
            for hh in range(2):
                cs = slice(hh * 1232, (hh + 1) * 1232)
                jobs.append((dr["w_in_even"][0, kk * 128:(kk + 1) * 128, cs], Win[:, kk, cs], Wb))
            jobs.append((dr["w_in_sw"][kk * 128:(kk + 1) * 128, :], Wsw[:, kk, :], Wb))
        for kk in range(3):
            jobs.append((dr["mla_w_uq"][0, kk * 128:(kk + 1) * 128, :], Wuq[:, kk, :], Wb))
            jobs.append((dr["w_uq_sw"][kk * 128:(kk + 1) * 128, :], Wuqs[:, kk, :], Wb))
        for kk in range(2):
            jobs.append((dr["w_ukv_r"][kk * 128:(kk + 1) * 128, :], Wukv[:, kk, :], Wb))
        load_cast(rec, st_rot, jobs)
        ctab = sb("ctab", [128, 8]); ctabb = Buf()
        rec.dma("sp", ctab[:], dr["ctab"][:, :], writes=[ctabb])
        negpi = sb("negpi", [128, 1]);
        rec.op("pool", lambda e: e.memset(negpi[:], -np.pi), writes=[ctabb])
        Cq = sb("Cq", [96, N]); Sq = sb("Sq", [96, N]); trb = Buf()
        rec.op("pool", lambda e: e.memset(Cq[0:64, :], MLA_SCALE), writes=[trb])
        rec.op("pool", lambda e: e.memset(Sq[0:64, :], 0.0), writes=[trb])
        Ck = sb("Ck", [32, N]); Sk = sb("Sk", [32, N])
        posi = sb("posi", [96, N], I32); posib = Buf()
        posf = sb("posf", [96, N]); posfb = Buf()
        ua = sb("ua", [96, N]); uab = Buf()
        ui = sb("ui", [96, N], I32); uib = Buf()
        uf = sb("uf", [96, N]); ufb = Buf()
        um = sb("um", [96, N]); umb = Buf()
        trig = [sb("sinv", [96, N]), sb("cosv", [96, N])]; trigb = [Buf(), Buf()]
        xin = sb("xin", [128, NS, D]); xinb = Buf()
        xT = sb("xT", [128, KC, N]); xTb = Buf()
        sq = sb("sq", [128, KC, N], BF16); sqb = Buf()
        hT = sb("hT", [128, KC, N], BF16); hb = Buf()
        tr = Rot([sb("tmp%d" % i, [128, N]) for i in range(3)])
        rstd = sb("rstd", [128, N]); rstdb = Buf()
        cq = sb("cq", [128, 3, N]); cqb = Buf()
        cqn = sb("cqn", [128, 3, N], BF16); cqnb = Buf()
        ckv = sb("ckv", [128, 2, N]); ckvb = Buf()
        ckvn = sb("ckvn", [128, 2, N], BF16); ckvnb = Buf()
        qo = Rot([sb("qo%d" % i, [96, N], BF16) for i in range(2)])
        ko = Rot([sb("ko%d" % i, [128, N], BF16) for i in range(2)])
        kro = sb("kro", [32, N], BF16); krob = Buf()
        vo = Rot([sb("vo%d" % i, [128, 512], BF16) for i in range(2)])
        rpo = Rot([sb("rpo%d" % i, [128, N]) for i in range(3)])
        ei = 0
        for ti in range(ntile):
            b = (ti * N) // T
            t0 = ti * N
            rec.dma("sp", xin[:], dr["x"][t0:t0 + N, :].rearrange("(s p) d -> p s d", p=128), writes=[xinb])
            for m in range(KC):
                ps, psb = next_ps(g)
                for s in range(NS):
                    rec.op("pe", lambda e, ps=ps, m=m, s=s: e.transpose(
                        ps[:, s * 128:(s + 1) * 128], xin[:, s, m * 128:(m + 1) * 128], g.ident[:]),
                        reads=[xinb, g.const_b], writes=[psb])
                evac(rec, m, xT[:, m, :], ps[:, 0:N], [psb], [xTb])
            rec.dma("sp", XT[:, :, t0:t0 + N], xT[:], reads=[xTb])
            rec.dma("sp", posi[:], dr["pos"][b:b + 1, (t0 % T):(t0 % T) + N].partition_broadcast(96), writes=[posib])
            rec.op("dve", lambda e: e.tensor_copy(out=posf[:], in_=posi[:]), reads=[posib], writes=[posfb])
            for w in range(2):
                off = 0.5 if w == 0 else 0.75
                rec.op("dve", lambda e, off=off: e.tensor_scalar(out=ua[:], in0=posf[:], scalar1=ctab[0:96, 0:1], scalar2=off,
                                                                 op0=ALU.mult, op1=ALU.add), reads=[posfb, ctabb], writes=[uab])
                rec.op("dve", lambda e: e.tensor_copy(out=ui[:], in_=ua[:]), reads=[uab], writes=[uib])
                rec.op("dve", lambda e: e.tensor_copy(out=uf[:], in_=ui[:]), reads=[uib], writes=[ufb])
                rec.op("dve", lambda e: e.tensor_tensor(out=ua[:], in0=ua[:], in1=uf[:], op=ALU.subtract), reads=[uab, ufb], writes=[uab])
                rec.op("dve", lambda e: e.tensor_scalar(out=um[:], in0=ua[:], scalar1=0.0, scalar2=None, op0=ALU.is_lt),
                       reads=[uab], writes=[umb])
                rec.op("dve", lambda e: e.tensor_tensor(out=ua[:], in0=ua[:], in1=um[:], op=ALU.add), reads=[uab, umb], writes=[uab])
                rec.op("act", lambda e, w=w: e.activation(out=trig[w][:], in_=ua[:], func=AF.Sin, scale=TWO_PI, bias=negpi[0:96, :]),
                       reads=[uab, ctabb], writes=[trigb[w]])
            rec.op("dve", lambda e: e.tensor_copy(out=Ck[:], in_=trig[1][0:32, :]), reads=[trigb[1]], writes=[trb])
            rec.op("dve", lambda e: e.tensor_scalar(out=Sk[:], in0=trig[0][0:32, :], scalar1=ctab[0:32, 1:2], scalar2=None, op0=ALU.mult),
                   reads=[trigb[0], ctabb], writes=[trb])
            rec.op("dve", lambda e: e.tensor_scalar(out=Cq[64:96, :], in0=trig[1][64:96, :], scalar1=MLA_SCALE, scalar2=None, op0=ALU.mult),
                   reads=[trigb[1]], writes=[trb])
            rec.op("dve", lambda e: e.tensor_scalar(out=Sq[64:96, :], in0=trig[0][64:96, :], scalar1=ctab[64:96, 2:3], scalar2=None, op0=ALU.mult),
                   reads=[trigb[0], ctabb], writes=[trb])
            t1, t1b = tr.next()
            rms_stats(rec, g, xT[:], xTb, KC, N, sq, sqb, t1, t1b, rstd, rstdb, D)
            for m in range(KC):
                t1, t1b = tr.next()
                rec.op("dve", lambda e, t1=t1, m=m: e.tensor_tensor(out=t1[:], in0=xT[:, m, :], in1=rstd[:], op=ALU.mult),
                       reads=[xTb, rstdb], writes=[t1b])
                rec.op("act", lambda e, t1=t1, m=m, b=b: e.activation(
                    out=hT[:, m, :], in_=t1[:], func=AF.Identity, scale=g.Amod[:, l, 0, m, b:b + 1],
                    bias=g.modT[:, l, 0 + m, b:b + 1]), reads=[t1b, g.Amod_b, g.modT_b], writes=[hb])

            def proj(ps, psb, c0, M, W=Win):
                for kk in range(KC):
                    rec.op("pe", lambda e, kk=kk: e.matmul(ps[0:M, 0:N], lhsT=W[:, kk, c0:c0 + M], rhs=hT[:, kk, :],
                                                            start=(kk == 0), stop=(kk == KC - 1)), reads=[Wb, hb], writes=[psb])
            for c in range(3):
                ps, psb = next_ps(g)
                proj(ps, psb, c * 128, 128)
                evac(rec, c, cq[:, c, :], ps[:, 0:N], [psb], [cqb])
            for c in range(2):
                ps, psb = next_ps(g)
                proj(ps, psb, 384 + c * 128, 128)
                evac(rec, c + 1, ckv[:, c, :], ps[:, 0:N], [psb], [ckvb])
            for (src, srcb, nch, dst, dstb, gcol, dim) in ((cq, cqb, 3, cqn, cqnb, 3, 384), (ckv, ckvb, 2, ckvn, ckvnb, 6, 256)):
                t1, t1b = tr.next()
                rms_stats(rec, g, src[:], srcb, nch, N, sq, sqb, t1, t1b, rstd, rstdb, dim)
                for c in range(nch):
                    rec.op("dve", lambda e, c=c, src=src, dst=dst, gcol=gcol: e.scalar_tensor_tensor(
                        out=dst[:, c, :], in0=src[:, c, :], scalar=ctab[:, gcol + c:gcol + c + 1], in1=rstd[:],
                        op0=ALU.mult, op1=ALU.mult), reads=[srcb, rstdb, ctabb], writes=[dstb])
            ps, psb = next_ps(g)
            proj(ps, psb, 640, 32)
            ps2, psb2 = next_ps(g)
            proj(ps2, psb2, 0, 32, W=Wsw)
            t1, t1b = tr.next()
            t2, t2b = tr.next()
            rec.op("dve", lambda e, t1=t1, ps2=ps2: e.tensor_tensor(out=t1[0:32, :], in0=ps2[0:32, 0:N], in1=Sk[:], op=ALU.mult),
                   reads=[psb2, trb], writes=[t1b])
            rec.op("dve", lambda e, t2=t2, ps=ps: e.tensor_tensor(out=t2[0:32, :], in0=ps[0:32, 0:N], in1=Ck[:], op=ALU.mult),
                   reads=[psb, trb], writes=[t2b])
            rec.op("dve", lambda e, t1=t1, t2=t2: e.tensor_tensor(out=kro[:], in0=t1[0:32, :], in1=t2[0:32, :], op=ALU.add),
                   reads=[t1b, t2b], writes=[krob])
            for h in range(8):
                rec.dma("pool" if h % 2 else "sp", dr["KT"][h, 64:96, t0:t0 + N], kro[:], reads=[krob])
            for h in range(8):
                psA, psAb = next_ps(g)
                psB, psBb = next_ps(g)
                for (ps, psb, W) in ((psA, psAb, Wuq), (psB, psBb, Wuqs)):
                    for kk in range(3):
                        rec.op("pe", lambda e, ps=ps, W=W, kk=kk, h=h: e.matmul(
                            ps[0:96, 0:N], lhsT=W[:, kk, h * 96:(h + 1) * 96], rhs=cqn[:, kk, :],
                            start=(kk == 0), stop=(kk == 2)), reads=[Wb, cqnb], writes=[psb])
                t1, t1b = tr.next()
                t2, t2b = tr.next()
                q, qb = qo.next()
                rec.op("dve", lambda e, t1=t1, psB=psB: e.tensor_tensor(out=t1[0:96, :], in0=psB[0:96, 0:N], in1=Sq[:], op=ALU.mult),
                       reads=[psBb, trb], writes=[t1b])
                rec.op("dve", lambda e, t2=t2, psA=psA: e.tensor_tensor(out=t2[0:96, :], in0=psA[0:96, 0:N], in1=Cq[:], op=ALU.mult),
                       reads=[psAb, trb], writes=[t2b])
                rec.op("pool", lambda e, t1=t1, t2=t2, q=q: e.tensor_tensor(out=q[:], in0=t1[0:96, :], in1=t2[0:96, :], op=ALU.add),
                       reads=[t1b, t2b], writes=[qb])
                rec.dma("sp", dr["QT"][h, :, t0:t0 + N], q[:], reads=[qb])
            for hp in range(4):
                ps, psb = next_ps(g)
                for kk in range(2):
                    rec.op("pe", lambda e, ps=ps, kk=kk, hp=hp: e.matmul(
                        ps[:, 0:N], lhsT=Wukv[:, kk, hp * 128:(hp + 1) * 128], rhs=ckvn[:, kk, :],
                        start=(kk == 0), stop=(kk == 1)), reads=[Wb, ckvnb], writes=[psb])
                k, kb = ko.next()
                evac(rec, hp, k[:], ps[:, 0:N], [psb], [kb])
                rec.dma("sp", dr["KT"][2 * hp, 0:64, t0:t0 + N], k[0:64, :], reads=[kb])
                rec.dma("pool", dr["KT"][2 * hp + 1, 0:64, t0:t0 + N], k[64:128, :], reads=[kb])
            for s in range(NS):
                ps, psb = next_ps(g)
                for kk in range(2):
                    rec.op("pe", lambda e, ps=ps, kk=kk, s=s: e.matmul(
                        ps[:, :], lhsT=ckvn[:, kk, s * 128:(s + 1) * 128], rhs=Wukv[:, kk, 512:1024],
                        start=(kk == 0), stop=(kk == 1)), reads=[Wb, ckvnb], writes=[psb])
                v, vb = vo.next()
                evac(rec, s, v[:], ps[:, :], [psb], [vb])
                rec.dma("sp", dr["V"][t0 + s * 128:t0 + (s + 1) * 128, :], v[:], reads=[vb])
            for c in range(14):
                ps, psb = next_ps(g)
                proj(ps, psb, 672 + c * 128, 128)
                o, ob = rpo.next()
                evac(rec, c, o[:], ps[:, 0:N], [psb], [ob])
                rec.dma("sp" if c % 2 else "pool", RP[:, c, t0:t0 + N], o[:], reads=[ob])
        rec.emit()


def phase_mla_g(nc, rec, g, dr, NB, T, nptr=2):
    NKB = T // 128
    YT = dr["YT"].rearrange("(k p) t -> p k t", p=128)
    offs = [0]
    for kb in range(NKB):
        offs.append(offs[-1] + (T - kb * 128))
    with ExitStack() as es:
        sb = lambda n, s, d=F32: es.enter_context(nc.sbuf_tensor(uniq(n), s, d))
        qr = Rot([sb("q%d" % i, [96, T], BF16) for i in range(2)])
        kr = Rot([sb("k%d" % i, [96, T], BF16) for i in range(2)])
        va = [sb("va%d" % i, [128, NKB, 65], BF16) for i in range(2)]
        vab = [Buf(), Buf()]
        for i in range(2):
            rec.op("pool", lambda e, i=i: e.memset(va[i][:], 1.0), writes=[vab[i]])
        ptr = Rot([sb("pt%d" % i, [128, offs[-1]], BF16) for i in range(nptr)])
        mask = sb("mask", [128, 128], BF16); maskb = Buf()
        mstage = sb("mstage", [128, 128])
        rec.dma("sp", mstage[:], dr["cmask"][:, :], writes=[maskb])
        rec.op("dve", lambda e: e.tensor_copy(out=mask[:], in_=mstage[:]), reads=[maskb], writes=[maskb])
        ytm = sb("ytm", [128, NKB, 512]); ytmb = Buf()
        rc = Rot([sb("rc%d" % i, [128, 1]) for i in range(4)])
        ytr = Rot([sb("yts%d" % i, [128, 4, 128], BF16) for i in range(2)])
        it = 0
        for b in range(NB):
            for h in range(8):
                q, qb_ = qr.next()
                k, kb_ = kr.next()
                v, vb_ = va[it % 2], vab[it % 2]
                it += 1
                pt, ptb = ptr.next()
                rec.dma("sp", q[:], dr["QT"][h, :, b * T:(b + 1) * T], writes=[qb_])
                rec.dma("pool", k[:], dr["KT"][h, :, b * T:(b + 1) * T], writes=[kb_])
                rec.dma("sp", v[:, :, 0:64], dr["V"][b * T:(b + 1) * T, h * 64:(h + 1) * 64].rearrange("(k p) d -> p k d", p=128),
                        writes=[vb_])
                for kb in range(NKB):
                    q0 = kb * 128
                    c = q0
                    while c < T:
                        n = min(512, T - c)
                        ps, psb = next_ps(g)
                        rec.op("pe", lambda e, ps=ps, k=k, q=q, q0=q0, c=c, n=n: e.matmul(
                            ps[:, 0:n], lhsT=k[:, q0:q0 + 128], rhs=q[:, c:c + n], start=True, stop=True),
                            reads=[kb_, qb_], writes=[psb])
                        o0 = offs[kb] + (c - q0)
                        rec.op("act", lambda e, ps=ps, pt=pt, o0=o0, n=n: e.activation(out=pt[:, o0:o0 + n], in_=ps[:, 0:n], func=AF.Exp),
                               reads=[psb], writes=[ptb])
                        c += n
                    o0 = offs[kb]
                    rec.op("pool", lambda e, pt=pt, o0=o0: e.tensor_tensor(out=pt[:, o0:o0 + 128], in0=pt[:, o0:o0 + 128], in1=mask[:], op=ALU.mult),
                           reads=[ptb, maskb], writes=[ptb])
                for qb in range(NKB):
                    ps, psb = next_ps(g)
                    for kb in range(qb + 1):
                        o0 = offs[kb] + (qb - kb) * 128
                        rec.op("pe", lambda e, ps=ps, pt=pt, v=v, o0=o0, kb=kb, qb=qb: e.matmul(
                            ps[:, 0:65], lhsT=pt[:, o0:o0 + 128], rhs=v[:, kb, :], start=(kb == 0), stop=(kb == qb)),
                            reads=[ptb, vb_], writes=[psb])
                    r, rb = rc.next()
                    rec.op("dve", lambda e, r=r, ps=ps: e.reciprocal(out=r[:], in_=ps[:, 64:65]), reads=[psb], writes=[rb])
                    rec.op("dve", lambda e, r=r, ps=ps, qb=qb, h=h: e.tensor_scalar(
                        out=ytm[:, qb, h * 64:(h + 1) * 64], in0=ps[:, 0:64], scalar1=r[:, 0:1], scalar2=None, op0=ALU.mult),
                        reads=[psb, rb], writes=[ytmb])
                yield 1
            for qb in range(NKB):
                ps, psb = next_ps(g)
                for c in range(4):
                    rec.op("pe", lambda e, ps=ps, qb=qb, c=c: e.transpose(
                        ps[:, c * 128:(c + 1) * 128], ytm[:, qb, c * 128:(c + 1) * 128], g.ident[:]),
                        reads=[ytmb, g.const_b], writes=[psb])
                yt, ytb = ytr.next()
                evac(rec, qb, yt[:].rearrange("p c t -> p (c t)"), ps[:, :], [psb], [ytb])
                t0 = b * T + qb * 128
                rec.dma("sp", YT[:, 0:4, t0:t0 + 128], yt[:], reads=[ytb])
        yield "END"


def run_gens(rec, gens):
    gens = list(gens)
    live = list(gens)
    while live:
        for gen in list(live):
            if next(gen) == "END":
                live.remove(gen)
    rec.emit()
    for gen in reversed(gens):
        for _ in gen:
            pass


def phase_mla(nc, rec, g, dr, NB, T):
    run_gens(rec, [phase_mla_g(nc, rec, g, dr, NB, T)])


def host_layout(inp):
    f32 = np.float32
    d = {}
    NB = inp["c"].shape[0]
    d["ident"] = np.eye(128, dtype=f32)
    d["cmask"] = np.triu(np.ones((128, 128), dtype=f32))
    d["x"] = np.ascontiguousarray(inp["x"].reshape(-1, 1024))
    d["pos"] = np.ascontiguousarray(inp["positions"].astype(np.int32))
    d["cT"] = np.ascontiguousarray(inp["c"].T.reshape(8, 128, NB).transpose(1, 0, 2))
    d["ada_bT"] = np.ascontiguousarray(inp["ada_b"].reshape(2, 48, 128).transpose(2, 0, 1))
    d["norm_mixT"] = np.ascontiguousarray(inp["norm_mix"].reshape(2, 8, 128).transpose(2, 0, 1))
    d["norm_ffnT"] = np.ascontiguousarray(inp["norm_ffn"].reshape(2, 8, 128).transpose(2, 0, 1))
    d["final_normT"] = np.ascontiguousarray(inp["final_norm"].reshape(8, 128).T)
    for k in ["ada_w", "w_out_even", "w_out_odd", "ffn_w_gate", "ffn_w_up", "ffn_w_down", "w_in_even", "mla_w_uq", "w_in_odd"]:
        d[k] = inp[k]
    wi = inp["w_in_even"][0]
    d["w_in_sw"] = np.ascontiguousarray(np.concatenate([wi[:, 656:672], wi[:, 640:656]], axis=1))
    wq = inp["mla_w_uq"][0].reshape(384, 8, 96)
    d["w_uq_sw"] = np.ascontiguousarray(np.concatenate([wq[:, :, 0:64], wq[:, :, 80:96], wq[:, :, 64:80]], axis=2).reshape(384, 768))
    wkv = inp["mla_w_ukv"][0].reshape(256, 8, 128)
    d["w_ukv_r"] = np.ascontiguousarray(np.concatenate([wkv[:, :, 0:64].reshape(256, 512), wkv[:, :, 64:128].reshape(256, 512)], axis=1))
    ctab = np.zeros((128, 8), dtype=f32)
    invf = (1.0 / (10000.0 ** (np.arange(0, 32, 2, dtype=np.float32) / 32))).astype(np.float32)
    for base in (0, 16, 64, 80):
        ctab[base:base + 16, 0] = invf / np.float32(2 * np.pi)
    ctab[0:16, 1] = -1.0
    ctab[16:32, 1] = 1.0
    ctab[64:80, 2] = -MLA_SCALE
    ctab[80:96, 2] = MLA_SCALE
    ctab[:, 3:6] = inp["mla_q_norm"][0].reshape(3, 128).T
    ctab[:, 6:8] = inp["mla_kv_norm"][0].reshape(2, 128).T
    d["ctab"] = ctab
    su = np.triu(np.ones((128, 128), dtype=f32), 1)
    iu = np.triu(np.ones((128, 128), dtype=f32), 0)
    sl = np.tril(np.ones((128, 128), dtype=f32), -1)
    d["rmasks"] = np.ascontiguousarray(np.concatenate([su, iu, sl], axis=1))
    for k in ("rwkv_ln_w", "rwkv_ln_b"):
        d[k] = inp[k]
    d["hg_lbT"] = np.ascontiguousarray(inp["hg_lb_logits"].reshape(2, 8, 128).transpose(2, 0, 1))
    d["hg_onT"] = np.ascontiguousarray(inp["hg_out_norm"][0].reshape(128, 1))
    blk = np.arange(128) // 32
    d["hmask"] = np.ascontiguousarray(((blk[:, None] == blk[None, :]) & (np.arange(128)[:, None] <= np.arange(128)[None, :])).astype(f32))
    d["rw_mu"] = np.ascontiguousarray(inp["rwkv_mu"][0].reshape(14, 128).T)
    d["rw_par"] = np.ascontiguousarray(np.concatenate(
        [inp[k][0].reshape(4, 128).T for k in ("rwkv_w0", "rwkv_a0", "rwkv_k_k", "rwkv_k_a", "rwkv_r_k")], axis=1))
    for k in ("rwkv_w2", "rwkv_a2", "rwkv_g2"):
        d[k] = inp[k]
    return d


NEG_EM05 = -float(np.exp(-0.5))


def phase_rwkv_prep_g(nc, rec, g, dr, NTOK, T, N=256):
    RP = dr["RP"].rearrange("(k p) t -> p k t", p=128)
    outs = {k: dr[k].rearrange("(c p) t -> p c t", p=128) for k in ("AT", "BT", "KTt", "RTt", "VT", "RKT")}
    PC = dr["PC"].rearrange("(c p) n -> p c n", p=128)
    ntile = NTOK // N
    NS = N // 128
    with ExitStack() as es:
        sb = lambda n, s, d=F32: es.enter_context(nc.sbuf_tensor(uniq(n), s, d))
        par = sb("par", [128, 64]); parb = Buf()
        rec.dma("sp", par[:, 0:14], dr["rw_mu"][:, :], writes=[parb])
        rec.dma("sp", par[:, 28:48], dr["rw_par"][:, :], writes=[parb])
        rec.op("dve", lambda e: e.tensor_scalar(out=par[:, 14:28], in0=par[:, 0:14], scalar1=-1.0, scalar2=1.0, op0=ALU.mult, op1=ALU.add),
               reads=[parb], writes=[parb])
        rec.op("dve", lambda e: e.tensor_scalar(out=par[:, 48:52], in0=par[:, 40:44], scalar1=-1.0, scalar2=1.0, op0=ALU.mult, op1=ALU.add),
               reads=[parb], writes=[parb])
        wst = sb("wst", [128, 512]); wstb = Buf()
        w2b = sb("w2b", [128, 512], BF16); g2b = sb("g2b", [128, 512], BF16); Wb = Buf()
        rec.dma("sp", wst[0:64, :], dr["rwkv_w2"][0, :, :], writes=[wstb])
        rec.dma("sp", wst[64:128, :], dr["rwkv_a2"][0, :, :], writes=[wstb])
        rec.op("dve", lambda e: e.tensor_copy(out=w2b[:], in_=wst[:]), reads=[wstb], writes=[Wb])
        rec.dma("sp", wst[:, :], dr["rwkv_g2"][0, :, :], reads=[], writes=[wstb])
        rec.op("dve", lambda e: e.tensor_copy(out=g2b[:], in_=wst[:]), reads=[wstb], writes=[Wb])
        bones = sb("bones", [128, 128], BF16)
        rec.op("pool", lambda e: e.memset(bones[:], 0.0), writes=[Wb])
        rec.op("pool", lambda e: e.memset(bones[0:64, 0:64], 1.0), writes=[Wb])
        rec.op("pool", lambda e: e.memset(bones[64:128, 64:128], 1.0), writes=[Wb])
        rmask = sb("rmask", [128, N])
        rec.op("pool", lambda e: e.memset(rmask[:], 1.0), writes=[Wb])
        rec.op("pool", lambda e: e.memset(rmask[:].rearrange("p (c t) -> p c t", t=128)[:, :, 0:1], 0.0), writes=[Wb])
        p = sb("p", [128, 14, N]); pb = Buf()
        praw = [sb("praw%d" % i, [128, 14, N + 1]) for i in range(2)]
        prb = [Buf(), Buf()]

        def load(ti):
            t0 = ti * N
            pr, prbb = praw[ti % 2], prb[ti % 2]
            if t0 % T == 0:
                rec.op("pool", lambda e: e.memset(pr[:, :, 0:1], 0.0), writes=[prbb])
                rec.dma("sp", pr[:, :, 1:N + 1], RP[:, :, t0:t0 + N], writes=[prbb])
            else:
                rec.dma("sp", pr[:, :, :], RP[:, :, t0 - 1:t0 + N], writes=[prbb])
        load(0)
        tmpr = Rot([sb("mt%d" % i, [128, N]) for i in range(3)])
        wab = sb("wab", [128, N], BF16); wabb = Buf()
        sgl = sb("sgl", [128, N], BF16); sglb = Buf()
        F4 = lambda n: (sb(n, [128, 4, N]), Buf())
        ld, ldb = F4("ld"); bb, bbb = F4("bb"); epos, eposb = F4("epos"); eneg, enegb = F4("eneg"); eprev, eprevb = F4("eprev")
        aa, aab = F4("aa"); kk, kkb = F4("kk"); kp, kpb = F4("kp"); rn, rnb = F4("rn")
        sqk = sb("sqk", [128, 4, N], BF16); sqkb = Buf()
        ob = {k: (sb("o_" + k, [128, 4, N], BF16), Buf()) for k in outs}
        pco = sb("pco", [128, 4, NS]); pcob = Buf()
        gto = Rot([sb("gto%d" % i, [128, 512], BF16) for i in range(2)])
        for ti in range(ntile):
            t0 = ti * N
            if ti + 1 < ntile:
                load(ti + 1)
            pr, prbb = praw[ti % 2], prb[ti % 2]
            for j in range(14):
                t1, t1b = tmpr.next()
                rec.op("act", lambda e, j=j, t1=t1, pr=pr: e.activation(out=t1[:], in_=pr[:, j, 1:N + 1], func=AF.Identity, scale=par[:, 14 + j:15 + j]),
                       reads=[prbb, parb], writes=[t1b])
                rec.op("dve", lambda e, j=j, t1=t1, pr=pr: e.scalar_tensor_tensor(out=p[:, j, :], in0=pr[:, j, 0:N], scalar=par[:, j:j + 1], in1=t1[:],
                                                                                 op0=ALU.mult, op1=ALU.add), reads=[prbb, t1b, parb], writes=[pb])
            rec.op("act", lambda e: e.activation(out=wab[0:64, :], in_=p[0:64, 12, :], func=AF.Tanh), reads=[pb], writes=[wabb])
            rec.op("dve", lambda e: e.tensor_copy(out=wab[64:128, :], in_=p[64:128, 12, :]), reads=[pb], writes=[wabb])
            for c in range(4):
                ps, psb = next_ps(g)
                rec.op("pe", lambda e, ps=ps, c=c: e.matmul(ps[:, 0:N], lhsT=w2b[0:64, c * 128:(c + 1) * 128], rhs=wab[0:64, :], start=True, stop=True),
                       reads=[Wb, wabb], writes=[psb])
                rec.op("act", lambda e, ps=ps, c=c: e.activation(out=ld[:, c, :], in_=ps[:, 0:N], func=AF.Sigmoid, bias=par[:, 28 + c:29 + c]),
                       reads=[psb, parb], writes=[ldb])
            for c in range(4):
                ps, psb = next_ps(g)
                rec.op("pe", lambda e, ps=ps, c=c: e.matmul(ps[:, 0:N], lhsT=w2b[64:128, c * 128:(c + 1) * 128], rhs=wab[64:128, :], start=True, stop=True),
                       reads=[Wb, wabb], writes=[psb])
                rec.op("act", lambda e, ps=ps, c=c: e.activation(out=aa[:, c, :], in_=ps[:, 0:N], func=AF.Sigmoid, bias=par[:, 32 + c:33 + c]),
                       reads=[psb, parb], writes=[aab])
            rec.op("act", lambda e: e.activation(out=sgl[:], in_=p[:, 13, :], func=AF.Sigmoid), reads=[pb], writes=[sglb])
            rec.op("dve", lambda e: e.tensor_scalar(out=ld[:], in0=ld[:], scalar1=NEG_EM05, scalar2=None, op0=ALU.mult), reads=[ldb], writes=[ldb])
            for c in range(4):
                rec.op("dve", lambda e, c=c: e.tensor_tensor_scan(out=bb[:, c, :], data0=rmask[:], data1=ld[:, c, :], initial=0.0,
                                                                  op0=ALU.mult, op1=ALU.add), reads=[ldb, Wb], writes=[bbb])
            rec.op("pool", lambda e: e.tensor_tensor(out=eprev[:], in0=bb[:], in1=ld[:], op=ALU.subtract), reads=[bbb, ldb], writes=[eprevb])
            rec.op("act", lambda e: e.activation(out=epos[:], in_=bb[:], func=AF.Exp), reads=[bbb], writes=[eposb])
            rec.op("act", lambda e: e.activation(out=eneg[:], in_=bb[:], func=AF.Exp, scale=-1.0), reads=[bbb], writes=[enegb])
            rec.op("act", lambda e: e.activation(out=eprev[:], in_=eprev[:], func=AF.Exp), reads=[eprevb], writes=[eprevb])
            for c in range(4):
                rec.op("act", lambda e, c=c: e.activation(out=sqk[:, c, :], in_=p[:, 4 + c, :], func=AF.Square, scale=par[:, 36 + c:37 + c]),
                       reads=[pb, parb], writes=[sqkb])
            for c in range(4):
                ps, psb = next_ps(g)
                rec.op("pe", lambda e, ps=ps, c=c: e.matmul(ps[:, 0:N], lhsT=bones[:], rhs=sqk[:, c, :], start=True, stop=True),
                       reads=[Wb, sqkb], writes=[psb])
                rec.op("act", lambda e, ps=ps, c=c: e.activation(out=rn[:, c, :], in_=ps[:, 0:N], func=AF.Sqrt), reads=[psb], writes=[rnb])
            rec.op("dve", lambda e: e.tensor_scalar(out=rn[:], in0=rn[:], scalar1=1e-12, scalar2=None, op0=ALU.max), reads=[rnb], writes=[rnb])
            rec.op("dve", lambda e: e.reciprocal(out=rn[:], in_=rn[:]), reads=[rnb], writes=[rnb])
            for c in range(4):
                rec.op("dve", lambda e, c=c: e.scalar_tensor_tensor(out=kk[:, c, :], in0=p[:, 4 + c, :], scalar=par[:, 36 + c:37 + c], in1=rn[:, c, :],
                                                                   op0=ALU.mult, op1=ALU.mult), reads=[pb, parb, rnb], writes=[kkb])
                rec.op("dve", lambda e, c=c: e.tensor_scalar(out=kp[:, c, :], in0=aa[:, c, :], scalar1=par[:, 40 + c:41 + c], scalar2=par[:, 48 + c:49 + c],
                                                              op0=ALU.mult, op1=ALU.add), reads=[aab, parb], writes=[kpb])
            rec.op("dve", lambda e: e.tensor_tensor(out=kp[:], in0=kp[:], in1=p[:, 4:8, :], op=ALU.mult), reads=[kpb, pb], writes=[kpb])
            o, obb = ob["AT"]
            rec.op("dve", lambda e, o=o: e.scalar_tensor_tensor(out=o[:], in0=kk[:], scalar=-1.0, in1=eprev[:], op0=ALU.mult, op1=ALU.mult),
                   reads=[kkb, eprevb], writes=[obb])
            o, obb = ob["BT"]
            rec.op("pool", lambda e: e.tensor_tensor(out=kk[:], in0=kk[:], in1=aa[:], op=ALU.mult), reads=[kkb, aab], writes=[kkb])
            rec.op("dve", lambda e, o=o: e.tensor_tensor(out=o[:], in0=kk[:], in1=eneg[:], op=ALU.mult), reads=[kkb, enegb], writes=[obb])
            o, obb = ob["KTt"]
            rec.op("pool", lambda e, o=o: e.tensor_tensor(out=o[:], in0=kp[:], in1=eneg[:], op=ALU.mult), reads=[kpb, enegb], writes=[obb])
            o, obb = ob["RTt"]
            rec.op("dve", lambda e, o=o: e.tensor_tensor(out=o[:], in0=p[:, 0:4, :], in1=epos[:], op=ALU.mult), reads=[pb, eposb], writes=[obb])
            o, obb = ob["VT"]
            rec.op("act", lambda e, o=o: e.copy(out=o[:], in_=p[:, 8:12, :]), reads=[pb], writes=[obb])
            o, obb = ob["RKT"]
            for c in range(4):
                rec.op("dve", lambda e, o=o, c=c: e.scalar_tensor_tensor(out=o[:, c, :], in0=p[:, c, :], scalar=par[:, 44 + c:45 + c], in1=kp[:, c, :],
                                                                        op0=ALU.mult, op1=ALU.mult), reads=[pb, parb, kpb], writes=[obb])
            rec.op("pool", lambda e: e.tensor_copy(out=pco[:], in_=epos[:].rearrange("p c (s t) -> p c s t", t=128)[:, :, :, 127]),
                   reads=[eposb], writes=[pcob])
            rec.dma("sp", PC[:, :, t0 // 128:t0 // 128 + NS], pco[:], reads=[pcob])
            for i, k in enumerate(outs):
                o, obb = ob[k]
                rec.dma("sp" if i % 2 == 0 else "pool", outs[k][:, :, t0:t0 + N], o[:], reads=[obb])
            for s in range(NS):
                ps, psb = next_ps(g)
                rec.op("pe", lambda e, ps=ps, s=s: e.matmul(ps[:, :], lhsT=sgl[:, s * 128:(s + 1) * 128], rhs=g2b[:], start=True, stop=True),
                       reads=[sglb, Wb], writes=[psb])
                go, gob = gto.next()
                evac(rec, s, go[:], ps[:, :], [psb], [gob])
                rec.dma("sp", dr["G_tm"][t0 + s * 128:t0 + (s + 1) * 128, :], go[:], reads=[gob])
            yield 1
        yield "END"


def phase_rwkv_prep(nc, rec, g, dr, NTOK, T, N=256):
    run_gens(rec, [phase_rwkv_prep_g(nc, rec, g, dr, NTOK, T, N)])


def MM(rec, out, lhsT, rhs, start, stop, reads, writes):
    rec.op("pe", lambda e: e.matmul(out, lhsT=lhsT, rhs=rhs, start=start, stop=stop), reads=reads, writes=writes)


def CP(rec, eng, out, in_, reads, writes):
    if eng == "act":
        rec.op("act", lambda e: e.copy(out=out, in_=in_), reads=reads, writes=writes)
    else:
        rec.op(eng, lambda e: e.tensor_copy(out=out, in_=in_), reads=reads, writes=writes)


def TT(rec, eng, out, in0, in1, op, reads, writes):
    rec.op(eng, lambda e: e.tensor_tensor(out=out, in0=in0, in1=in1, op=op), reads=reads, writes=writes)


def TS(rec, eng, out, in0, s1, op0, reads, writes, s2=None, op1=None):
    if op1 is None:
        rec.op(eng, lambda e: e.tensor_scalar(out=out, in0=in0, scalar1=s1, scalar2=None, op0=op0), reads=reads, writes=writes)
    else:
        rec.op(eng, lambda e: e.tensor_scalar(out=out, in0=in0, scalar1=s1, scalar2=s2, op0=op0, op1=op1), reads=reads, writes=writes)


def STT(rec, out, in0, scalar, in1, op0, op1, reads, writes):
    rec.op("dve", lambda e: e.scalar_tensor_tensor(out=out, in0=in0, scalar=scalar, in1=in1, op0=op0, op1=op1), reads=reads, writes=writes)


def ACTF(rec, out, in_, func, reads, writes, scale=1.0, bias=None):
    if bias is None:
        rec.op("act", lambda e: e.activation(out=out, in_=in_, func=func, scale=scale), reads=reads, writes=writes)
    else:
        rec.op("act", lambda e: e.activation(out=out, in_=in_, func=func, scale=scale, bias=bias), reads=reads, writes=writes)


def phase_rwkv_main(nc, rec, g, dr, NB, T):
    import os
    NCH = int(os.environ.get("RW_NCH", T // 128))
    YT = dr["YT"].rearrange("(k p) t -> p k t", p=128)
    src = {k: dr[k].rearrange("(h k) t -> k h t", k=64) for k in ("AT", "BT", "KTt", "RTt", "VT", "RKT")}
    PCd = dr["PC"].rearrange("(h k) n -> k h n", k=64)
    NH = 8
    NHL = int(os.environ.get("RW_NH", "8"))
    LIM = int(os.environ.get("RW_LIM", "9"))
    with ExitStack() as es:
        sb = lambda n, s, d=F32: es.enter_context(nc.sbuf_tensor(uniq(n), s, d))
        cb = g.const_b
        masks = sb("masks", [128, 512]); mb = Buf()
        mlow = sb("mlow", [128, 128])
        rec.dma("sp", masks[:, 0:256], dr["rmasks"][:, 0:256], writes=[mb])
        rec.dma("sp", masks[:, 256:512], dr["rmasks"][:, 0:256], writes=[mb])
        rec.dma("sp", mlow[:], dr["rmasks"][:, 256:384], writes=[mb])
        lnw = sb("lnw", [128, 512]); lnb = sb("lnb", [128, 512]); lnbuf = Buf()
        rec.dma("sp", lnw[:], dr["rwkv_ln_w"][0:1, :].partition_broadcast(128), writes=[lnbuf])
        rec.dma("sp", lnb[:], dr["rwkv_ln_b"][0:1, :].partition_broadcast(128), writes=[lnbuf])
        eps2 = sb("eps2", [128, 1])
        rec.op("pool", lambda e: e.memset(eps2[:], 64e-5), writes=[lnbuf])
        NBUF = 3
        ARt = [sb("AR%d" % i, [64, NH, 2, 128], BF16) for i in range(NBUF)]
        Btt = [sb("Bt%d" % i, [64, NH, 128], BF16) for i in range(NBUF)]
        Ktt = [sb("Kt%d" % i, [64, NH, 128], BF16) for i in range(NBUF)]
        Vtt = [sb("Vt%d" % i, [64, NH, 128], BF16) for i in range(NBUF)]
        RKt = [sb("RK%d" % i, [64, NH, 128], BF16) for i in range(NBUF)]
        Gtm = [sb("Gtm%d" % i, [128, 512], BF16) for i in range(NBUF)]
        inb = [Buf() for _ in range(NBUF)]
        PCs = sb("PCs", [64, NH, NCH]); PCb = Buf()
        P2 = 2
        TM = [[sb("TM%d_%d" % (h, i), [128, 256], BF16) for i in range(P2)] for h in range(NH)]
        rks = [[sb("rks%d_%d" % (h, i), [128, 2]) for i in range(P2)] for h in range(NH)]
        M12 = [[sb("M12%d_%d" % (h, i), [128, 512], BF16) for i in range(P2)] for h in range(NH)]
        F32R = mybir.dt.float32r
        L0 = [[sb("L0%d_%d" % (h, i), [128, 128], F32R) for i in range(P2)] for h in range(NH)]
        LTr = [[sb("LTr%d_%d" % (h, i), [128, 128], F32R) for i in range(P2)] for h in range(NH)]
        Lpw = [[sb("Lpw%d_%d" % (h, i), [128, 256], F32R) for i in range(2)] for h in range(NH)]
        Xf = [[sb("Xr%d_%d" % (h, i), [128, 128], F32R) for i in range(P2)] for h in range(NH)]
        Xb = [[sb("Xb%d_%d" % (h, i), [128, 128], BF16) for i in range(P2)] for h in range(NH)]
        Gs = [[sb("Gs%d_%d" % (h, i), [64, 64], BF16) for i in range(P2)] for h in range(NH)]
        RhT = [[sb("RhT%d_%d" % (h, i), [64, 128], BF16) for i in range(P2)] for h in range(NH)]
        hb = [[{k: Buf() for k in ("TM", "rks", "M12", "L0", "X", "Xb", "G", "Rh")} for i in range(P2)] for h in range(NH)]
        Lpb = [[Buf() for i in range(2)] for h in range(NH)]
        Sf = [sb("Sf%d" % h, [64, 64]) for h in range(NH)]
        Sb_ = [sb("Sb%d" % h, [64, 64], BF16) for h in range(NH)]
        St = [sb("St%d" % h, [64, 64]) for h in range(NH)]
        Sfb = [Buf() for h in range(NH)]; Sbb = [Buf() for h in range(NH)]; Stb = [Buf() for h in range(NH)]
        Ytm = [sb("Ytm%d" % i, [128, 512]) for i in range(2)]; Ytmb = [Buf(), Buf()]
        ysq = sb("ysq", [128, 512]); ysqb = Buf()
        st = sb("gnst", [128, 5, 8]); stb = Buf()
        yto = Rot([sb("yto%d" % i, [128, 512], BF16) for i in range(2)])
        gi = 0
        for b in range(NB):
            rec.dma("sp", PCs[:], PCd[:, :, b * NCH:(b + 1) * NCH], writes=[PCb])
            for h in range(NH):
                rec.op("pool", lambda e, h=h: e.memset(Sf[h][:], 0.0), writes=[Sfb[h]])
                rec.op("pool", lambda e, h=h: e.memset(Sb_[h][:], 0.0), writes=[Sbb[h]])
            ctx = {}
            ctx2 = {}

            def load(c):
                gidx = b * NCH + c
                bi = gidx % NBUF
                tk = slice(b * T + c * 128, b * T + (c + 1) * 128)
                ib = inb[bi]
                AR, Bt, Kt, Vt, RK, Gt = ARt[bi], Btt[bi], Ktt[bi], Vtt[bi], RKt[bi], Gtm[bi]
                rec.dma("sp", AR[:, :, 0, :], src["AT"][:, :, tk], writes=[ib])
                rec.dma(os.environ.get("RW_DQ", "pool"), AR[:, :, 1, :], src["RTt"][:, :, tk], writes=[ib])
                rec.dma("sp", Bt[:], src["BT"][:, :, tk], writes=[ib])
                rec.dma(os.environ.get("RW_DQ", "pool"), Kt[:], src["KTt"][:, :, tk], writes=[ib])
                rec.dma("sp", Vt[:], src["VT"][:, :, tk], writes=[ib])
                rec.dma(os.environ.get("RW_DQ", "pool"), RK[:], src["RKT"][:, :, tk], writes=[ib])
                rec.dma("sp", Gt[:], dr["G_tm"][tk, :], writes=[ib])

            def front(c):
                gidx = b * NCH + c
                bi = gidx % NBUF
                pi = gidx % P2
                tk = slice(b * T + c * 128, b * T + (c + 1) * 128)
                ib = inb[bi]
                AR, Bt, Kt, Vt, RK, Gt = ARt[bi], Btt[bi], Ktt[bi], Vtt[bi], RKt[bi], Gtm[bi]
                Y = Ytm[pi]; Yb = Ytmb[pi]
                idb = g.ident_bf
                for h in range(NHL if LIM >= 1 else 0):
                    B = hb[h][pi]
                    pA, pAb = next_ps(g)
                    MM(rec, pA[:, 0:64], Bt[:, h, :], idb[0:64, 0:64], True, True, [ib, cb], [pAb])
                    MM(rec, pA[:, 64:128], Kt[:, h, :], idb[0:64, 0:64], True, True, [ib, cb], [pAb])
                    MM(rec, pA[:, 128:192], Vt[:, h, :], idb[0:64, 0:64], True, True, [ib, cb], [pAb])
                    MM(rec, pA[:, 192:256], AR[:, h, 0, :], idb[0:64, 0:64], True, True, [ib, cb], [pAb])
                    MM(rec, pA[:, 256:320], RK[:, h, :], g.ones_bf[0:64, 0:64], True, True, [ib, cb], [pAb])
                    CP(rec, "act", TM[h][pi][:], pA[:, 0:256], [pAb], [B["TM"]])
                    CP(rec, "act", rks[h][pi][:], pA[:, 256:258], [pAb], [B["rks"]])
                    pL, pLb = next_ps(g)
                    MM(rec, pL[:, 0:128], AR[:, h, 0, :], Bt[:, h, :], True, True, [ib], [pLb])
                    TT(rec, "dve", L0[h][pi][:], pL[:, 0:128], mlow[:], ALU.mult, [pLb, mb], [B["L0"]])
                    if LIM < 2:
                        continue
                    pB, pBb = next_ps(g)
                    arh = AR[:, h, :, :].rearrange("k a t -> k (a t)")
                    MM(rec, pB[:, 0:256], Bt[:, h, :], arh, True, True, [ib], [pBb])
                    MM(rec, pB[:, 256:512], Kt[:, h, :], arh, True, True, [ib], [pBb])
                    TT(rec, "dve", M12[h][pi][:], pB[:, :], masks[:], ALU.mult, [pBb, mb], [B["M12"]])
                    TT(rec, "dve", LTr[h][pi][:], pB[:, 0:128], masks[:, 0:128], ALU.mult, [pBb, mb], [B["L0"]])
                for h in range(NH if LIM >= 3 else 0):
                    B = hb[h][pi]
                    p3, p3b = next_ps(g)
                    MM(rec, p3[:, 0:64], M12[h][pi][:, 256:384], TM[h][pi][:, 128:192], True, True, [B["M12"], B["TM"]], [p3b])
                    CP(rec, "dve", Xf[h][pi][:, 64:128], p3[:, 0:64], [p3b], [B["X"]])
                    CP(rec, "pool", Xf[h][pi][:, 0:64], TM[h][pi][:, 192:256], [B["TM"]], [B["X"]])
                for i in range(7 if LIM >= 4 else 0):
                    for h in range(NH):
                        B = hb[h][pi]
                        if i == 0:
                            LT_ap, L_ap, lreads = LTr[h][pi][:], L0[h][pi][:], [B["L0"]]
                        else:
                            cur = Lpw[h][(i - 1) % 2]
                            L_ap, LT_ap, lreads = cur[:, 0:128], cur[:, 128:256], [Lpb[h][(i - 1) % 2]]
                        px, pxb = next_ps(g)
                        MM(rec, px[:, 0:128], LT_ap, Xf[h][pi][:], True, True, lreads + [B["X"]], [pxb])
                        if i < 6:
                            pc, pcb = next_ps(g)
                            MM(rec, pc[:, 0:128], LT_ap, L_ap, True, True, lreads, [pcb])
                            MM(rec, pc[:, 128:256], L_ap, LT_ap, True, True, lreads, [pcb])
                            CP(rec, "act", Lpw[h][i % 2][:], pc[:, 0:256], [pcb], [Lpb[h][i % 2]])
                        TT(rec, "dve", Xf[h][pi][:], Xf[h][pi][:].bitcast(F32), px[:, 0:128], ALU.add, [B["X"], pxb], [B["X"]])
                        if i == 6:
                            CP(rec, "pool", Xb[h][pi][:], Xf[h][pi][:].bitcast(F32), [B["X"]], [B["Xb"]])
                for h in range(NH if LIM >= 5 else 0):
                    B = hb[h][pi]
                    p5, p5b = next_ps(g)
                    MM(rec, p5[0:64, 0:64], Xb[h][pi][:, 0:64], TM[h][pi][:, 0:64], True, True, [B["Xb"], B["TM"]], [p5b])
                    CP(rec, "act", Gs[h][pi][:], p5[0:64, 0:64], [p5b], [B["G"]])
                    p5r, p5rb = next_ps(g)
                    MM(rec, p5r[0:64, 0:128], Xb[h][pi][:, 0:64], M12[h][pi][:, 128:256], True, True, [B["Xb"], B["M12"]], [p5rb])
                    TT(rec, "dve", RhT[h][pi][:], p5r[0:64, 0:128], AR[:, h, 1, :], ALU.add, [p5rb, ib], [B["Rh"]])
                ctx[c] = (bi, pi, tk, ib, AR, Gt, Y, Yb)

            def back(c):
                bi, pi, tk, ib, AR, Gt, Y, Yb = ctx.pop(c)
                for h in range(NH if LIM >= 6 else 0):
                    B = hb[h][pi]
                    p6, p6b = next_ps(g)
                    U = Xb[h][pi][:, 64:128]
                    Vm = TM[h][pi][:, 128:192]
                    MM(rec, p6[:, 0:64], M12[h][pi][:, 128:256], U, True, False, [B["M12"], B["Xb"]], [p6b])
                    MM(rec, p6[:, 0:64], M12[h][pi][:, 384:512], Vm, False, False, [B["M12"], B["TM"]], [p6b])
                    MM(rec, p6[:, 0:64], RhT[h][pi][:], Sb_[h][:], False, True, [B["Rh"], Sbb[h]], [p6b])
                    p7, p7b = next_ps(g)
                    MM(rec, p7[0:64, 0:64], TM[h][pi][:, 0:64], U, True, False, [B["TM"], B["Xb"]], [p7b])
                    MM(rec, p7[0:64, 0:64], TM[h][pi][:, 64:128], Vm, False, False, [B["TM"]], [p7b])
                    MM(rec, p7[0:64, 0:64], Gs[h][pi][:], Sb_[h][:], False, True, [B["G"], Sbb[h]], [p7b])
                    CP(rec, "act", Y[:, h * 64:(h + 1) * 64], p6[:, 0:64], [p6b], [Yb])
                    TT(rec, "dve", St[h][:], Sf[h][:], p7[0:64, 0:64], ALU.add, [Sfb[h], p7b], [Stb[h]])
                    TS(rec, "pool", Sf[h][:], St[h][:], PCs[:, h, c:c + 1], ALU.mult, [Stb[h], PCb], [Sfb[h]])
                    ACTF(rec, Sb_[h][:], St[h][:], AF.Identity, [Stb[h], PCb], [Sbb[h]], scale=PCs[:, h, c:c + 1])
                if LIM < 7:
                    return
                Y3 = Y[:].rearrange("p (h v) -> p h v", v=64)
                rec.op("dve", lambda e, Y3=Y3: e.tensor_reduce(out=st[:, 0, :], in_=Y3, op=ALU.add, axis=AX.X), reads=[Yb], writes=[stb])
                ACTF(rec, ysq[:], Y[:], AF.Square, [Yb], [ysqb])
                rec.op("dve", lambda e: e.tensor_reduce(out=st[:, 1, :], in_=ysq[:].rearrange("p (h v) -> p h v", v=64), op=ALU.add, axis=AX.X),
                       reads=[ysqb], writes=[stb])
                TS(rec, "dve", st[:, 2, :], st[:, 0, :], 1.0 / 64, ALU.mult, [stb], [stb])
                TT(rec, "dve", st[:, 3, :], st[:, 2, :], st[:, 2, :], ALU.mult, [stb], [stb])
                STT(rec, st[:, 3, :], st[:, 1, :], 1.0 / 64, st[:, 3, :], ALU.mult, ALU.subtract, [stb], [stb])
                ACTF(rec, st[:, 3, :], st[:, 3, :], AF.Sqrt, [stb, lnbuf], [stb], bias=eps2[:])
                rec.op("dve", lambda e: e.reciprocal(out=st[:, 4, :], in_=st[:, 3, :]), reads=[stb], writes=[stb])
                for h in range(NH):
                    TS(rec, "dve" if h % 2 else "pool", Y[:, h * 64:(h + 1) * 64], Y[:, h * 64:(h + 1) * 64], st[:, 2, h:h + 1], ALU.subtract,
                       [Yb, stb], [Yb], s2=st[:, 4, h:h + 1], op1=ALU.mult)
                TT(rec, "pool", Y[:], Y[:], lnw[:], ALU.mult, [Yb, lnbuf], [Yb])
                TT(rec, "pool", Y[:], Y[:], lnb[:], ALU.add, [Yb, lnbuf], [Yb])
                for h in range(NH):
                    STT(rec, Y[:, h * 64:(h + 1) * 64], TM[h][pi][:, 128:192], rks[h][pi][:, 0:1], Y[:, h * 64:(h + 1) * 64],
                        ALU.mult, ALU.add, [hb[h][pi]["TM"], hb[h][pi]["rks"], Yb], [Yb])
                TT(rec, "dve", Y[:], Y[:], Gt[:], ALU.mult, [Yb, ib], [Yb])
                ctx2[c] = (tk, Y, Yb)

            def back_b(c):
                tk, Y, Yb = ctx2.pop(c)
                pT, pTb = next_ps(g)
                for cc in range(4):
                    rec.op("pe", lambda e, pT=pT, cc=cc, Y=Y: e.transpose(pT[:, cc * 128:(cc + 1) * 128], Y[:, cc * 128:(cc + 1) * 128], g.ident[:]),
                           reads=[Yb, cb], writes=[pTb])
                yo, yob = yto.next()
                CP(rec, "act", yo[:], pT[:, :], [pTb], [yob])
                rec.dma("sp", YT[:, 4:8, tk], yo[:].rearrange("p (c t) -> p c t", t=128), reads=[yob])

            for c in range(NCH + 2):
                if c == 0:
                    load(0)
                if c + 1 < NCH:
                    load(c + 1)
                if c < NCH:
                    front(c)
                if 0 <= c - 2 < NCH:
                    back_b(c - 2)
                if 0 <= c - 1 < NCH:
                    back(c - 1)
        rec.emit()


def phase_pre1(nc, rec, g, dr, NTOK, T, N=256):
    l = 1
    XT = dr["XT"].rearrange("(k p) t -> p k t", p=128)
    fo = {k: dr[k].rearrange("(h p) t -> p h t", p=128) for k in ("QTl", "KTl", "KHl", "SGT")}
    PCh = dr["PCh"].rearrange("(h p) n -> p h n", p=128)
    ntile = NTOK // N
    NS = N // 128
    NC32 = 8 * N // 32
    with ExitStack() as es:
        sb = lambda n, s, d=F32: es.enter_context(nc.sbuf_tensor(uniq(n), s, d))
        W = sb("W1", [128, KC, 4096], BF16); Wb = Buf()
        st_rot = Rot([sb("wst%d" % i, [128, 1024]) for i in range(2)])
        jobs = []
        for kk in range(KC):
            for q4 in range(4):
                cs = slice(q4 * 1024, (q4 + 1) * 1024)
                jobs.append((dr["w_in_odd"][0, kk * 128:(kk + 1) * 128, cs], W[:, kk, cs], Wb))
        load_cast(rec, st_rot, jobs)
        lbl = sb("lbl", [128, 2, 8]); lbb = Buf()
        lb = sb("lb", [128, 8]); oml = sb("oml", [128, 8])
        rec.dma("sp", lbl[:], dr["hg_lbT"][:, :, :], writes=[lbb])
        TT(rec, "dve", lb[:], lbl[:, 1, :], lbl[:, 0, :], ALU.subtract, [lbb], [lbb])
        ACTF(rec, lb[:], lb[:], AF.Sigmoid, [lbb], [lbb])
        TS(rec, "dve", oml[:], lb[:], -1.0, ALU.mult, [lbb], [lbb], s2=1.0, op1=ALU.add)
        m32 = sb("m32", [128, 8 * N])
        rec.op("pool", lambda e: e.memset(m32[:], 1.0), writes=[lbb])
        rec.op("pool", lambda e: e.memset(m32[:].rearrange("p (c t) -> p c t", t=32)[:, :, 0:1], 0.0), writes=[lbb])
        x = sb("x", [128, KC, N]); xb = Buf()
        sq = sb("sq", [128, KC, N], BF16); sqb = Buf()
        hT = sb("hT", [128, KC, N], BF16); hb = Buf()
        tr = Rot([sb("tmp%d" % i, [128, N]) for i in range(3)])
        rstd = sb("rstd", [128, N]); rstdb = Buf()
        raw = sb("raw", [128, 24, N]); rawb = Buf()
        lf = sb("lf", [128, 8 * N]); lfb = Buf()
        bt = sb("bt", [128, 8 * N]); btb = Buf()
        ept = sb("ept", [128, 8 * N]); epb = Buf()
        ent = sb("ent", [128, 8 * N]); enb = Buf()
        ect = sb("ect", [128, 8 * N]); ecb = Buf()
        pct = sb("pct", [128, NC32]); pcb_ = Buf()
        ob = {k: (sb("o_" + k, [128, 8, N], BF16), Buf()) for k in fo}
        ito = Rot([sb("ito%d" % i, [128, 1024], BF16) for i in range(2)])
        for ti in range(ntile):
            b = (ti * N) // T
            t0 = ti * N
            rec.dma("sp", x[:], XT[:, :, t0:t0 + N], writes=[xb])
            t1, t1b = tr.next()
            rms_stats(rec, g, x[:], xb, KC, N, sq, sqb, t1, t1b, rstd, rstdb, D)
            for m in range(KC):
                t1, t1b = tr.next()
                TT(rec, "dve", t1[:], x[:, m, :], rstd[:], ALU.mult, [xb, rstdb], [t1b])
                ACTF(rec, hT[:, m, :], t1[:], AF.Identity, [t1b, g.Amod_b, g.modT_b], [hb],
                     scale=g.Amod[:, l, 0, m, b:b + 1], bias=g.modT[:, l, 0 + m, b:b + 1])
            for j in range(24):
                grp, h = j // 8, j % 8
                c0 = (0, 1024, 3072)[grp] + h * 128
                ps, psb = next_ps(g)
                for kk in range(KC):
                    MM(rec, ps[:, 0:N], W[:, kk, c0:c0 + 128], hT[:, kk, :], kk == 0, kk == KC - 1, [Wb, hb], [psb])
                evac(rec, j, raw[:, j, :], ps[:, 0:N], [psb], [rawb])
            rq = raw[:, 0:8, :].rearrange("p h t -> p (h t)")
            rf = raw[:, 8:16, :].rearrange("p h t -> p (h t)")
            rg = raw[:, 16:24, :].rearrange("p h t -> p (h t)")
            ACTF(rec, rq, rq, AF.Silu, [rawb], [rawb])
            o, obb = ob["SGT"]
            ACTF(rec, o[:].rearrange("p h t -> p (h t)"), rg, AF.Silu, [rawb], [obb])
            ACTF(rec, rf, rf, AF.Sigmoid, [rawb], [rawb])
            for h in range(8):
                TS(rec, "pool" if h % 2 else "dve", raw[:, 8 + h, :], raw[:, 8 + h, :], oml[:, h:h + 1], ALU.mult, [rawb, lbb], [rawb],
                   s2=lb[:, h:h + 1], op1=ALU.add)
            ACTF(rec, lf[:], rf, AF.Ln, [rawb], [lfb])
            TS(rec, "pool", rf, rf, -1.0, ALU.mult, [rawb], [rawb], s2=1.0, op1=ALU.add)
            rec.op("dve", lambda e: e.tensor_tensor_scan(out=bt[:], data0=m32[:], data1=lf[:], initial=0.0, op0=ALU.mult, op1=ALU.add),
                   reads=[lfb, lbb], writes=[btb])
            b3 = bt[:].rearrange("p (c t) -> p c t", t=32)
            ACTF(rec, ept[:], bt[:], AF.Exp, [btb], [epb])
            ACTF(rec, ent[:], bt[:], AF.Exp, [btb], [enb], scale=-1.0)
            TT(rec, "dve", ect[:].rearrange("p (c t) -> p c t", t=32), b3[:, :, 31:32].broadcast_to([128, NC32, 32]), b3, ALU.subtract,
               [btb], [ecb])
            ACTF(rec, ect[:], ect[:], AF.Exp, [ecb], [ecb])
            ACTF(rec, pct[:], b3[:, :, 31], AF.Exp, [btb], [pcb_])
            o, obb = ob["QTl"]
            TT(rec, "dve", o[:].rearrange("p h t -> p (h t)"), rq, ept[:], ALU.mult, [rawb, epb], [obb])
            o, obb = ob["KTl"]
            TT(rec, "pool", o[:].rearrange("p h t -> p (h t)"), rf, ent[:], ALU.mult, [rawb, enb], [obb])
            o, obb = ob["KHl"]
            TT(rec, "dve", o[:].rearrange("p h t -> p (h t)"), rf, ect[:], ALU.mult, [rawb, ecb], [obb])
            for i, k in enumerate(fo):
                o, obb = ob[k]
                rec.dma("sp" if i % 2 == 0 else "pool", fo[k][:, :, t0:t0 + N], o[:], reads=[obb])
            rec.dma("sp", PCh[:, :, t0 // 32:(t0 + N) // 32], pct[:].rearrange("p (h c) -> p h c", h=8), reads=[pcb_])
            for s in range(NS):
                io, iob = ito.next()
                for n in range(2):
                    ps, psb = next_ps(g)
                    for kk in range(KC):
                        MM(rec, ps[:, :], hT[:, kk, s * 128:(s + 1) * 128], W[:, kk, 2048 + n * 512:2048 + (n + 1) * 512],
                           kk == 0, kk == KC - 1, [Wb, hb], [psb])
                    evac(rec, n, io[:, n * 512:(n + 1) * 512], ps[:, :], [psb], [iob])
                rec.dma("sp", dr["I_tm"][t0 + s * 128:t0 + (s + 1) * 128, :], io[:], reads=[iob])
        rec.emit()


def phase_hgrn(nc, rec, g, dr, NB, T, N=512):
    NG = N // 128
    fi = {k: dr[k].rearrange("(h p) t -> p h t", p=128) for k in ("QTl", "KTl", "KHl", "SGT")}
    PCh = dr["PCh"].rearrange("(h p) n -> p h n", p=128)
    YT = dr.get("YT1", dr["YT"]).rearrange("(h p) t -> p h t", p=128)
    NH = 8
    import os
    HL = int(os.environ.get("HG_LIM", "9"))
    with ExitStack() as es:
        sb = lambda n, s, d=F32: es.enter_context(nc.sbuf_tensor(uniq(n), s, d))
        cb = g.const_b
        bmask = sb("bmask", [128, 128]); mb = Buf()
        rec.dma("sp", bmask[:], dr["hmask"][:, :], writes=[mb])
        onorm = sb("onorm", [128, 1])
        rec.dma("sp", onorm[:], dr["hg_onT"][:, :], writes=[mb])
        NBUF = 2
        ft = {k: [sb("f_%s%d" % (k, i), [128, NH, N], BF16) for i in range(NBUF)] for k in fi}
        itm = [sb("itm%d" % i, [128, NG, 1024], BF16) for i in range(NBUF)]
        pcs = [sb("pcs%d" % i, [128, NH, N // 32]) for i in range(NBUF)]
        inb = [Buf() for _ in range(NBUF)]
        yt = [sb("yt%d" % i, [128, NH, N], BF16) for i in range(2)]; ytb = [Buf(), Buf()]
        Sf = [sb("Sf%d" % h, [128, 128]) for h in range(NH)]
        Sb_ = [sb("Sbf%d" % h, [128, 128], BF16) for h in range(NH)]
        Sfb = [Buf() for h in range(NH)]; Sbb = [Buf() for h in range(NH)]
        khm = [[sb("khm%d_%d" % (h, i), [128, 128], BF16) for i in range(2)] for h in range(NH)]
        attm = [[sb("attm%d_%d" % (h, i), [128, 128], BF16) for i in range(2)] for h in range(NH)]
        khb = [[Buf() for i in range(2)] for h in range(NH)]
        atb = [[Buf() for i in range(2)] for h in range(NH)]
        oT = [sb("oT%d" % h, [128, 128]) for h in range(NH)]; oTb = [Buf() for h in range(NH)]
        osq = [sb("osq%d" % h, [128, 128], BF16) for h in range(NH)]; osqb = [Buf() for h in range(NH)]
        sd = [sb("sd%d" % h, [128, 128]) for h in range(NH)]; sdb = [Buf() for h in range(NH)]
        gi = 0
        ti_glob = 0
        pending = []
        for b in range(NB):
            for h in range(NH):
                rec.op("pool", lambda e, h=h: e.memset(Sf[h][:], 0.0), writes=[Sfb[h]])
                rec.op("pool", lambda e, h=h: e.memset(Sb_[h][:], 0.0), writes=[Sbb[h]])
            for tt in range(T // N):
                bi = ti_glob % NBUF
                ti_glob += 1
                t0 = b * T + tt * N
                ib = inb[bi]
                for i, k in enumerate(fi):
                    rec.dma("sp" if i % 2 == 0 else "pool", ft[k][bi][:], fi[k][:, :, t0:t0 + N], writes=[ib])
                rec.dma("sp", itm[bi][:], dr["I_tm"][t0:t0 + N, :].rearrange("(g p) d -> p g d", p=128), writes=[ib])
                rec.dma("sp", pcs[bi][:], PCh[:, :, t0 // 32:(t0 + N) // 32], writes=[ib])
                Q, K, KH, SG, IT, PC = ft["QTl"][bi], ft["KTl"][bi], ft["KHl"][bi], ft["SGT"][bi], itm[bi], pcs[bi]
                Yt, Ytb = yt[bi], ytb[bi]
                for gq in range(NG):
                    pi = gi % 2
                    gi += 1
                    gs = slice(gq * 128, (gq + 1) * 128)
                    for h in range(NH):
                        pK, pKb = next_ps(g)
                        MM(rec, pK[:, 0:128], KH[:, h, gs], g.ident_bf[:], True, True, [ib, cb], [pKb])
                        CP(rec, "act", khm[h][pi][:], pK[:, 0:128], [pKb], [khb[h][pi]])
                        pA, pAb = next_ps(g)
                        MM(rec, pA[:, 0:128], K[:, h, gs], Q[:, h, gs], True, True, [ib], [pAb])
                        TT(rec, "dve", attm[h][pi][:], pA[:, 0:128], bmask[:], ALU.mult, [pAb, mb], [atb[h][pi]])
                    for f_ in pending:
                        f_()
                    pending.clear()
                    for half in range(2 if HL >= 2 else 0):
                        hs = range(half * 4, half * 4 + 4)
                        pS = {}; pO = {}
                        pSj = [next_ps(g) for j in range(4)]
                        for h in hs:
                            hl = h - half * 4
                            for j in range(4):
                                r0, r1 = (32 * j, 32 * j + 32) if j < 3 else (64, 128)
                                MM(rec, pSj[j][0][:, hl * 128:(hl + 1) * 128], khm[h][pi][r0:r1, :],
                                   IT[r0:r1, gq, h * 128:(h + 1) * 128], True, True, [khb[h][pi], ib], [pSj[j][1]])
                        if HL < 3:
                            continue
                        for h in hs:
                            pO[h] = next_ps(g)
                            MM(rec, pO[h][0][:, 0:128], IT[:, gq, h * 128:(h + 1) * 128], attm[h][pi][:], True, False, [ib, atb[h][pi]], [pO[h][1]])
                        for j in range(4 if HL >= 4 else 0):
                            for h in hs:
                                c0 = gq * 128 + 32 * j
                                MM(rec, pO[h][0][:, 32 * j:32 * j + 32], Sb_[h][:], Q[:, h, c0:c0 + 32], False, j == 3, [Sbb[h], ib], [pO[h][1]])
                                ch = gq * 4 + j
                                hl = h - half * 4
                                STT(rec, Sf[h][:], Sf[h][:], PC[:, h, ch:ch + 1], pSj[j][0][:, hl * 128:(hl + 1) * 128], ALU.mult, ALU.add,
                                    [Sfb[h], ib, pSj[j][1]], [Sfb[h]])
                                if j == 3:
                                    TT(rec, "dve", Sf[h][:], Sf[h][:], pSj[2][0][:, hl * 128:(hl + 1) * 128], ALU.subtract, [Sfb[h], pSj[2][1]], [Sfb[h]])
                                CP(rec, "act", Sb_[h][:], Sf[h][:], [Sfb[h]], [Sbb[h]])
                        for h in (hs if HL >= 5 else []):
                            CP(rec, "act", oT[h][:], pO[h][0][:, 0:128], [pO[h][1]], [oTb[h]])
                            ACTF(rec, osq[h][:], oT[h][:], AF.Square, [oTb[h]], [osqb[h]])

                            def fin(h=h, gs=gs, SG=SG, Yt=Yt, Ytb=Ytb, ib=ib):
                                pZ, pZb = next_ps(g)
                                MM(rec, pZ[:, 0:128], g.ones_bf[:], osq[h][:], True, True, [cb, osqb[h]], [pZb])
                                ACTF(rec, sd[h][:], pZ[:, 0:128], AF.Ln, [pZb, cb], [sdb[h]], scale=1.0 / 128, bias=g.epsb[:])
                                ACTF(rec, sd[h][:], sd[h][:], AF.Exp, [sdb[h]], [sdb[h]], scale=-0.5)
                                STT(rec, oT[h][:], oT[h][:], onorm[:, 0:1], sd[h][:], ALU.mult, ALU.mult, [oTb[h], sdb[h], mb], [oTb[h]])
                                TT(rec, "dve", Yt[:, h, gs], oT[h][:], SG[:, h, gs], ALU.mult, [oTb[h], ib], [Ytb])
                            pending.append(fin)
                for f_ in pending:
                    f_()
                pending.clear()
                rec.dma("sp", YT[:, :, t0:t0 + N], Yt[:], reads=[Ytb])
        rec.emit()


SCRATCH = None


def build_program(NB, T, ext_scratch=False):
    NTOK = NB * T
    nc = bass.Bass("TRN2", target_bir_lowering=False)
    dr = {}
    shapes = input_shapes(NB, T)
    for k, (shp, dt) in shapes.items():
        dr[k] = nc.dram_tensor(k, list(shp), dt, kind="ExternalInput").ap()
    dr["out"] = nc.dram_tensor("out", [NTOK, 1024], F32, kind="ExternalOutput").ap()
    kind = "ExternalOutput" if ext_scratch else "Internal"

    def scr(name, shape, dt):
        dr[name] = nc.dram_tensor(name, shape, dt, kind=kind).ap()
    scr("XT", [1024, NTOK], F32)
    scr("RP", [1792, NTOK], F32)
    scr("QT", [8, 96, NTOK], BF16)
    scr("KT", [8, 96, NTOK], BF16)
    scr("V", [NTOK, 512], BF16)
    scr("YT", [1024, NTOK], BF16)
    for k in ("AT", "BT", "KTt", "RTt", "VT", "RKT"):
        scr(k, [512, NTOK], BF16)
    scr("G_tm", [NTOK, 512], BF16)
    scr("PC", [512, NTOK // 128], F32)
    for k in ("QTl", "KTl", "KHl", "SGT"):
        scr(k, [1024, NTOK], BF16)
    scr("I_tm", [NTOK, 1024], BF16)
    scr("PCh", [1024, NTOK // 32], F32)
    if ext_scratch:
        scr("YT1", [1024, NTOK], BF16)
    g = G()
    with ExitStack() as es:
        rec = Rec(nc, es)
        setup_globals(nc, es, g, NB)
        phase_mods(nc, rec, g, dr)
        phase_pre0(nc, rec, g, dr, NTOK, T)
        run_gens(rec, [phase_mla_g(nc, rec, g, dr, NB, T, nptr=1), phase_rwkv_prep_g(nc, rec, g, dr, NTOK, T)])
        phase_rwkv_main(nc, rec, g, dr, NB, T)
        phase_post(nc, rec, g, dr, 0, NTOK, T, last=False)
        phase_pre1(nc, rec, g, dr, NTOK, T)
        phase_hgrn(nc, rec, g, dr, NB, T)
        phase_post(nc, rec, g, dr, 1, NTOK, T, last=True)
    return nc, rec


def input_shapes(NB, T):
    NTOK = NB * T
    f = F32
    return {
        "ident": ((128, 128), f), "cmask": ((128, 128), f), "x": ((NTOK, 1024), f), "pos": ((NB, T), I32),
        "cT": ((128, 8, NB), f), "ada_bT": ((128, 2, 48), f), "norm_mixT": ((128, 2, 8), f), "norm_ffnT": ((128, 2, 8), f),
        "final_normT": ((128, 8), f), "ada_w": ((2, 1024, 6144), f), "w_out_even": ((1, 1024, 1024), f),
        "w_out_odd": ((1, 1024, 1024), f), "ffn_w_gate": ((2, 1024, 2816), f), "ffn_w_up": ((2, 1024, 2816), f),
        "ffn_w_down": ((2, 2816, 1024), f), "w_in_even": ((1, 1024, 2464), f), "mla_w_uq": ((1, 384, 768), f),
        "w_in_odd": ((1, 1024, 4096), f), "w_in_sw": ((1024, 32), f), "w_uq_sw": ((384, 768), f), "w_ukv_r": ((256, 1024), f),
        "ctab": ((128, 8), f), "rmasks": ((128, 384), f), "rwkv_ln_w": ((1, 512), f), "rwkv_ln_b": ((1, 512), f),
        "hg_lbT": ((128, 2, 8), f), "hg_onT": ((128, 1), f), "hmask": ((128, 128), f), "rw_mu": ((128, 14), f),
        "rw_par": ((128, 20), f), "rwkv_w2": ((1, 64, 512), f), "rwkv_a2": ((1, 64, 512), f), "rwkv_g2": ((1, 128, 512), f),
    }


_CACHE = {}


def kernel(**inputs):
    from concourse.bass_utils import run_bass_kernel_spmd
    NB, T, NCORE = 4, 2048, 8
    inputs = {k: np.asarray(v) for k, v in inputs.items()}
    if "nc" not in _CACHE:
        _CACHE["nc"] = build_program(NB, T)[0]
    nc = _CACHE["nc"]
    shared = None
    in_maps = []
    for i in range(NCORE):
        sl = slice(i * NB, (i + 1) * NB)
        inp = dict(inputs)
        inp["x"] = inputs["x"][sl]
        inp["c"] = inputs["c"][sl]
        inp["positions"] = inputs["positions"][sl]
        if shared is None:
            shared = host_layout(inp)
            d = dict(shared)
        else:
            d = dict(shared)
            d["x"] = np.ascontiguousarray(inp["x"].reshape(-1, 1024))
            d["pos"] = np.ascontiguousarray(inp["positions"].astype(np.int32))
            d["cT"] = np.ascontiguousarray(inp["c"].T.reshape(8, 128, NB).transpose(1, 0, 2))
        in_maps.append(d)
    res = run_bass_kernel_spmd(nc, in_maps, core_ids=list(range(NCORE)))
    outs = [np.asarray(r["out"]).reshape(NB, T, 1024) for r in res.results]
    return np.concatenate(outs, axis=0).astype(np.float32)
```
